# Optimizing a Trainium2 kernel written in Bass

```python
import jax, jax.numpy as jnp
from jax import lax
import numpy as np

D_MODEL = 2048
BATCH = 2
SEQ = 8192
DEPTH = 4

POOL_WIDTH = D_MODEL // 4
POOL_WINDOWS = (2, 4, 8, 16)
POOL_GROUP = POOL_WIDTH // len(POOL_WINDOWS)
CONV_WIDTH = D_MODEL // 4
CONV_KERNEL = 31
RET_WIDTH = D_MODEL // 2
RET_HEADS = 8
RET_HEAD_DIM = RET_WIDTH // RET_HEADS
RET_CHUNK = 128
ROPE_BASE = 10000.0
N_BRANCH = 3
FFN_HIDDEN = -(-8 * D_MODEL // (3 * 256)) * 256
NORM_EPS = 1e-6
LN_EPS = 1e-5

IN_SPLITS = (POOL_WIDTH, 2 * CONV_WIDTH, RET_WIDTH, RET_WIDTH, RET_WIDTH, RET_WIDTH, N_BRANCH * D_MODEL)
N_IN = sum(IN_SPLITS)
IN_OFFSETS = tuple(int(v) for v in np.cumsum(IN_SPLITS)[:-1])

kernel_name = "hybrid_pool_conformer_retention_gated"


def rms_norm(x, g):
    xf = x.astype(jnp.float32)
    y = xf * lax.rsqrt(jnp.mean(xf * xf, axis=-1, keepdims=True) + NORM_EPS)
    return (y * g.astype(jnp.float32)).astype(x.dtype)


def layer_norm(x, g, b):
    xf = x.astype(jnp.float32)
    mu = jnp.mean(xf, axis=-1, keepdims=True)
    var = jnp.mean(jnp.square(xf - mu), axis=-1, keepdims=True)
    y = (xf - mu) * lax.rsqrt(var + LN_EPS)
    return (y * g.astype(jnp.float32) + b.astype(jnp.float32)).astype(x.dtype)


def pool_mixer(u, pool_w, pool_scale):
    B, S, _ = u.shape
    uf = u.astype(jnp.float32)
    csum = jnp.cumsum(uf, axis=1)
    cpad = jnp.concatenate([jnp.zeros((B, 1, POOL_WIDTH), jnp.float32), csum], axis=1)
    t = jnp.arange(S)
    groups = []
    for gi, w in enumerate(POOL_WINDOWS):
        sl = slice(gi * POOL_GROUP, (gi + 1) * POOL_GROUP)
        hi = cpad[:, 1:, sl]
        lo = jnp.concatenate([jnp.zeros((B, w - 1, POOL_GROUP), jnp.float32),
                              cpad[:, :S - w + 1, sl]], axis=1)
        cnt = jnp.minimum(t + 1, w).astype(jnp.float32)[None, :, None]
        groups.append((hi - lo) / cnt - uf[:, :, sl])
    p = jnp.stack(groups, axis=2)
    y = jnp.einsum('bsgc,gcd->bsgd', p, pool_w.astype(jnp.float32)).reshape(B, S, POOL_WIDTH)
    return (y * pool_scale.astype(jnp.float32)).astype(u.dtype)


def conv_module(u, dw, db, ln_g, ln_b):
    a, g = jnp.split(u, 2, axis=-1)
    h = a * jax.nn.sigmoid(g)
    h = lax.conv_general_dilated(h, dw[:, None, :].astype(h.dtype), window_strides=(1,),
                                 padding=[(CONV_KERNEL - 1, 0)],
                                 dimension_numbers=('NWC', 'WIO', 'NWC'),
                                 feature_group_count=CONV_WIDTH) + db.astype(h.dtype)
    h = layer_norm(h, ln_g, ln_b)
    return jax.nn.silu(h)


def rotary(t, positions):
    half = t.shape[-1] // 2
    inv = ROPE_BASE ** (-jnp.arange(half, dtype=jnp.float32) / half)
    ang = positions.astype(jnp.float32)[..., None] * inv
    cos = jnp.cos(ang)[:, :, None, :]
    sin = jnp.sin(ang)[:, :, None, :]
    t1, t2 = t[..., :half], t[..., half:]
    return jnp.concatenate([t1 * cos - t2 * sin, t1 * sin + t2 * cos], axis=-1)


def retention(q, k, v, gate, positions, gn_g):
    B, S, _ = q.shape
    H, Dh, C = RET_HEADS, RET_HEAD_DIM, RET_CHUNK
    N = S // C
    qf = rotary(q.astype(jnp.float32).reshape(B, S, H, Dh), positions)
    kf = rotary(k.astype(jnp.float32).reshape(B, S, H, Dh), positions) * (Dh ** -0.5)
    vf = v.astype(jnp.float32).reshape(B, S, H, Dh)

    log_gamma = jnp.log1p(-jnp.exp2(-5.0 - jnp.arange(H, dtype=jnp.float32)))
    idx = jnp.arange(C, dtype=jnp.float32)
    rel = idx[:, None] - idx[None, :]
    decay = jnp.where(rel >= 0, jnp.exp(jnp.maximum(rel, 0.0)[None] * log_gamma[:, None, None]), 0.0)
    xi = jnp.exp((idx + 1.0)[None, :] * log_gamma[:, None])
    zeta = jnp.exp((C - 1.0 - idx)[None, :] * log_gamma[:, None])
    gamma_c = jnp.exp(C * log_gamma)

    def chunks(t):
        return t.reshape(B, N, C, H, Dh).transpose(0, 3, 1, 2, 4)
    qc, kc, vc = chunks(qf), chunks(kf), chunks(vf)

    scores = jnp.einsum('bhncd,bhned->bhnce', qc, kc) * decay[None, :, None]
    intra = jnp.einsum('bhnce,bhned->bhncd', scores, vc)
    kv = jnp.einsum('bhncd,bhnce->bhnde', kc, vc * zeta[None, :, None, :, None])

    def step(state, kv_n):
        return gamma_c[None, :, None, None] * state + kv_n, state
    _, prev = lax.scan(step, jnp.zeros((B, H, Dh, Dh), jnp.float32), kv.transpose(2, 0, 1, 3, 4))
    prev = prev.transpose(1, 2, 0, 3, 4)
    inter = jnp.einsum('bhncd,bhnde->bhnce', qc, prev) * xi[None, :, None, :, None]

    o = (intra + inter).transpose(0, 2, 3, 1, 4).reshape(B, S, H, Dh)
    mu = jnp.mean(o, axis=-1, keepdims=True)
    var = jnp.mean(jnp.square(o - mu), axis=-1, keepdims=True)
    o = ((o - mu) * lax.rsqrt(var + LN_EPS)).reshape(B, S, RET_WIDTH) * gn_g.astype(jnp.float32)
    return (jax.nn.silu(gate.astype(jnp.float32)) * o).astype(q.dtype)


def hybrid_layer(x, positions, g_mix_pre, g_mix_post, g_ffn_pre, g_ffn_post, w_in,
                 pool_w, pool_scale, conv_dw, conv_b, conv_ln_g, conv_ln_b, ret_gn_g,
                 w_pool_proj, w_conv_proj, w_ret_proj, w_out, w_ffn_in, w_ffn_out):
    B, S, D = x.shape
    h = rms_norm(x, g_mix_pre)
    proj = h @ w_in
    u_pool, u_conv, q, k, v, g_ret, gate_pre = jnp.split(proj, IN_OFFSETS, axis=-1)
    y_pool = pool_mixer(u_pool, pool_w, pool_scale) @ w_pool_proj
    y_conv = conv_module(u_conv, conv_dw, conv_b, conv_ln_g, conv_ln_b) @ w_conv_proj
    y_ret = retention(q, k, v, g_ret, positions, ret_gn_g) @ w_ret_proj
    gates = jax.nn.sigmoid(gate_pre.reshape(B, S, N_BRANCH, D))
    merged = gates[:, :, 0] * y_pool + gates[:, :, 1] * y_conv + gates[:, :, 2] * y_ret
    x = x + rms_norm(merged @ w_out, g_mix_post)

    h = rms_norm(x, g_ffn_pre)
    a, b = jnp.split(h @ w_ffn_in, 2, axis=-1)
    x = x + rms_norm((jax.nn.silu(a) * b) @ w_ffn_out, g_ffn_post)
    return x


def setup_inputs(seed: int = 0) -> dict:
    key = jax.random.key(seed)
    ks = jax.random.split(key, 24)

    def nrm(k, shape, scale):
        return jax.random.normal(k, shape, jnp.float32) * scale

    def gain(k, shape):
        return 1.0 + 0.05 * jax.random.normal(k, shape, jnp.float32)

    L, D = DEPTH, D_MODEL
    return {
        "x": nrm(ks[0], (BATCH, SEQ, D), 1.0),
        "positions": jnp.broadcast_to(jnp.arange(SEQ, dtype=jnp.int32), (BATCH, SEQ)),
        "g_mix_pre": gain(ks[1], (L, D)),
        "g_mix_post": gain(ks[2], (L, D)),
        "g_ffn_pre": gain(ks[3], (L, D)),
        "g_ffn_post": gain(ks[4], (L, D)),
        "w_in": nrm(ks[5], (L, D, N_IN), D ** -0.5),
        "pool_w": nrm(ks[6], (L, len(POOL_WINDOWS), POOL_GROUP, POOL_GROUP), POOL_GROUP ** -0.5),
        "pool_scale": gain(ks[7], (L, POOL_WIDTH)),
        "conv_dw": nrm(ks[8], (L, CONV_KERNEL, CONV_WIDTH), CONV_KERNEL ** -0.5),
        "conv_b": nrm(ks[9], (L, CONV_WIDTH), 0.02),
        "conv_ln_g": gain(ks[10], (L, CONV_WIDTH)),
        "conv_ln_b": nrm(ks[11], (L, CONV_WIDTH), 0.02),
        "ret_gn_g": gain(ks[12], (L, RET_WIDTH)),
        "w_pool_proj": nrm(ks[13], (L, POOL_WIDTH, D), POOL_WIDTH ** -0.5),
        "w_conv_proj": nrm(ks[14], (L, CONV_WIDTH, D), CONV_WIDTH ** -0.5),
        "w_ret_proj": nrm(ks[15], (L, RET_WIDTH, D), RET_WIDTH ** -0.5),
        "w_out": nrm(ks[16], (L, D, D), D ** -0.5),
        "w_ffn_in": nrm(ks[17], (L, D, 2 * FFN_HIDDEN), D ** -0.5),
        "w_ffn_out": nrm(ks[18], (L, FFN_HIDDEN, D), FFN_HIDDEN ** -0.5),
    }


def reference(x, positions, g_mix_pre, g_mix_post, g_ffn_pre, g_ffn_post, w_in,
              pool_w, pool_scale, conv_dw, conv_b, conv_ln_g, conv_ln_b, ret_gn_g,
              w_pool_proj, w_conv_proj, w_ret_proj, w_out, w_ffn_in, w_ffn_out):
    for l in range(DEPTH):
        x = hybrid_layer(x, positions, g_mix_pre[l], g_mix_post[l], g_ffn_pre[l], g_ffn_post[l], w_in[l],
                         pool_w[l], pool_scale[l], conv_dw[l], conv_b[l], conv_ln_g[l], conv_ln_b[l],
                         ret_gn_g[l], w_pool_proj[l], w_conv_proj[l], w_ret_proj[l], w_out[l],
                         w_ffn_in[l], w_ffn_out[l])
    return x
```

```python
import numpy as np
from contextlib import ExitStack
import concourse.bass as bass
import concourse.mybir as mybir
from concourse.bass_utils import run_bass_kernel_spmd

F32 = mybir.dt.float32
BF = mybir.dt.bfloat16
I32 = mybir.dt.int32
ALU = mybir.AluOpType
AF = mybir.ActivationFunctionType

DM = 2048
NTOK = 2048
TT = 512
NTILE = NTOK // TT
NIN = 11776
FH = 5632
O_POOL, O_CA, O_CG, O_Q, O_K, O_V, O_GR, O_GATE = 0, 512, 1024, 1536, 2560, 3584, 4608, 5632
NSLOT = 6
XW = 1216
NSP = 212
NCST = 1564
PI = float(np.pi)
DEBUG = False
STOP = None


class _Stop(Exception):
    pass


class V:
    def __init__(self, ap, k):
        self.ap = ap
        self.k = k


def _flat(lst):
    out = []
    for x in lst:
        if isinstance(x, V):
            out.extend(x.k)
        elif isinstance(x, list):
            out.extend(_flat(x))
        else:
            out.append(x)
    return out


class Prog:
    ENG = ['pe', 'act', 'dve', 'pool', 'sp']

    def __init__(self):
        self.ops = []
        self.st = {}
        self.seqc = {}
        self.dry = False

    def add(self, eng, meth, kw, R=(), W=(), group=None):
        if self.dry:
            return -1
        R = _flat(list(R))
        W = _flat(list(W))
        idx = len(self.ops)
        stream = group if group is not None else eng
        deps = set()
        for k in R:
            e = self.st.get(k)
            if e is not None and e[0] is not None:
                deps.add(e[0])
        for k in W:
            e = self.st.get(k)
            if e is not None:
                if e[0] is not None:
                    deps.add(e[0])
                deps.update(e[1].values())
        seq = self.seqc.get(stream, 0) + 1
        self.seqc[stream] = seq
        self.ops.append(dict(eng=eng, meth=meth, kw=kw, deps=deps, stream=stream, seq=seq,
                             dma=group is not None, ms=False))
        for k in R:
            e = self.st.get(k)
            if e is None:
                e = [None, {}]
                self.st[k] = e
            e[1][stream] = idx
        for k in W:
            self.st[k] = [idx, {}]
        return idx

    def finalize(self):
        ops = self.ops
        hasdep = set()
        for op in ops:
            hasdep.update(op['deps'])
        know = {e: {} for e in self.ENG}
        snap = {}
        for i, op in enumerate(ops):
            E = op['eng']
            kn = know[E]
            waits = []
            for j in sorted(op['deps'], reverse=True):
                d = ops[j]
                if d['stream'] == 'pe' and E == 'pe':
                    continue
                if kn.get(d['stream'], 0) >= d['seq']:
                    continue
                waits.append(j)
                d['ms'] = True
                for s2, sq in snap[j].items():
                    if kn.get(s2, 0) < sq:
                        kn[s2] = sq
            op['waits'] = waits
            if i in hasdep:
                sn = dict(kn)
                if sn.get(op['stream'], 0) < op['seq']:
                    sn[op['stream']] = op['seq']
                snap[i] = sn
        cnt = {}
        for op in ops:
            if op['dma']:
                op['ms'] = True
            if op['ms']:
                c = cnt.get(op['stream'], 0) + (16 if op['dma'] else 1)
                cnt[op['stream']] = c
                op['cnt'] = c
        return sorted(cnt.keys())

    def emit(self, nc, block, sems):
        per = {e: [] for e in self.ENG}
        for op in self.ops:
            per[op['eng']].append(op)
        ops = self.ops

        def mk(E):
            def f(e):
                for op in per[E]:
                    for j in op['waits']:
                        d = ops[j]
                        e.wait_ge(sems[d['stream']], d['cnt'])
                    if op['meth'] is None:
                        continue
                    ins = getattr(e, op['meth'])(**op['kw'])
                    if op['ms']:
                        ins.then_inc(sems[op['stream']], 16 if op['dma'] else 1)
            return f
        block.tensor(mk('pe'))
        block.scalar(mk('act'))
        block.vector(mk('dve'))
        block.gpsimd(mk('pool'))
        block.sync(mk('sp'))


def build(L, mode, ntok=2048):
    nc = bass.Bass("TRN2", target_bir_lowering=False)
    NTOK = ntok
    NTILE = ntok // TT
    NCH = ntok // 128
    seq = mode == 'seq'
    P = Prog()
    es = ExitStack()

    def din(name, shape, dt):
        return nc.dram_tensor(name, shape, dt, kind="ExternalInput").ap()

    xT = din("xT", [DM, NTOK], F32)
    posT = din("posT", [128, NCH], I32)
    cstD = din("cst", [128, NCST], F32)
    spD = din("sp", [L, 128, NSP], F32)
    pwD = din("pw", [L, 128, 512], F32)
    w_in = din("w_in", [L, DM, NIN], F32)
    if mode != 'pre':
        w_pp = din("w_pp", [L, 512, DM], F32)
        w_cp = din("w_cp", [L, 512, DM], F32)
        w_rp = din("w_rp", [L, 1024, DM], F32)
        w_out = din("w_out", [L, DM, DM], F32)
        w_fi = din("w_fi", [L, DM, 2 * FH], F32)
        w_fo = din("w_fo", [L, FH, DM], F32)
    if mode == 'full':
        xinD = din("xin", [8 * 128, XW], F32)
    if mode == 'pre':
        xoutD = nc.dram_tensor("xout", [128, XW], F32, kind="ExternalOutput").ap()
    else:
        yT = nc.dram_tensor("yT", [DM, NTOK], F32, kind="ExternalOutput").ap()
    if DEBUG and mode == 'full':
        dbgA = nc.dram_tensor("dbgA", [128, 16 * 512], BF, kind="ExternalOutput").ap()
        dbgBR = nc.dram_tensor("dbgBR", [128, 16 * 512], BF, kind="ExternalOutput").ap()
        dbgM = nc.dram_tensor("dbgM", [128, 16 * 512], BF, kind="ExternalOutput").ap()
        dbgY = nc.dram_tensor("dbgY", [128, 16 * 512], F32, kind="ExternalOutput").ap()
        dbgQ = nc.dram_tensor("dbgQ", [128, 4 * 1024], BF, kind="ExternalOutput").ap()
        dbgK = nc.dram_tensor("dbgK", [128, 4 * 1024], BF, kind="ExternalOutput").ap()
        dbgCC = nc.dram_tensor("dbgCC", [128, 2048], F32, kind="ExternalOutput").ap()
        dbgSS = nc.dram_tensor("dbgSS", [128, 2048], F32, kind="ExternalOutput").ap()
        dbgHB = nc.dram_tensor("dbgHB", [128, 4 * 544], F32, kind="ExternalOutput").ap()
        dbgACC = nc.dram_tensor("dbgACC", [128, 4 * 512], F32, kind="ExternalOutput").ap()
        dbgSG = nc.dram_tensor("dbgSG", [128, 4 * 512], F32, kind="ExternalOutput").ap()
    ktmD = nc.dram_tensor("ktm_s", [NTOK, 1024], BF).ap()
    vtmD = nc.dram_tensor("vtm_s", [NTOK, 1024], BF).ap()
    if seq:
        xbuf = [nc.dram_tensor("xb%d" % i, [DM, NTOK], F32).ap() for i in range(2)]
        ccD = nc.dram_tensor("cc_s", [128, NCH * 128], F32).ap()
        ssD = nc.dram_tensor("ss_s", [128, NCH * 128], F32).ap()
    if mode == 'fused':
        xbuf = [nc.dram_tensor("xb%d" % i, [DM, NTOK], F32).ap() for i in range(2)]
        xchD = nc.dram_tensor("xch_s", [128, XW], F32).ap()
        xgD = nc.dram_tensor("xg_s", [8 * 128, XW], F32).ap()

    def sb(name, shape, dt):
        return es.enter_context(nc.sbuf_tensor(name, shape, dt))

    WR = [sb("wr%d" % i, [128, 4096], BF) for i in range(NSLOT)]
    ARB = 112 * 1024
    AR = sb("arena", [128, ARB // 2], BF)
    CST = sb("cstt", [128, NCST], F32)
    SPR = sb("spr", [128, NSP], F32)
    PWB = sb("pwb", [128, 512], BF)
    CC = sb("cc", [128, 4 if seq else 16, 128], F32)
    SS = sb("ss", [128, 4 if seq else 16, 128], F32)
    IDN = sb("idn", [128, 128], BF)
    ON_D = sb("ond", [128, 128], BF)
    ON_5 = sb("on5", [128, 128], BF)
    ON_1 = sb("on1", [128, 128], BF)
    ST = sb("stt", [128, 1024], F32)
    UPT = sb("upt", [128, 4, 16], F32)
    HBT = sb("hbt", [128, 4, 32], F32)
    POSI = sb("posi", [128, NCH], I32)
    RS = [sb("rs%d" % i, [128, 512], F32) for i in range(2)]
    FT = [sb("ft%d" % i, [128, 512], F32) for i in range(4)]
    BT = [sb("bt%d" % i, [128, 512], BF) for i in range(3)]
    SMTB = [sb("smt%d" % i, [128, 512], BF) for i in range(2)]
    DGB = [sb("dg%d" % i, [128, 128], BF) for i in range(4)]
    PSB = [es.enter_context(nc.psum_tensor("ps%d" % i, [128, 512], F32)) for i in range(8)]

    def sv(t, name):
        return V(t[:], [(name,)])

    vCST = sv(CST, "cst"); vSPR = sv(SPR, "spr"); vPWB = sv(PWB, "pwb")
    vCC = sv(CC, "cc"); vSS = sv(SS, "ss"); vIDN = sv(IDN, "idn")
    vOND = sv(ON_D, "ond"); vON5 = sv(ON_5, "on5"); vON1 = sv(ON_1, "on1")
    vST = [V(ST[:, i * 512:(i + 1) * 512], [("st", i)]) for i in range(2)]
    vUPT = sv(UPT, "upt"); vHBT = sv(HBT, "hbt"); vPOSI = sv(POSI, "posi")
    vRS = [sv(RS[i], "rs%d" % i) for i in range(2)]
    vFT = [sv(FT[i], "ft%d" % i) for i in range(4)]
    vBT = [sv(BT[i], "bt%d" % i) for i in range(3)]
    vSMT = [sv(SMTB[i], "smt%d" % i) for i in range(2)]
    vDG = [sv(DGB[i], "dg%d" % i) for i in range(4)]
    PS = [V(PSB[i][:], [("ps", i)]) for i in range(8)]
    rot = {}

    def nxt(name, lst):
        i = rot.get(name, 0)
        rot[name] = i + 1
        return lst[i % len(lst)]

    def bank():
        return nxt("bank", PS[0:7])
    PSTAT = PS[7]

    def arv(off, shape, dt):
        n = 1
        for s in shape[1:]:
            n *= s
        nb = n * (2 if dt == BF else 4)
        ap = AR[:, off // 2: off // 2 + nb // 2]
        if dt != BF:
            ap = ap.bitcast(dt)
        if len(shape) == 3:
            ap = ap.rearrange("p (a b) -> p a b", a=shape[1])
        keys = [("AR", g) for g in range(off // 1024, (off + nb + 1023) // 1024)]
        return V(ap, keys)

    KB = 1024
    A = [arv(kc * KB, [128, 512], BF) for kc in range(16)]
    A_all = arv(0, [128, 16, 512], BF)
    BR = [arv(16 * KB + i * KB, [128, 512], BF) for i in range(16)]
    M = [arv(32 * KB + i * KB, [128, 512], BF) for i in range(16)]
    HID = [arv(32 * KB + j * KB, [128, 512], BF) for j in range(44)]
    Y = [arv(80 * KB + i * 2 * KB, [128, 512], F32) for i in range(16)]
    Y_all = arv(80 * KB, [128, 16, 512], F32)
    Y2 = [arv(i * 2 * KB, [128, 512], F32) for i in range(16)]
    QTM = [arv(48 * KB + n * 2 * KB, [128, 1024], BF) for n in range(4)]
    KTM = [arv(56 * KB + n * 2 * KB, [128, 1024], BF) for n in range(4)]
    VTM = [arv(64 * KB + n * 2 * KB, [128, 1024], BF) for n in range(4)]
    KTM_all = arv(56 * KB, [128, 4, 1024], BF)
    VTM_all = arv(64 * KB, [128, 4, 1024], BF)
    QT = [arv(72 * KB + h * KB, [128, 512], BF) for h in range(8)]
    KT = [arv(80 * KB + h * KB, [128, 512], BF) for h in range(8)]
    GS = [arv(88 * KB + h * KB, [128, 512], BF) for h in range(8)]
    SBS = [[arv(96 * KB + n * 2 * KB + hf * KB, [128, 512], BF) for hf in range(2)] for n in range(5)]
    UP = arv(48 * KB, [128, 4, 528], F32)
    SA = arv(48 * KB + 8448, [128, 4, 528], F32)
    SBF = arv(48 * KB + 2 * 8448, [128, 4, 528], F32)
    PP = [arv(48 * KB + 3 * 8448 + g * KB, [128, 512], BF) for g in range(4)]
    HB = arv(48 * KB, [128, 4, 544], BF)
    SG = arv(48 * KB + 8704, [128, 4, 512], F32)
    ACC = arv(48 * KB + 8704 + 8192, [128, 4, 512], F32)
    XT_ = [arv(80 * KB + i * 5 * KB, [128, XW], F32) for i in range(2)]
    vXC = [arv(i * 2 * KB, [128, 512], F32) for i in range(2)]

    class WStream:
        def __init__(self):
            self.reqs = []
            self.pos = 0
            self.issued = 0

        @staticmethod
        def _views(slot, parts):
            vs = []
            off = 0
            for (wt, l, r0, nk, c0, cols) in parts:
                vs.append(WR[slot][:, off:off + nk * cols].rearrange("p (k c) -> p k c", k=nk))
                off += nk * cols
            return vs

        @staticmethod
        def _keys(slot, pi, np_):
            if np_ == 1:
                return [("wr", slot, 0), ("wr", slot, 1), ("wr", slot, 2)]
            return [("wr", slot, pi)]

        def _issue(self, j):
            parts = self.reqs[j]
            slot = j % NSLOT
            vs = self._views(slot, parts)
            np_ = len(parts)
            for pi, (view, (wt, l, r0, nk, c0, cols)) in enumerate(zip(vs, parts)):
                src = wt[l, r0:r0 + nk * 128, c0:c0 + cols].rearrange("(k p) c -> p k c", p=128)
                P.add('pool', 'dma_start', dict(out=view, in_=src), R=[], W=self._keys(slot, pi, np_),
                      group="w%d_%d" % (slot, pi))

        def getm(self, parts, back=1):
            if P.dry:
                self.reqs.append(tuple(parts))
                return [V(v, [("wr", 0, 0)]) for v in self._views(0, parts)]
            i = self.pos
            self.pos += 1
            while self.issued <= min(len(self.reqs) - 1, i - back + NSLOT - 1):
                self._issue(self.issued)
                self.issued += 1
            slot = i % NSLOT
            return [V(v, self._keys(slot, pi, len(parts))) for pi, v in enumerate(self._views(slot, parts))]

        def get(self, wt, l, r0, nk, c0, cols, back=1):
            return self.getm([(wt, l, r0, nk, c0, cols)], back)[0]
    WS = WStream()

    def mm(out, lhsT, rhs, start, stop, R, W):
        P.add('pe', 'matmul', dict(out=out, lhsT=lhsT, rhs=rhs, start=start, stop=stop), R=R, W=W)

    def dve(meth, R, W, **kw):
        P.add('dve', meth, kw, R=R, W=W)

    def act(out, in_, func, R, W, **kw):
        P.add('act', 'activation', dict(out=out, in_=in_, func=func, **kw), R=R, W=W)

    def dma(q, out, in_, R, W, group):
        return P.add(q, 'dma_start', dict(out=out, in_=in_), R=R, W=W, group=group)

    def b3(ap2, n):
        return ap2.rearrange("p (a b) -> p a b", a=n)

    def rstd_of(src, eps_col):
        r = nxt("rs", vRS)
        act(r.ap, src.ap, AF.Sqrt, [src, vCST], [r], bias=CST[:, eps_col:eps_col + 1], scale=1.0)
        dve('reciprocal', [r], [r], out=r.ap, in_=r.ap)
        return r

    def norm(X, gcol0, Aout):
        pb = bank()
        for kc in range(16):
            sq = nxt("bt", vBT)
            act(sq.ap, X[kc].ap, AF.Square, [X[kc]], [sq])
            mm(pb.ap, ON_D[:], sq.ap, kc == 0, kc == 15, [sq, vOND], [pb])
        r = rstd_of(pb, 1562)
        for kc in range(16):
            dve('scalar_tensor_tensor', [X[kc], r, vSPR], [Aout[kc]], out=Aout[kc].ap, in0=X[kc].ap,
                scalar=SPR[:, gcol0 + kc:gcol0 + kc + 1], in1=r.ap, op0=ALU.mult, op1=ALU.mult)

    def rotary(src, gn, dst_ap, dstv):
        s3 = b3(src.ap, 4)
        ta = nxt("ft", vFT)
        tb = nxt("ft", vFT)
        dve('tensor_tensor', [src, vCC], [ta], out=b3(ta.ap, 4), in0=s3,
            in1=CC[:, gn, :].unsqueeze(1).to_broadcast([128, 4, 128]), op=ALU.mult)
        dve('tensor_tensor', [src, vSS], [tb], out=b3(tb.ap, 4)[:, :, 0:64], in0=s3[:, :, 64:128],
            in1=SS[:, gn, 0:64].unsqueeze(1).to_broadcast([128, 4, 64]), op=ALU.mult)
        dve('tensor_tensor', [src, vSS, tb], [tb], out=b3(tb.ap, 4)[:, :, 64:128], in0=s3[:, :, 0:64],
            in1=SS[:, gn, 64:128].unsqueeze(1).to_broadcast([128, 4, 64]), op=ALU.mult)
        dve('tensor_tensor', [ta, tb], [dstv], out=dst_ap, in0=ta.ap, in1=tb.ap, op=ALU.add)

    def tok_proj(l, coff, cg, n, Aall):
        pb = bank()
        w0, w1 = tok_proj.w
        for kc in range(16):
            w = w0 if kc < 8 else w1
            mm(pb.ap, A[kc].ap[:, n * 128:(n + 1) * 128], w.ap[:, kc % 8, :], kc == 0, kc == 15,
               [A[kc], w], [pb])
        return pb

    def load_x_tile(xsrc, xname, tt):
        t0 = tt * TT
        dma('sp', Y_all.ap, xsrc[:, t0:t0 + TT].rearrange("(k p) t -> p k t", p=128), [(xname, tt)], [Y_all], "ldx")

    def kv_update(n, write_sb):
        for hf in range(2):
            pb = bank()
            for hq in range(4):
                h = hf * 4 + hq
                mm(pb.ap[:, hq * 128:(hq + 1) * 128], KTM[n].ap[:, h * 128:(h + 1) * 128],
                   VTM[n].ap[:, h * 128:(h + 1) * 128], True, True, [KTM[n], VTM[n]], [pb])
            s = vST[hf]
            dve('tensor_tensor', [s, pb], [s], out=s.ap, in0=s.ap, in1=pb.ap, op=ALU.add)
            dve('tensor_tensor', [s, vCST], [s], out=s.ap, in0=s.ap,
                in1=CST[:, 144 + hf * 512:144 + (hf + 1) * 512], op=ALU.mult)
            if write_sb:
                act(SBS[n + 1][hf].ap, s.ap, AF.Copy, [s], [SBS[n + 1][hf]])

    def pre_kv_tile(l, tt, cbase):
        for (coff, isk) in ((O_K, True), (O_V, False)):
            for cg in range(2):
                w0 = WS.get(w_in, l, 0, 8, coff + cg * 512, 512)
                w1 = WS.get(w_in, l, 1024, 8, coff + cg * 512, 512)
                tok_proj.w = (w0, w1)
                for n in range(4):
                    pb = tok_proj(l, coff, cg, n, None)
                    if isk:
                        kd = nxt("ft", vFT)
                        dve('tensor_tensor', [pb, vCST], [kd], out=b3(kd.ap, 4), in0=b3(pb.ap, 4),
                            in1=CST[:, 136 + cg * 4:136 + cg * 4 + 4].unsqueeze(2).to_broadcast([128, 4, 128]),
                            op=ALU.mult)
                        rotary(kd, cbase + n, KTM[n].ap[:, cg * 512:(cg + 1) * 512], KTM[n])
                    else:
                        act(VTM[n].ap[:, cg * 512:(cg + 1) * 512], pb.ap, AF.Copy, [pb], [VTM[n]])

    def u_proj_conv_pool(l, tt, tails_only):
        dve('tensor_copy', [vUPT], [UP], out=UP.ap[:, :, 1:16], in_=UPT[:, :, 1:16])
        for half in range(2):
            w = WS.get(w_in, l, 0, 16, O_POOL + half * 256, 256)
            for gi in range(2):
                g = half * 2 + gi
                pb = bank()
                for kc in range(16):
                    mm(pb.ap, w.ap[:, kc, gi * 128:(gi + 1) * 128], A[kc].ap, kc == 0, kc == 15, [A[kc], w], [pb])
                act(UP.ap[:, g, 16:528], pb.ap, AF.Copy, [pb], [UP])
        dve('tensor_copy', [UP], [vUPT], out=UPT[:, :, 1:16], in_=UP.ap[:, :, 513:528])
        if not tails_only:
            wins = (2, 4, 8, 16)
            src = UP
            bufs = [SA, SBF]
            for g in range(4):
                sh = wins[g] // 2
                dstb = bufs[g % 2]
                dve('tensor_tensor', [src], [dstb], out=dstb.ap[:, g:4, 2 * sh:528], in0=src.ap[:, g:4, 2 * sh:528],
                    in1=src.ap[:, g:4, sh:528 - sh], op=ALU.add)
                dve('scalar_tensor_tensor', [dstb, UP], [PP[g]], out=PP[g].ap, in0=dstb.ap[:, g, 16:528],
                    scalar=1.0 / wins[g], in1=UP.ap[:, g, 16:528], op0=ALU.mult, op1=ALU.subtract)
                if tt == 0:
                    t1 = nxt("ft", vFT)
                    dve('tensor_tensor', [dstb, vCST], [t1], out=t1.ap[:, 0:16], in0=dstb.ap[:, g, 16:32],
                        in1=CST[:, 1240 + g * 16:1240 + (g + 1) * 16], op=ALU.mult)
                    dve('tensor_tensor', [t1, UP, PP[g]], [PP[g]], out=PP[g].ap[:, 0:16], in0=t1.ap[:, 0:16],
                        in1=UP.ap[:, g, 16:32], op=ALU.subtract)
                src = dstb
            for g in range(4):
                pb = bank()
                mm(pb.ap, PWB[:, g * 128:(g + 1) * 128], PP[g].ap, True, True, [PP[g], vPWB], [pb])
                dve('tensor_scalar', [pb, vSPR], [BR[g]], out=BR[g].ap, in0=pb.ap, scalar1=SPR[:, 64 + g:65 + g],
                    scalar2=None, op0=ALU.mult)
        for half in range(2):
            w = WS.get(w_in, l, 0, 16, O_CG + half * 256, 256)
            for gi in range(2):
                j = half * 2 + gi
                pb = bank()
                for kc in range(16):
                    mm(pb.ap, w.ap[:, kc, gi * 128:(gi + 1) * 128], A[kc].ap, kc == 0, kc == 15, [A[kc], w], [pb])
                act(SG.ap[:, j, :], pb.ap, AF.Sigmoid, [pb], [SG])
        dve('tensor_copy', [vHBT], [HB], out=HB.ap[:, :, 2:32], in_=HBT[:, :, 2:32])
        for half in range(2):
            w = WS.get(w_in, l, 0, 16, O_CA + half * 256, 256)
            for gi in range(2):
                j = half * 2 + gi
                pb = bank()
                for kc in range(16):
                    mm(pb.ap, w.ap[:, kc, gi * 128:(gi + 1) * 128], A[kc].ap, kc == 0, kc == 15, [A[kc], w], [pb])
                dve('tensor_tensor', [pb, SG], [HB], out=HB.ap[:, j, 32:544], in0=pb.ap, in1=SG.ap[:, j, :], op=ALU.mult)
        dve('tensor_copy', [HB], [vHBT], out=HBT[:, :, 2:32], in_=HB.ap[:, :, 514:544])
        if tails_only:
            return
        for j in range(4):
            pc = bank()
            for t in range(31):
                dg = nxt("dg", vDG)
                dve('tensor_scalar', [vIDN, vSPR], [dg], out=dg.ap, in0=IDN[:], scalar1=SPR[:, 88 + j * 31 + t:89 + j * 31 + t],
                    scalar2=None, op0=ALU.mult)
                mm(pc.ap, dg.ap, HB.ap[:, j, 2 + t:514 + t], t == 0, t == 30, [dg, HB], [pc])
            act(ACC.ap[:, j, :], pc.ap, AF.Identity, [pc, vSPR], [ACC], bias=SPR[:, 68 + j:69 + j], scale=1.0)
        if DEBUG and mode == 'full' and tt == 0:
            dma('sp', dbgHB, HB.ap.rearrange("p a b -> p (a b)"), [HB], [("dbgHB",)], "dbgHB")
            dma('sp', dbgACC, ACC.ap.rearrange("p a b -> p (a b)"), [ACC], [("dbgACC",)], "dbgACC")
            dma('sp', dbgSG, SG.ap.rearrange("p a b -> p (a b)"), [SG], [("dbgSG",)], "dbgSG")
        pm = bank()
        pq = bank()
        for j in range(4):
            c16 = nxt("bt", vBT)
            act(c16.ap, ACC.ap[:, j, :], AF.Copy, [ACC], [c16])
            mm(pm.ap, ON_5[:], c16.ap, j == 0, j == 3, [c16, vON5], [pm])
            s16 = nxt("bt", vBT)
            act(s16.ap, ACC.ap[:, j, :], AF.Square, [ACC], [s16])
            mm(pq.ap, ON_5[:], s16.ap, j == 0, j == 3, [s16, vON5], [pq])
        ms = nxt("ft", vFT)
        act(ms.ap, pm.ap, AF.Copy, [pm], [ms])
        m2 = nxt("ft", vFT)
        dve('tensor_tensor', [ms], [m2], out=m2.ap, in0=ms.ap, in1=ms.ap, op=ALU.mult)
        dve('tensor_tensor', [pq, m2], [m2], out=m2.ap, in0=pq.ap, in1=m2.ap, op=ALU.subtract)
        r = rstd_of(m2, 1563)
        tpair = [nxt("ft", vFT), nxt("ft", vFT)]
        for j in range(4):
            t = tpair[j % 2]
            dve('tensor_tensor', [ACC, ms], [t], out=t.ap, in0=ACC.ap[:, j, :], in1=ms.ap, op=ALU.subtract)
            dve('tensor_tensor', [t, r], [t], out=t.ap, in0=t.ap, in1=r.ap, op=ALU.mult)
            act(BR[4 + j].ap, t.ap, AF.Silu, [t, vSPR], [BR[4 + j]], scale=SPR[:, 72 + j:73 + j],
                bias=SPR[:, 76 + j:77 + j])

    def post_norm_residual(Yo, gcol0, xget, tt):
        r = rstd_of(PSTAT, 1562)
        for dc in range(16):
            xv = xget(dc)
            t = nxt("ft", vFT)
            dve('scalar_tensor_tensor', [Yo[dc], r, vSPR], [t], out=t.ap, in0=Yo[dc].ap,
                scalar=SPR[:, gcol0 + dc:gcol0 + dc + 1], in1=r.ap, op0=ALU.mult, op1=ALU.mult)
            dve('tensor_tensor', [t, xv], [Y[dc]], out=Y[dc].ap, in0=t.ap, in1=xv.ap, op=ALU.add)

    def evac_stats(pb, Yo, dc):
        act(Yo[dc].ap, pb.ap, AF.Copy, [pb], [Yo[dc]])
        sq = nxt("bt", vBT)
        act(sq.ap, pb.ap, AF.Square, [pb], [sq])
        mm(PSTAT.ap, ON_D[:], sq.ap, dc == 0, dc == 15, [sq, vOND], [PSTAT])

    def body():
        rot.clear()
        out_dmas = []
        try:
            body_main(out_dmas)
        except _Stop:
            pass
        body_fin(out_dmas)

    def body_main(out_dmas):
        dma('sp', CST[:], cstD[:, :], [], [vCST], "ldc")
        dma('sp', POSI[:], posT[:, :], [], [vPOSI], "ldp")
        dve('tensor_copy', [vCST], [vIDN], out=IDN[:], in_=CST[:, 1434:1562])
        dve('memset', [], [vOND], ap=ON_D[:], constant=1.0 / 2048)
        dve('memset', [], [vON5], ap=ON_5[:], constant=1.0 / 512)
        dve('memset', [], [vON1], ap=ON_1[:], constant=1.0 / 128)
        posf = vFT[0]
        dve('tensor_copy', [vPOSI], [posf], out=posf.ap[:, 0:NCH], in_=POSI[:])
        T1 = arv(0, [128, 16, 128], F32)
        T2 = arv(8 * KB, [128, 16, 128], F32)
        T3 = arv(16 * KB, [128, 16, 128], F32)
        TIv = arv(24 * KB, [128, 16, 128], F32)
        TI = V(TIv.ap.bitcast(I32), TIv.k)
        T4 = arv(32 * KB, [128, 16, 128], F32)
        T5 = arv(40 * KB, [128, 16, 128], F32)
        for piece in range(NCH // 16):
            if seq:
                cdst, sdst, cv, sv_ = T4.ap, T5.ap, T4, T5
            else:
                cdst, sdst, cv, sv_ = CC[:], SS[:], vCC, vSS
            dve('tensor_tensor', [posf, vCST], [T1], out=T1.ap,
                in0=posf.ap[:, piece * 16:(piece + 1) * 16].unsqueeze(2).to_broadcast([128, 16, 128]),
                in1=CST[:, 0:128].unsqueeze(1).to_broadcast([128, 16, 128]), op=ALU.mult)
            dve('tensor_scalar', [T1], [T2], out=T2.ap, in0=T1.ap, scalar1=1.0 / (2 * PI), scalar2=None, op0=ALU.mult)
            dve('tensor_copy', [T2], [TI], out=TI.ap, in_=T2.ap)
            dve('tensor_copy', [TI], [T2], out=T2.ap, in_=TI.ap)
            dve('scalar_tensor_tensor', [T2, T1], [T1], out=T1.ap, in0=T2.ap, scalar=-2 * PI, in1=T1.ap, op0=ALU.mult, op1=ALU.add)
            act(T2.ap, T1.ap, AF.Sin, [T1], [T2], scale=0.5)
            act(T3.ap, T1.ap, AF.Sin, [T1], [T3], scale=0.25)
            dve('tensor_tensor', [T2], [cv], out=cdst, in0=T2.ap, in1=T2.ap, op=ALU.mult)
            dve('tensor_scalar', [cv], [cv], out=cdst, in0=cdst, scalar1=-2.0, scalar2=1.0, op0=ALU.mult, op1=ALU.add)
            dve('tensor_tensor', [T3], [T3], out=T3.ap, in0=T3.ap, in1=T3.ap, op=ALU.mult)
            dve('tensor_scalar', [T3], [T3], out=T3.ap, in0=T3.ap, scalar1=-2.0, scalar2=1.0, op0=ALU.mult, op1=ALU.add)
            dve('scalar_tensor_tensor', [T2, T3], [sv_], out=sdst[:, :, 64:128], in0=T2.ap[:, :, 64:128], scalar=2.0,
                in1=T3.ap[:, :, 64:128], op0=ALU.mult, op1=ALU.mult)
            dve('scalar_tensor_tensor', [T2, T3, sv_], [sv_], out=sdst[:, :, 0:64], in0=T2.ap[:, :, 0:64], scalar=-2.0,
                in1=T3.ap[:, :, 0:64], op0=ALU.mult, op1=ALU.mult)
            if seq:
                dma('sp', ccD[:, piece * 2048:(piece + 1) * 2048], T4.ap.rearrange("p a b -> p (a b)"), [T4], [("ccD", piece)], "stcc")
                dma('sp', ssD[:, piece * 2048:(piece + 1) * 2048], T5.ap.rearrange("p a b -> p (a b)"), [T5], [("ssD", piece)], "stss")

        if DEBUG and mode == 'full':
            dma('sp', dbgCC, CC[:].rearrange("p a b -> p (a b)"), [vCC], [("dbgCC",)], "dbgCC")
            dma('sp', dbgSS, SS[:].rearrange("p a b -> p (a b)"), [vSS], [("dbgSS",)], "dbgSS")
        for l in range(L) if True else []:
            if mode == 'fused' or seq:
                xsrc = xT if l == 0 else xbuf[(l - 1) % 2]
                xdst = yT if l == L - 1 else xbuf[l % 2]
                xsn = "xT" if l == 0 else "xb%d" % ((l - 1) % 2)
                xdn = "yT" if l == L - 1 else "xb%d" % (l % 2)
            else:
                xsrc = xT
                xdst = None if mode == 'pre' else yT
                xsn, xdn = "xT", "yT"
            dma('sp', SPR[:], spD[l], [], [vSPR], "ldsp")
            dma('pool', PWB[:], pwD[l], [], [vPWB], "ldpw")
            dve('memset', [], [vST[0]], ap=ST[:, 0:512], constant=0.0)
            dve('memset', [], [vST[1]], ap=ST[:, 512:1024], constant=0.0)
            dve('memset', [], [vUPT], ap=UPT[:], constant=0.0)
            dve('memset', [], [vHBT], ap=HBT[:], constant=0.0)
            for tt in range(0 if seq else NTILE):
                t0 = tt * TT
                load_x_tile(xsrc, xsn, tt)
                norm(Y, 0, A)
                pre_kv_tile(l, tt, tt * 4)
                dma('sp', ktmD[t0:t0 + TT, :].rearrange("(n p) c -> p n c", p=128), KTM_all.ap, [KTM_all], [("ktmD", tt)], "stk")
                dma('sp', vtmD[t0:t0 + TT, :].rearrange("(n p) c -> p n c", p=128), VTM_all.ap, [VTM_all], [("vtmD", tt)], "stv")
                for n in range(4):
                    kv_update(n, False)
                if tt == NTILE - 1:
                    u_proj_conv_pool(l, tt, True)
            if mode == 'pre':
                xo = xoutD
            elif mode == 'fused':
                xo = xchD
            if mode in ('pre', 'fused'):
                i1 = dma('sp', xo[:, 0:512], ST[:, 0:512], [vST[0]], [("xo", 0)], "sx0")
                i2 = dma('sp', xo[:, 512:1024], ST[:, 512:1024], [vST[1]], [("xo", 1)], "sx1")
                i3 = dma('sp', xo[:, 1024:1152], HBT[:].rearrange("p a b -> p (a b)"), [vHBT], [("xo", 2)], "sx2")
                i4 = dma('sp', xo[:, 1152:1216], UPT[:].rearrange("p a b -> p (a b)"), [vUPT], [("xo", 3)], "sx3")
                out_dmas += [i1, i2, i3, i4]
            if mode == 'pre':
                continue
            if mode == 'fused':
                P.add('pool', 'collective_compute',
                      dict(kind="AllGather", op=ALU.bypass, replica_groups=[list(range(8))],
                           ins=[xchD[:, :]], outs=[xgD[:, :]]),
                      R=[("xo", 0), ("xo", 1), ("xo", 2), ("xo", 3)], W=[("xg",)], group="cc")
                xin_src = xgD
            elif not seq:
                xin_src = xinD
            if not seq:
                dve('memset', [vST[0]], [vST[0]], ap=ST[:, 0:512], constant=0.0)
                dve('memset', [vST[1]], [vST[1]], ap=ST[:, 512:1024], constant=0.0)
                dve('memset', [vUPT], [vUPT], ap=UPT[:], constant=0.0)
                dve('memset', [vHBT], [vHBT], ap=HBT[:], constant=0.0)
            for r in range(0 if seq else 8):
                xt = XT_[r % 2]
                dma('sp', xt.ap, xin_src[r * 128:(r + 1) * 128, :], [("xg",)], [xt], "ldxg%d" % (r % 2))
                for h in range(8):
                    s = vST[h // 4]
                    dve('scalar_tensor_tensor', [xt, s, vCST], [s], out=ST[:, h * 128:(h + 1) * 128],
                        in0=xt.ap[:, h * 128:(h + 1) * 128], scalar=CST[:, 1168 + r * 8 + h:1169 + r * 8 + h],
                        in1=ST[:, h * 128:(h + 1) * 128], op0=ALU.mult, op1=ALU.add)
                hb2 = HBT[:].rearrange("p a b -> p (a b)")
                dve('scalar_tensor_tensor', [xt, vHBT, vCST], [vHBT], out=hb2, in0=xt.ap[:, 1024:1152],
                    scalar=CST[:, 1232 + r:1233 + r], in1=hb2, op0=ALU.mult, op1=ALU.add)
                up2 = UPT[:].rearrange("p a b -> p (a b)")
                dve('scalar_tensor_tensor', [xt, vUPT, vCST], [vUPT], out=up2, in0=xt.ap[:, 1152:1216],
                    scalar=CST[:, 1232 + r:1233 + r], in1=up2, op0=ALU.mult, op1=ALU.add)
            for tt in range(NTILE):
                t0 = tt * TT
                if seq:
                    dma('sp', CC[:], ccD[:, tt * 512:(tt + 1) * 512].rearrange("p (a b) -> p a b", a=4), [("ccD", tt // 4)], [vCC], "ldcc")
                    dma('sp', SS[:], ssD[:, tt * 512:(tt + 1) * 512].rearrange("p (a b) -> p a b", a=4), [("ssD", tt // 4)], [vSS], "ldss")
                load_x_tile(xsrc, xsn, tt)
                norm(Y, 0, A)
                if DEBUG and tt == 0:
                    dma('sp', dbgA, AR[:, 0:8192], [A_all], [("dbgA",)], "dbgA")
                u_proj_conv_pool(l, tt, False)
                if STOP == 'proj' and tt == 0:
                    raise _Stop()
                for cg in range(2):
                    w0 = WS.get(w_in, l, 0, 8, O_Q + cg * 512, 512)
                    w1 = WS.get(w_in, l, 1024, 8, O_Q + cg * 512, 512)
                    tok_proj.w = (w0, w1)
                    for n in range(4):
                        pb = tok_proj(l, O_Q, cg, n, None)
                        qd = nxt("ft", vFT)
                        dve('tensor_tensor', [pb, vCST], [qd], out=b3(qd.ap, 4), in0=b3(pb.ap, 4),
                            in1=CST[:, 128 + cg * 4:128 + cg * 4 + 4].unsqueeze(2).to_broadcast([128, 4, 128]),
                            op=ALU.mult)
                        rotary(qd, (0 if seq else tt * 4) + n, QTM[n].ap[:, cg * 512:(cg + 1) * 512], QTM[n])
                if seq:
                    pre_kv_tile(l, tt, 0)
                else:
                    dma('sp', KTM_all.ap, ktmD[t0:t0 + TT, :].rearrange("(n p) c -> p n c", p=128), [("ktmD", tt)], [KTM_all], "ldk")
                    dma('sp', VTM_all.ap, vtmD[t0:t0 + TT, :].rearrange("(n p) c -> p n c", p=128), [("vtmD", tt)], [VTM_all], "ldv")
                if DEBUG and tt == 0:
                    dma('sp', dbgQ, AR[:, 24 * KB:28 * KB], [QTM], [("dbgQ",)], "dbgQ")
                    dma('sp', dbgK, AR[:, 28 * KB:32 * KB], [KTM], [("dbgK",)], "dbgK")
                for (src, dst) in ((QTM, QT), (KTM, KT)):
                    for h in range(8):
                        pb = bank()
                        pbb = pb.ap.bitcast(BF)
                        for n in range(4):
                            P.add('pe', 'transpose', dict(out=pbb[:, n * 128:(n + 1) * 128],
                                                          in_=src[n].ap[:, h * 128:(h + 1) * 128], identity=IDN[:]),
                                  R=[src[n], vIDN], W=[pb])
                        act(dst[h].ap, pbb[:, 0:512], AF.Copy, [pb], [dst[h]])
                if STOP == 'q' and tt == 0:
                    raise _Stop()
                def stage_G(i4):
                    w = WS.get(w_in, l, 0, 16, O_GR + i4 * 256, 256)
                    for gi in range(2):
                        h = i4 * 2 + gi
                        pb = bank()
                        for kc in range(16):
                            mm(pb.ap, w.ap[:, kc, gi * 128:(gi + 1) * 128], A[kc].ap, kc == 0, kc == 15, [A[kc], w], [pb])
                        act(GS[h].ap, pb.ap, AF.Silu, [pb], [GS[h]])
                for i4 in range(4):
                    stage_G(i4)
                for hf in range(2):
                    act(SBS[0][hf].ap, vST[hf].ap, AF.Copy, [vST[hf]], [SBS[0][hf]])
                for n in range(4):
                    kv_update(n, True)
                OFb = [arv(32 * KB + i * 2 * KB, [128, 512], F32) for i in range(2)]
                MSb = [arv(36 * KB + i * 2 * KB, [128, 512], F32) for i in range(2)]
                M2b = [arv(40 * KB + i * 2 * KB, [128, 512], F32) for i in range(2)]
                OBb = [arv(44 * KB + i * KB, [128, 512], BF) for i in range(2)]
                OQb = [arv(46 * KB + i * KB, [128, 512], BF) for i in range(2)]
                smts = {}
                pOs = {}

                def stage_S(h):
                    pS = bank()
                    for n in range(4):
                        mm(pS.ap[:, n * 128:(n + 1) * 128], KT[h].ap[:, n * 128:(n + 1) * 128],
                           QT[h].ap[:, n * 128:(n + 1) * 128], True, True, [KT[h], QT[h]], [pS])
                    smt = nxt("smt", vSMT)
                    dve('tensor_tensor', [pS, vCST], [smt], out=b3(smt.ap, 4), in0=b3(pS.ap, 4),
                        in1=CST[:, 1306:1434].unsqueeze(1).to_broadcast([128, 4, 128]), op=ALU.mult)
                    smts[h] = smt

                def stage_O(h):
                    hf, hq = h // 4, h % 4
                    smt = smts[h]
                    pO = bank()
                    for n in range(4):
                        mm(pO.ap[:, n * 128:(n + 1) * 128], VTM[n].ap[:, h * 128:(h + 1) * 128],
                           smt.ap[:, n * 128:(n + 1) * 128], True, False, [VTM[n], smt], [pO])
                        mm(pO.ap[:, n * 128:(n + 1) * 128], SBS[n][hf].ap[:, hq * 128:(hq + 1) * 128],
                           QT[h].ap[:, n * 128:(n + 1) * 128], False, True, [SBS[n][hf], QT[h]], [pO])
                    of, ob, osq = OFb[h % 2], OBb[h % 2], OQb[h % 2]
                    act(ob.ap, pO.ap, AF.Copy, [pO], [ob])
                    act(osq.ap, pO.ap, AF.Square, [pO], [osq])
                    act(of.ap, pO.ap, AF.Copy, [pO], [of])

                def stage_T(h):
                    of, ob, osq, ms, m2 = OFb[h % 2], OBb[h % 2], OQb[h % 2], MSb[h % 2], M2b[h % 2]
                    pm = bank()
                    mm(pm.ap, ON_1[:], ob.ap, True, True, [ob, vON1], [pm])
                    pq = bank()
                    mm(pq.ap, ON_1[:], osq.ap, True, True, [osq, vON1], [pq])
                    act(ms.ap, pm.ap, AF.Copy, [pm], [ms])
                    dve('tensor_tensor', [ms], [m2], out=m2.ap, in0=ms.ap, in1=ms.ap, op=ALU.mult)
                    dve('tensor_tensor', [pq, m2], [m2], out=m2.ap, in0=pq.ap, in1=m2.ap, op=ALU.subtract)
                    r = rstd_of(m2, 1563)
                    dve('tensor_tensor', [of, ms], [of], out=of.ap, in0=of.ap, in1=ms.ap, op=ALU.subtract)
                    dve('tensor_tensor', [of, r], [of], out=of.ap, in0=of.ap, in1=r.ap, op=ALU.mult)
                    dve('scalar_tensor_tensor', [of, GS[h], vSPR], [BR[8 + h]], out=BR[8 + h].ap, in0=of.ap,
                        scalar=SPR[:, 80 + h:81 + h], in1=GS[h].ap, op0=ALU.mult, op1=ALU.mult)
                for step in range(10):
                    if step < 8:
                        stage_S(step)
                    if 0 <= step - 1 < 8:
                        stage_O(step - 1)
                    if 0 <= step - 2 < 8:
                        stage_T(step - 2)
                if DEBUG and tt == 0:
                    dma('sp', dbgBR, AR[:, 8192:16384], [BR], [("dbgBR",)], "dbgBR")
                if STOP == 'ret' and tt == 0:
                    raise _Stop()
                accs = [vFT[0], vFT[1]]
                sgb = [vFT[2], vFT[3]]
                for grp in range(8):
                    wbr = WS.getm([(w_pp, l, 0, 4, grp * 256, 256), (w_cp, l, 0, 4, grp * 256, 256),
                                   (w_rp, l, 0, 8, grp * 256, 256)])
                    for b in range(3):
                        wg = WS.get(w_in, l, 0, 16, O_GATE + b * 2048 + grp * 256, 256, back=b + 1)
                        nkb, off = ((4, 0), (4, 4), (8, 8))[b]
                        for dci in range(2):
                            dc = grp * 2 + dci
                            pg = bank()
                            for kc in range(16):
                                mm(pg.ap, wg.ap[:, kc, dci * 128:(dci + 1) * 128], A[kc].ap, kc == 0, kc == 15,
                                   [A[kc], wg], [pg])
                            sg = sgb[dci]
                            act(sg.ap, pg.ap, AF.Sigmoid, [pg], [sg])
                            py = bank()
                            for kc in range(nkb):
                                mm(py.ap, wbr[b].ap[:, kc, dci * 128:(dci + 1) * 128], BR[off + kc].ap, kc == 0,
                                   kc == nkb - 1, [BR[off + kc], wbr[b]], [py])
                            if b == 0:
                                dve('tensor_tensor', [sg, py], [accs[dci]], out=accs[dci].ap, in0=sg.ap, in1=py.ap, op=ALU.mult)
                            else:
                                dve('tensor_tensor', [sg, py], [sg], out=sg.ap, in0=sg.ap, in1=py.ap, op=ALU.mult)
                                if b == 1:
                                    dve('tensor_tensor', [accs[dci], sg], [accs[dci]], out=accs[dci].ap, in0=accs[dci].ap,
                                        in1=sg.ap, op=ALU.add)
                                else:
                                    dve('tensor_tensor', [accs[dci], sg], [M[dc]], out=M[dc].ap, in0=accs[dci].ap,
                                        in1=sg.ap, op=ALU.add)
                if DEBUG and tt == 0:
                    dma('sp', dbgM, AR[:, 16384:24576], [M], [("dbgM",)], "dbgM")
                if STOP == 'merge' and tt == 0:
                    raise _Stop()
                for cgi in range(8):
                    w = WS.get(w_out, l, 0, 16, cgi * 256, 256)
                    for dci in range(2):
                        dc = cgi * 2 + dci
                        pb = bank()
                        for kc in range(16):
                            mm(pb.ap, w.ap[:, kc, dci * 128:(dci + 1) * 128], M[kc].ap, kc == 0, kc == 15, [M[kc], w], [pb])
                        evac_stats(pb, Y, dc)

                def xget(dc):
                    xc = nxt("xc", vXC)
                    dma('sp', xc.ap, xsrc[dc * 128:(dc + 1) * 128, t0:t0 + TT], [(xsn, tt)], [xc], "ldxc%d" % (rot["xc"] % 2))
                    return xc
                post_norm_residual(Y, 16, xget, tt)
                if DEBUG and tt == 0:
                    dma('sp', dbgY, Y_all.ap.rearrange("p a b -> p (a b)"), [Y_all], [("dbgY",)], "dbgY")
                if STOP == 'wout' and tt == 0:
                    raise _Stop()
                norm(Y, 32, A)
                for jg in range(22):
                    wa = WS.get(w_fi, l, 0, 16, jg * 256, 256)
                    wb = WS.get(w_fi, l, 0, 16, FH + jg * 256, 256)
                    for ji in range(2):
                        j = jg * 2 + ji
                        pa = bank()
                        for kc in range(16):
                            mm(pa.ap, wa.ap[:, kc, ji * 128:(ji + 1) * 128], A[kc].ap, kc == 0, kc == 15, [A[kc], wa], [pa])
                        sa = nxt("ft", vFT)
                        act(sa.ap, pa.ap, AF.Silu, [pa], [sa])
                        pbb = bank()
                        for kc in range(16):
                            mm(pbb.ap, wb.ap[:, kc, ji * 128:(ji + 1) * 128], A[kc].ap, kc == 0, kc == 15, [A[kc], wb], [pbb])
                        dve('tensor_tensor', [sa, pbb], [HID[j]], out=HID[j].ap, in0=sa.ap, in1=pbb.ap, op=ALU.mult)
                for cgi in range(8):
                    pbs = [bank(), bank()]
                    for rg in range(3):
                        nk = 16 if rg < 2 else 12
                        w = WS.get(w_fo, l, rg * 2048, nk, cgi * 256, 256)
                        for dci in range(2):
                            for kk in range(nk):
                                mm(pbs[dci].ap, w.ap[:, kk, dci * 128:(dci + 1) * 128], HID[rg * 16 + kk].ap,
                                   rg == 0 and kk == 0, rg == 2 and kk == nk - 1, [HID[rg * 16 + kk], w], [pbs[dci]])
                    for dci in range(2):
                        evac_stats(pbs[dci], Y2, cgi * 2 + dci)
                post_norm_residual(Y2, 48, lambda dc: Y[dc], tt)
                i = dma('sp', xdst[:, t0:t0 + TT].rearrange("(k p) t -> p k t", p=128), Y_all.ap, [Y_all],
                        [(xdn, tt)], "sty")
                if l == L - 1:
                    out_dmas.append(i)
            if mode == 'fused' and l < L - 1:
                pass
    def body_fin(out_dmas):
        if not P.dry:
            lastd = {}
            for i, op in enumerate(P.ops):
                if op['dma']:
                    lastd[op['stream']] = i
            out_dmas = list(out_dmas) + list(lastd.values())
            P.ops.append(dict(eng='sp', meth=None, kw=None, deps=set(out_dmas), stream='sp', seq=P.seqc.get('sp', 0) + 1,
                              dma=False, ms=False))
            P.seqc['sp'] = P.seqc.get('sp', 0) + 1

    P.dry = True
    body()
    P.dry = False
    body()
    streams = P.finalize()
    sems = {}
    for s in streams:
        sems[s] = es.enter_context(nc.semaphore("s_" + s))
    with nc.Block() as block:
        P.emit(nc, block, sems)
    es.close()
    return nc, len(P.ops)


_GAMMA = 1.0 - np.exp2(-5.0 - np.arange(8, dtype=np.float64))


def _consts(core):
    s = core % 4
    b = core // 4
    c = np.zeros((128, NCST), np.float32)
    half = 64
    inv = (np.float32(10000.0) ** (-np.arange(half, dtype=np.float32) / np.float32(half))).astype(np.float32)
    c[:, 0:64] = inv[None, :]
    c[:, 64:128] = inv[None, :]
    p = np.arange(128, dtype=np.float64)
    lg = np.log(_GAMMA)
    c[:, 128:136] = np.exp((p[:, None] + 1.0) * lg[None, :])
    c[:, 136:144] = np.exp(-(p[:, None] + 1.0) * lg[None, :]) * (128.0 ** -0.5)
    gC = np.exp(128.0 * lg)
    c[:, 144:1168] = np.repeat(gC, 128)[None, :]
    G = np.exp(2048.0 * lg)
    for r in range(8):
        rb, rs = r // 4, r % 4
        if rb == b and rs < s:
            c[:, 1168 + r * 8:1168 + r * 8 + 8] = (G ** (s - 1 - rs))[None, :]
        if rb == b and rs == s - 1:
            c[:, 1232 + r] = 1.0
    wins = (2, 4, 8, 16)
    for g in range(4):
        t = np.arange(16)
        if s == 0:
            c[:, 1240 + g * 16:1240 + (g + 1) * 16] = (1.0 / np.minimum(t + 1, wins[g]))[None, :]
        else:
            c[:, 1240 + g * 16:1240 + (g + 1) * 16] = 1.0 / wins[g]
    c[:, 1304] = -np.pi
    c[:, 1305] = np.pi
    e = np.arange(128)
    c[:, 1306:1434] = (e[:, None] <= e[None, :]).astype(np.float32)
    c[:, 1434:1562] = np.eye(128, dtype=np.float32)
    c[:, 1562] = 1e-6
    c[:, 1563] = 1e-5
    return c


def _fm(v, n):
    return np.ascontiguousarray(v.reshape(n, 128).T)


def _pack_sp(inp, l):
    sp = np.zeros((128, NSP), np.float32)
    sp[:, 0:16] = _fm(inp["g_mix_pre"][l], 16)
    sp[:, 16:32] = _fm(inp["g_mix_post"][l], 16)
    sp[:, 32:48] = _fm(inp["g_ffn_pre"][l], 16)
    sp[:, 48:64] = _fm(inp["g_ffn_post"][l], 16)
    sp[:, 64:68] = _fm(inp["pool_scale"][l], 4)
    sp[:, 68:72] = _fm(inp["conv_b"][l], 4)
    sp[:, 72:76] = _fm(inp["conv_ln_g"][l], 4)
    sp[:, 76:80] = _fm(inp["conv_ln_b"][l], 4)
    sp[:, 80:88] = _fm(inp["ret_gn_g"][l], 8)
    dw = inp["conv_dw"][l]
    sp[:, 88:212] = dw.T.reshape(4, 128, 31).transpose(1, 0, 2).reshape(128, 124)
    return sp


def _pack_pw(inp, l):
    pw = inp["pool_w"][l]
    return np.ascontiguousarray(pw.transpose(1, 0, 2).reshape(128, 512))


_CACHE = {}


def _get(L, mode, ntok=2048):
    k = (L, mode, ntok)
    if k not in _CACHE:
        _CACHE[k] = build(L, mode, ntok)[0]
    return _CACHE[k]


def _core_x(x, c):
    b, s = c // 4, c % 4
    return np.ascontiguousarray(x[b, s * NTOK:(s + 1) * NTOK, :].T)


def _core_pos(pos, c):
    b, s = c // 4, c % 4
    return np.ascontiguousarray(pos[b, s * NTOK:(s + 1) * NTOK].reshape(16, 128).T.astype(np.int32))


def _layer_maps(inp, layers, xTs):
    sp = np.stack([_pack_sp(inp, l) for l in layers])
    pw = np.stack([_pack_pw(inp, l) for l in layers])
    sl = layers if len(layers) > 1 else slice(layers[0], layers[0] + 1)
    ws = dict(w_in=inp["w_in"][sl], w_pp=inp["w_pool_proj"][sl], w_cp=inp["w_conv_proj"][sl],
              w_rp=inp["w_ret_proj"][sl], w_out=inp["w_out"][sl], w_fi=inp["w_ffn_in"][sl], w_fo=inp["w_ffn_out"][sl])
    maps = []
    for c in range(8):
        m = dict(xT=xTs[c], posT=_core_pos(inp["positions"], c), cst=_CSTS[c], sp=sp, pw=pw)
        m.update(ws)
        maps.append(m)
    return maps


_CSTS = None


def kernel_unfused(**inputs):
    inp = {k: np.asarray(v) for k, v in inputs.items()}
    global _CSTS
    _CSTS = [_consts(c) for c in range(8)]
    x = inp["x"].astype(np.float32, copy=False)
    xTs = [_core_x(x, c) for c in range(8)]
    for l in range(4):
        maps = _layer_maps(inp, [l], xTs)
        nc_pre = _get(1, 'pre')
        pre_keys = ("xT", "posT", "cst", "sp", "pw", "w_in")
        res = run_bass_kernel_spmd(nc_pre, [{k: m[k] for k in pre_keys} for m in maps], core_ids=list(range(8)))
        xin = np.concatenate([res.results[c]["xout"] for c in range(8)], axis=0)
        for m in maps:
            m["xin"] = xin
        nc_full = _get(1, 'full')
        res = run_bass_kernel_spmd(nc_full, maps, core_ids=list(range(8)))
        xTs = [res.results[c]["yT"] for c in range(8)]
    out = np.empty((2, 8192, DM), np.float32)
    for c in range(8):
        b, s = c // 4, c % 4
        out[b, s * NTOK:(s + 1) * NTOK, :] = xTs[c].T
    return out


def kernel(**inputs):
    inp = {k: np.asarray(v) for k, v in inputs.items()}
    S = 8192
    nc = _get(4, 'seq', S)
    sp = np.stack([_pack_sp(inp, l) for l in range(4)])
    pw = np.stack([_pack_pw(inp, l) for l in range(4)])
    cst = _consts(0)
    x = inp["x"].astype(np.float32, copy=False)
    maps = []
    for b in range(2):
        maps.append(dict(
            xT=np.ascontiguousarray(x[b].T),
            posT=np.ascontiguousarray(inp["positions"][b].reshape(S // 128, 128).T.astype(np.int32)),
            cst=cst, sp=sp, pw=pw,
            w_in=inp["w_in"], w_pp=inp["w_pool_proj"], w_cp=inp["w_conv_proj"], w_rp=inp["w_ret_proj"],
            w_out=inp["w_out"], w_fi=inp["w_ffn_in"], w_fo=inp["w_ffn_out"]))
    res = run_bass_kernel_spmd(nc, maps, core_ids=[0, 1])
    out = np.empty((2, S, DM), np.float32)
    for b in range(2):
        out[b] = res.results[b]["yT"].T
    return out
```

```python
import numpy as np
from contextlib import ExitStack
import concourse.bass as bass
import concourse.mybir as mybir
from concourse.bass_utils import run_bass_kernel_spmd

F32 = mybir.dt.float32
BF = mybir.dt.bfloat16
I32 = mybir.dt.int32
ALU = mybir.AluOpType
AF = mybir.ActivationFunctionType

DM = 2048
NTOK = 2048
TT = 512
NTILE = NTOK // TT
NIN = 11776
FH = 5632
O_POOL, O_CA, O_CG, O_Q, O_K, O_V, O_GR, O_GATE = 0, 512, 1024, 1536, 2560, 3584, 4608, 5632
NSLOT = 6
XW = 1216
NSP = 212
NCST = 1564
PI = float(np.pi)
DEBUG = False
STOP = None


class _Stop(Exception):
    pass


class V:
    def __init__(self, ap, k):
        self.ap = ap
        self.k = k


def _flat(lst):
    out = []
    for x in lst:
        if isinstance(x, V):
            out.extend(x.k)
        elif isinstance(x, list):
            out.extend(_flat(x))
        else:
            out.append(x)
    return out


class Prog:
    ENG = ['pe', 'act', 'dve', 'pool', 'sp']

    def __init__(self):
        self.ops = []
        self.st = {}
        self.seqc = {}
        self.dry = False

    def add(self, eng, meth, kw, R=(), W=(), group=None):
        if self.dry:
            return -1
        R = _flat(list(R))
        W = _flat(list(W))
        idx = len(self.ops)
        stream = group if group is not None else eng
        deps = set()
        for k in R:
            e = self.st.get(k)
            if e is not None and e[0] is not None:
                deps.add(e[0])
        for k in W:
            e = self.st.get(k)
            if e is not None:
                if e[0] is not None:
                    deps.add(e[0])
                deps.update(e[1].values())
        seq = self.seqc.get(stream, 0) + 1
        self.seqc[stream] = seq
        self.ops.append(dict(eng=eng, meth=meth, kw=kw, deps=deps, stream=stream, seq=seq,
                             dma=group is not None, ms=False))
        for k in R:
            e = self.st.get(k)
            if e is None:
                e = [None, {}]
                self.st[k] = e
            e[1][stream] = idx
        for k in W:
            self.st[k] = [idx, {}]
        return idx

    def finalize(self):
        ops = self.ops
        hasdep = set()
        for op in ops:
            hasdep.update(op['deps'])
        know = {e: {} for e in self.ENG}
        snap = {}
        for i, op in enumerate(ops):
            E = op['eng']
            kn = know[E]
            waits = []
            for j in sorted(op['deps'], reverse=True):
                d = ops[j]
                if d['stream'] == 'pe' and E == 'pe':
                    continue
                if kn.get(d['stream'], 0) >= d['seq']:
                    continue
                waits.append(j)
                d['ms'] = True
                for s2, sq in snap[j].items():
                    if kn.get(s2, 0) < sq:
                        kn[s2] = sq
            op['waits'] = waits
            if i in hasdep:
                sn = dict(kn)
                if sn.get(op['stream'], 0) < op['seq']:
                    sn[op['stream']] = op['seq']
                snap[i] = sn
        cnt = {}
        for op in ops:
            if op['dma']:
                op['ms'] = True
            if op['ms']:
                c = cnt.get(op['stream'], 0) + (16 if op['dma'] else 1)
                cnt[op['stream']] = c
                op['cnt'] = c
        return sorted(cnt.keys())

    def emit(self, nc, block, sems):
        per = {e: [] for e in self.ENG}
        for op in self.ops:
            per[op['eng']].append(op)
        ops = self.ops

        def mk(E):
            def f(e):
                for op in per[E]:
                    for j in op['waits']:
                        d = ops[j]
                        e.wait_ge(sems[d['stream']], d['cnt'])
                    if op['meth'] is None:
                        continue
                    ins = getattr(e, op['meth'])(**op['kw'])
                    if op['ms']:
                        ins.then_inc(sems[op['stream']], 16 if op['dma'] else 1)
            return f
        block.tensor(mk('pe'))
        block.scalar(mk('act'))
        block.vector(mk('dve'))
        block.gpsimd(mk('pool'))
        block.sync(mk('sp'))


def build(L, mode, ntok=2048):
    nc = bass.Bass("TRN2", target_bir_lowering=False)
    NTOK = ntok
    NTILE = ntok // TT
    NCH = ntok // 128
    seq = mode == 'seq'
    P = Prog()
    es = ExitStack()

    def din(name, shape, dt):
        return nc.dram_tensor(name, shape, dt, kind="ExternalInput").ap()

    xT = din("xT", [DM, NTOK], F32)
    posT = din("posT", [128, NCH], I32)
    cstD = din("cst", [128, NCST], F32)
    spD = din("sp", [L, 128, NSP], F32)
    pwD = din("pw", [L, 128, 512], F32)
    w_in = din("w_in", [L, DM, NIN], F32)
    if mode != 'pre':
        w_pp = din("w_pp", [L, 512, DM], F32)
        w_cp = din("w_cp", [L, 512, DM], F32)
        w_rp = din("w_rp", [L, 1024, DM], F32)
        w_out = din("w_out", [L, DM, DM], F32)
        w_fi = din("w_fi", [L, DM, 2 * FH], F32)
        w_fo = din("w_fo", [L, FH, DM], F32)
    if mode == 'full':
        xinD = din("xin", [8 * 128, XW], F32)
    if mode == 'pre':
        xoutD = nc.dram_tensor("xout", [128, XW], F32, kind="ExternalOutput").ap()
    else:
        yT = nc.dram_tensor("yT", [DM, NTOK], F32, kind="ExternalOutput").ap()
    if DEBUG and mode == 'full':
        dbgA = nc.dram_tensor("dbgA", [128, 16 * 512], BF, kind="ExternalOutput").ap()
        dbgBR = nc.dram_tensor("dbgBR", [128, 16 * 512], BF, kind="ExternalOutput").ap()
        dbgM = nc.dram_tensor("dbgM", [128, 16 * 512], BF, kind="ExternalOutput").ap()
        dbgY = nc.dram_tensor("dbgY", [128, 16 * 512], F32, kind="ExternalOutput").ap()
        dbgQ = nc.dram_tensor("dbgQ", [128, 4 * 1024], BF, kind="ExternalOutput").ap()
        dbgK = nc.dram_tensor("dbgK", [128, 4 * 1024], BF, kind="ExternalOutput").ap()
        dbgCC = nc.dram_tensor("dbgCC", [128, 2048], F32, kind="ExternalOutput").ap()
        dbgSS = nc.dram_tensor("dbgSS", [128, 2048], F32, kind="ExternalOutput").ap()
        dbgHB = nc.dram_tensor("dbgHB", [128, 4 * 544], F32, kind="ExternalOutput").ap()
        dbgACC = nc.dram_tensor("dbgACC", [128, 4 * 512], F32, kind="ExternalOutput").ap()
        dbgSG = nc.dram_tensor("dbgSG", [128, 4 * 512], F32, kind="ExternalOutput").ap()
    ktmD = nc.dram_tensor("ktm_s", [NTOK, 1024], BF).ap()
    vtmD = nc.dram_tensor("vtm_s", [NTOK, 1024], BF).ap()
    if seq:
        xbuf = [nc.dram_tensor("xb%d" % i, [DM, NTOK], F32).ap() for i in range(2)]
        ccD = nc.dram_tensor("cc_s", [128, NCH * 128], F32).ap()
        ssD = nc.dram_tensor("ss_s", [128, NCH * 128], F32).ap()
    if mode == 'fused':
        xbuf = [nc.dram_tensor("xb%d" % i, [DM, NTOK], F32).ap() for i in range(2)]
        xchD = nc.dram_tensor("xch_s", [128, XW], F32).ap()
        xgD = nc.dram_tensor("xg_s", [8 * 128, XW], F32).ap()

    def sb(name, shape, dt):
        return es.enter_context(nc.sbuf_tensor(name, shape, dt))

    WR = [sb("wr%d" % i, [128, 4096], BF) for i in range(NSLOT)]
    ARB = 112 * 1024
    AR = sb("arena", [128, ARB // 2], BF)
    CST = sb("cstt", [128, NCST], F32)
    SPR = sb("spr", [128, NSP], F32)
    PWB = sb("pwb", [128, 512], BF)
    CC = sb("cc", [128, 4 if seq else 16, 128], F32)
    SS = sb("ss", [128, 4 if seq else 16, 128], F32)
    IDN = sb("idn", [128, 128], BF)
    ON_D = sb("ond", [128, 128], BF)
    ON_5 = sb("on5", [128, 128], BF)
    ON_1 = sb("on1", [128, 128], BF)
    ST = sb("stt", [128, 1024], F32)
    UPT = sb("upt", [128, 4, 16], F32)
    HBT = sb("hbt", [128, 4, 32], F32)
    POSI = sb("posi", [128, NCH], I32)
    RS = [sb("rs%d" % i, [128, 512], F32) for i in range(2)]
    FT = [sb("ft%d" % i, [128, 512], F32) for i in range(4)]
    BT = [sb("bt%d" % i, [128, 512], BF) for i in range(3)]
    SMTB = [sb("smt%d" % i, [128, 512], BF) for i in range(2)]
    DGB = [sb("dg%d" % i, [128, 128], BF) for i in range(4)]
    CEN = sb("cen", [128, 128], BF)
    PSB = [es.enter_context(nc.psum_tensor("ps%d" % i, [128, 512], F32)) for i in range(8)]

    def sv(t, name):
        return V(t[:], [(name,)])

    vCST = sv(CST, "cst"); vSPR = sv(SPR, "spr"); vPWB = sv(PWB, "pwb")
    vCC = sv(CC, "cc"); vSS = sv(SS, "ss"); vIDN = sv(IDN, "idn")
    vOND = sv(ON_D, "ond"); vON5 = sv(ON_5, "on5"); vON1 = sv(ON_1, "on1")
    vST = [V(ST[:, i * 512:(i + 1) * 512], [("st", i)]) for i in range(2)]
    vUPT = sv(UPT, "upt"); vHBT = sv(HBT, "hbt"); vPOSI = sv(POSI, "posi")
    vRS = [sv(RS[i], "rs%d" % i) for i in range(2)]
    vFT = [sv(FT[i], "ft%d" % i) for i in range(4)]
    vBT = [sv(BT[i], "bt%d" % i) for i in range(3)]
    vSMT = [sv(SMTB[i], "smt%d" % i) for i in range(2)]
    vDG = [sv(DGB[i], "dg%d" % i) for i in range(4)]
    vCEN = sv(CEN, "cen")
    PS = [V(PSB[i][:], [("ps", i)]) for i in range(8)]
    rot = {}

    def nxt(name, lst):
        i = rot.get(name, 0)
        rot[name] = i + 1
        return lst[i % len(lst)]

    def bank():
        return nxt("bank", PS[0:7])
    PSTAT = PS[7]

    def arv(off, shape, dt):
        n = 1
        for s in shape[1:]:
            n *= s
        nb = n * (2 if dt == BF else 4)
        ap = AR[:, off // 2: off // 2 + nb // 2]
        if dt != BF:
            ap = ap.bitcast(dt)
        if len(shape) == 3:
            ap = ap.rearrange("p (a b) -> p a b", a=shape[1])
        keys = [("AR", g) for g in range(off // 1024, (off + nb + 1023) // 1024)]
        return V(ap, keys)

    KB = 1024
    A = [arv(kc * KB, [128, 512], BF) for kc in range(16)]
    A_all = arv(0, [128, 16, 512], BF)
    BR = [arv(16 * KB + i * KB, [128, 512], BF) for i in range(16)]
    M = [arv(32 * KB + i * KB, [128, 512], BF) for i in range(16)]
    HID = [arv(32 * KB + j * KB, [128, 512], BF) for j in range(44)]
    Y = [arv(80 * KB + i * 2 * KB, [128, 512], F32) for i in range(16)]
    Y_all = arv(80 * KB, [128, 16, 512], F32)
    Y2 = [arv(i * 2 * KB, [128, 512], F32) for i in range(16)]
    QTM = [arv(48 * KB + n * 2 * KB, [128, 1024], BF) for n in range(4)]
    KTM = [arv(56 * KB + n * 2 * KB, [128, 1024], BF) for n in range(4)]
    VTM = [arv(64 * KB + n * 2 * KB, [128, 1024], BF) for n in range(4)]
    KTM_all = arv(56 * KB, [128, 4, 1024], BF)
    VTM_all = arv(64 * KB, [128, 4, 1024], BF)
    QT = [arv(72 * KB + h * KB, [128, 512], BF) for h in range(8)]
    KT = [arv(80 * KB + h * KB, [128, 512], BF) for h in range(8)]
    GS = [arv(88 * KB + h * KB, [128, 512], BF) for h in range(8)]
    SBS = [[arv(96 * KB + n * 2 * KB + hf * KB, [128, 512], BF) for hf in range(2)] for n in range(5)]
    UP = arv(48 * KB, [128, 4, 528], F32)
    SA = arv(48 * KB + 8448, [128, 4, 528], F32)
    SBF = arv(48 * KB + 2 * 8448, [128, 4, 528], F32)
    PP = [arv(48 * KB + 3 * 8448 + g * KB, [128, 512], BF) for g in range(4)]
    HB = arv(48 * KB, [128, 4, 544], BF)
    SG = arv(48 * KB + 8704, [128, 4, 512], F32)
    ACC = arv(48 * KB + 8704 + 8192, [128, 4, 512], F32)
    XT_ = [arv(80 * KB + i * 5 * KB, [128, XW], F32) for i in range(2)]
    vXC = [arv(i * 2 * KB, [128, 512], F32) for i in range(2)]

    class WStream:
        def __init__(self):
            self.reqs = []
            self.pos = 0
            self.issued = 0

        @staticmethod
        def _views(slot, parts):
            vs = []
            off = 0
            for (wt, l, r0, nk, c0, cols) in parts:
                vs.append(WR[slot][:, off:off + nk * cols].rearrange("p (k c) -> p k c", k=nk))
                off += nk * cols
            return vs

        @staticmethod
        def _keys(slot, pi, np_):
            if np_ == 1:
                return [("wr", slot, 0), ("wr", slot, 1), ("wr", slot, 2)]
            return [("wr", slot, pi)]

        def _issue(self, j):
            parts = self.reqs[j]
            slot = j % NSLOT
            vs = self._views(slot, parts)
            np_ = len(parts)
            for pi, (view, (wt, l, r0, nk, c0, cols)) in enumerate(zip(vs, parts)):
                src = wt[l, r0:r0 + nk * 128, c0:c0 + cols].rearrange("(k p) c -> p k c", p=128)
                P.add('pool', 'dma_start', dict(out=view, in_=src), R=[], W=self._keys(slot, pi, np_),
                      group="w%d_%d" % (slot, pi))

        def getm(self, parts, back=1):
            if P.dry:
                self.reqs.append(tuple(parts))
                return [V(v, [("wr", 0, 0)]) for v in self._views(0, parts)]
            i = self.pos
            self.pos += 1
            while self.issued <= min(len(self.reqs) - 1, i - back + NSLOT - 1):
                self._issue(self.issued)
                self.issued += 1
            slot = i % NSLOT
            return [V(v, self._keys(slot, pi, len(parts))) for pi, v in enumerate(self._views(slot, parts))]

        def get(self, wt, l, r0, nk, c0, cols, back=1):
            return self.getm([(wt, l, r0, nk, c0, cols)], back)[0]
    WS = WStream()

    def mm(out, lhsT, rhs, start, stop, R, W):
        P.add('pe', 'matmul', dict(out=out, lhsT=lhsT, rhs=rhs, start=start, stop=stop), R=R, W=W)

    def dve(meth, R, W, **kw):
        P.add('dve', meth, kw, R=R, W=W)

    def act(out, in_, func, R, W, **kw):
        P.add('act', 'activation', dict(out=out, in_=in_, func=func, **kw), R=R, W=W)

    def dma(q, out, in_, R, W, group):
        return P.add(q, 'dma_start', dict(out=out, in_=in_), R=R, W=W, group=group)

    def b3(ap2, n):
        return ap2.rearrange("p (a b) -> p a b", a=n)

    def rstd_of(src, eps_col):
        r = nxt("rs", vRS)
        act(r.ap, src.ap, AF.Sqrt, [src, vCST], [r], bias=CST[:, eps_col:eps_col + 1], scale=1.0)
        dve('reciprocal', [r], [r], out=r.ap, in_=r.ap)
        return r

    def norm(X, gcol0, Aout):
        pb = bank()
        for kc in range(16):
            sq = nxt("bt", vBT)
            act(sq.ap, X[kc].ap, AF.Square, [X[kc]], [sq])
            mm(pb.ap, ON_D[:], sq.ap, kc == 0, kc == 15, [sq, vOND], [pb])
        r = rstd_of(pb, 1562)
        for kc in range(16):
            dve('scalar_tensor_tensor', [X[kc], r, vSPR], [Aout[kc]], out=Aout[kc].ap, in0=X[kc].ap,
                scalar=SPR[:, gcol0 + kc:gcol0 + kc + 1], in1=r.ap, op0=ALU.mult, op1=ALU.mult)

    def rotary(src, gn, dst_ap, dstv):
        s3 = b3(src.ap, 4)
        ta = nxt("ft", vFT)
        tb = nxt("ft", vFT)
        dve('tensor_tensor', [src, vCC], [ta], out=b3(ta.ap, 4), in0=s3,
            in1=CC[:, gn, :].unsqueeze(1).to_broadcast([128, 4, 128]), op=ALU.mult)
        dve('tensor_tensor', [src, vSS], [tb], out=b3(tb.ap, 4)[:, :, 0:64], in0=s3[:, :, 64:128],
            in1=SS[:, gn, 0:64].unsqueeze(1).to_broadcast([128, 4, 64]), op=ALU.mult)
        dve('tensor_tensor', [src, vSS, tb], [tb], out=b3(tb.ap, 4)[:, :, 64:128], in0=s3[:, :, 0:64],
            in1=SS[:, gn, 64:128].unsqueeze(1).to_broadcast([128, 4, 64]), op=ALU.mult)
        dve('tensor_tensor', [ta, tb], [dstv], out=dst_ap, in0=ta.ap, in1=tb.ap, op=ALU.add)

    def tok_proj(l, coff, cg, n, Aall):
        pb = bank()
        w0, w1 = tok_proj.w
        for kc in range(16):
            w = w0 if kc < 8 else w1
            mm(pb.ap, A[kc].ap[:, n * 128:(n + 1) * 128], w.ap[:, kc % 8, :], kc == 0, kc == 15,
               [A[kc], w], [pb])
        return pb

    def load_x_tile(xsrc, xname, tt):
        t0 = tt * TT
        for kc in range(16):
            dma('sp', Y[kc].ap, xsrc[kc * 128:(kc + 1) * 128, t0:t0 + TT], [(xname, tt, kc)], [Y[kc]], "ldx%d" % kc)

    def kv_update(n, write_sb):
        for hf in range(2):
            pb = bank()
            for hq in range(4):
                h = hf * 4 + hq
                mm(pb.ap[:, hq * 128:(hq + 1) * 128], KTM[n].ap[:, h * 128:(h + 1) * 128],
                   VTM[n].ap[:, h * 128:(h + 1) * 128], True, True, [KTM[n], VTM[n]], [pb])
            s = vST[hf]
            dve('tensor_tensor', [s, pb], [s], out=s.ap, in0=s.ap, in1=pb.ap, op=ALU.add)
            dve('tensor_tensor', [s, vCST], [s], out=s.ap, in0=s.ap,
                in1=CST[:, 144 + hf * 512:144 + (hf + 1) * 512], op=ALU.mult)
            if write_sb:
                act(SBS[n + 1][hf].ap, s.ap, AF.Copy, [s], [SBS[n + 1][hf]])

    def pre_kv_tile(l, tt, cbase):
        for (coff, isk) in ((O_K, True), (O_V, False)):
            for cg in range(2):
                w0 = WS.get(w_in, l, 0, 8, coff + cg * 512, 512)
                w1 = WS.get(w_in, l, 1024, 8, coff + cg * 512, 512)
                tok_proj.w = (w0, w1)
                for n in range(4):
                    pb = tok_proj(l, coff, cg, n, None)
                    if isk:
                        kd = nxt("ft", vFT)
                        dve('tensor_tensor', [pb, vCST], [kd], out=b3(kd.ap, 4), in0=b3(pb.ap, 4),
                            in1=CST[:, 136 + cg * 4:136 + cg * 4 + 4].unsqueeze(2).to_broadcast([128, 4, 128]),
                            op=ALU.mult)
                        rotary(kd, cbase + n, KTM[n].ap[:, cg * 512:(cg + 1) * 512], KTM[n])
                    else:
                        act(VTM[n].ap[:, cg * 512:(cg + 1) * 512], pb.ap, AF.Copy, [pb], [VTM[n]])

    def u_proj_conv_pool(l, tt, tails_only):
        dve('tensor_copy', [vUPT], [UP], out=UP.ap[:, :, 1:16], in_=UPT[:, :, 1:16])
        for half in range(2):
            w = WS.get(w_in, l, 0, 16, O_POOL + half * 256, 256)
            for gi in range(2):
                g = half * 2 + gi
                pb = bank()
                for kc in range(16):
                    mm(pb.ap, w.ap[:, kc, gi * 128:(gi + 1) * 128], A[kc].ap, kc == 0, kc == 15, [A[kc], w], [pb])
                act(UP.ap[:, g, 16:528], pb.ap, AF.Copy, [pb], [UP])
        dve('tensor_copy', [UP], [vUPT], out=UPT[:, :, 1:16], in_=UP.ap[:, :, 513:528])
        if not tails_only:
            wins = (2, 4, 8, 16)
            src = UP
            bufs = [SA, SBF]
            for g in range(4):
                sh = wins[g] // 2
                dstb = bufs[g % 2]
                dve('tensor_tensor', [src], [dstb], out=dstb.ap[:, g:4, 2 * sh:528], in0=src.ap[:, g:4, 2 * sh:528],
                    in1=src.ap[:, g:4, sh:528 - sh], op=ALU.add)
                dve('scalar_tensor_tensor', [dstb, UP], [PP[g]], out=PP[g].ap, in0=dstb.ap[:, g, 16:528],
                    scalar=1.0 / wins[g], in1=UP.ap[:, g, 16:528], op0=ALU.mult, op1=ALU.subtract)
                if tt == 0:
                    t1 = nxt("ft", vFT)
                    dve('tensor_tensor', [dstb, vCST], [t1], out=t1.ap[:, 0:16], in0=dstb.ap[:, g, 16:32],
                        in1=CST[:, 1240 + g * 16:1240 + (g + 1) * 16], op=ALU.mult)
                    dve('tensor_tensor', [t1, UP, PP[g]], [PP[g]], out=PP[g].ap[:, 0:16], in0=t1.ap[:, 0:16],
                        in1=UP.ap[:, g, 16:32], op=ALU.subtract)
                src = dstb
            for g in range(4):
                pb = bank()
                mm(pb.ap, PWB[:, g * 128:(g + 1) * 128], PP[g].ap, True, True, [PP[g], vPWB], [pb])
                dve('tensor_scalar', [pb, vSPR], [BR[g]], out=BR[g].ap, in0=pb.ap, scalar1=SPR[:, 64 + g:65 + g],
                    scalar2=None, op0=ALU.mult)
        for half in range(2):
            w = WS.get(w_in, l, 0, 16, O_CG + half * 256, 256)
            for gi in range(2):
                j = half * 2 + gi
                pb = bank()
                for kc in range(16):
                    mm(pb.ap, w.ap[:, kc, gi * 128:(gi + 1) * 128], A[kc].ap, kc == 0, kc == 15, [A[kc], w], [pb])
                act(SG.ap[:, j, :], pb.ap, AF.Sigmoid, [pb], [SG])
        dve('tensor_copy', [vHBT], [HB], out=HB.ap[:, :, 2:32], in_=HBT[:, :, 2:32])
        for half in range(2):
            w = WS.get(w_in, l, 0, 16, O_CA + half * 256, 256)
            for gi in range(2):
                j = half * 2 + gi
                pb = bank()
                for kc in range(16):
                    mm(pb.ap, w.ap[:, kc, gi * 128:(gi + 1) * 128], A[kc].ap, kc == 0, kc == 15, [A[kc], w], [pb])
                dve('tensor_tensor', [pb, SG], [HB], out=HB.ap[:, j, 32:544], in0=pb.ap, in1=SG.ap[:, j, :], op=ALU.mult)
        dve('tensor_copy', [HB], [vHBT], out=HBT[:, :, 2:32], in_=HB.ap[:, :, 514:544])
        if tails_only:
            return
        for j in range(4):
            pc = bank()
            for t in range(31):
                dg = nxt("dg", vDG)
                dve('tensor_scalar', [vIDN, vSPR], [dg], out=dg.ap, in0=IDN[:], scalar1=SPR[:, 88 + j * 31 + t:89 + j * 31 + t],
                    scalar2=None, op0=ALU.mult)
                mm(pc.ap, dg.ap, HB.ap[:, j, 2 + t:514 + t], t == 0, t == 30, [dg, HB], [pc])
            act(ACC.ap[:, j, :], pc.ap, AF.Identity, [pc, vSPR], [ACC], bias=SPR[:, 68 + j:69 + j], scale=1.0)
        if DEBUG and mode == 'full' and tt == 0:
            dma('sp', dbgHB, HB.ap.rearrange("p a b -> p (a b)"), [HB], [("dbgHB",)], "dbgHB")
            dma('sp', dbgACC, ACC.ap.rearrange("p a b -> p (a b)"), [ACC], [("dbgACC",)], "dbgACC")
            dma('sp', dbgSG, SG.ap.rearrange("p a b -> p (a b)"), [SG], [("dbgSG",)], "dbgSG")
        pm = bank()
        pq = bank()
        for j in range(4):
            c16 = nxt("bt", vBT)
            act(c16.ap, ACC.ap[:, j, :], AF.Copy, [ACC], [c16])
            mm(pm.ap, ON_5[:], c16.ap, j == 0, j == 3, [c16, vON5], [pm])
            s16 = nxt("bt", vBT)
            act(s16.ap, ACC.ap[:, j, :], AF.Square, [ACC], [s16])
            mm(pq.ap, ON_5[:], s16.ap, j == 0, j == 3, [s16, vON5], [pq])
        ms = nxt("ft", vFT)
        act(ms.ap, pm.ap, AF.Copy, [pm], [ms])
        m2 = nxt("ft", vFT)
        dve('tensor_tensor', [ms], [m2], out=m2.ap, in0=ms.ap, in1=ms.ap, op=ALU.mult)
        dve('tensor_tensor', [pq, m2], [m2], out=m2.ap, in0=pq.ap, in1=m2.ap, op=ALU.subtract)
        r = rstd_of(m2, 1563)
        tpair = [nxt("ft", vFT), nxt("ft", vFT)]
        for j in range(4):
            t = tpair[j % 2]
            dve('tensor_tensor', [ACC, ms], [t], out=t.ap, in0=ACC.ap[:, j, :], in1=ms.ap, op=ALU.subtract)
            dve('tensor_tensor', [t, r], [t], out=t.ap, in0=t.ap, in1=r.ap, op=ALU.mult)
            act(BR[4 + j].ap, t.ap, AF.Silu, [t, vSPR], [BR[4 + j]], scale=SPR[:, 72 + j:73 + j],
                bias=SPR[:, 76 + j:77 + j])

    def post_norm_residual(Yo, gcol0, xget, tt, after=None):
        r = rstd_of(PSTAT, 1562)
        for dc in range(16):
            xv = xget(dc)
            t = nxt("ft", vFT)
            dve('scalar_tensor_tensor', [Yo[dc], r, vSPR], [t], out=t.ap, in0=Yo[dc].ap,
                scalar=SPR[:, gcol0 + dc:gcol0 + dc + 1], in1=r.ap, op0=ALU.mult, op1=ALU.mult)
            dve('tensor_tensor', [t, xv], [Y[dc]], out=Y[dc].ap, in0=t.ap, in1=xv.ap, op=ALU.add)
            if after is not None:
                after(dc)

    def evac_stats(pb, Yo, dc):
        act(Yo[dc].ap, pb.ap, AF.Copy, [pb], [Yo[dc]])
        sq = nxt("bt", vBT)
        act(sq.ap, pb.ap, AF.Square, [pb], [sq])
        mm(PSTAT.ap, ON_D[:], sq.ap, dc == 0, dc == 15, [sq, vOND], [PSTAT])

    def body():
        rot.clear()
        out_dmas = []
        try:
            body_main(out_dmas)
        except _Stop:
            pass
        body_fin(out_dmas)

    def body_main(out_dmas):
        dma('sp', CST[:], cstD[:, :], [], [vCST], "ldc")
        dma('sp', POSI[:], posT[:, :], [], [vPOSI], "ldp")
        dve('tensor_copy', [vCST], [vIDN], out=IDN[:], in_=CST[:, 1434:1562])
        dve('tensor_scalar', [vIDN], [vCEN], out=CEN[:], in0=IDN[:], scalar1=-1.0 / 128, scalar2=None, op0=ALU.add)
        dve('memset', [], [vOND], ap=ON_D[:], constant=1.0 / 2048)
        dve('memset', [], [vON5], ap=ON_5[:], constant=1.0 / 512)
        dve('memset', [], [vON1], ap=ON_1[:], constant=1.0 / 128)
        posf = vFT[0]
        dve('tensor_copy', [vPOSI], [posf], out=posf.ap[:, 0:NCH], in_=POSI[:])
        T1 = arv(0, [128, 16, 128], F32)
        T2 = arv(8 * KB, [128, 16, 128], F32)
        T3 = arv(16 * KB, [128, 16, 128], F32)
        TIv = arv(24 * KB, [128, 16, 128], F32)
        TI = V(TIv.ap.bitcast(I32), TIv.k)
        T4 = arv(32 * KB, [128, 16, 128], F32)
        T5 = arv(40 * KB, [128, 16, 128], F32)
        for piece in range(NCH // 16):
            if seq:
                cdst, sdst, cv, sv_ = T4.ap, T5.ap, T4, T5
            else:
                cdst, sdst, cv, sv_ = CC[:], SS[:], vCC, vSS
            dve('tensor_tensor', [posf, vCST], [T1], out=T1.ap,
                in0=posf.ap[:, piece * 16:(piece + 1) * 16].unsqueeze(2).to_broadcast([128, 16, 128]),
                in1=CST[:, 0:128].unsqueeze(1).to_broadcast([128, 16, 128]), op=ALU.mult)
            dve('tensor_scalar', [T1], [T2], out=T2.ap, in0=T1.ap, scalar1=1.0 / (2 * PI), scalar2=None, op0=ALU.mult)
            dve('tensor_copy', [T2], [TI], out=TI.ap, in_=T2.ap)
            dve('tensor_copy', [TI], [T2], out=T2.ap, in_=TI.ap)
            dve('scalar_tensor_tensor', [T2, T1], [T1], out=T1.ap, in0=T2.ap, scalar=-2 * PI, in1=T1.ap, op0=ALU.mult, op1=ALU.add)
            act(T2.ap, T1.ap, AF.Sin, [T1], [T2], scale=0.5)
            act(T3.ap, T1.ap, AF.Sin, [T1], [T3], scale=0.25)
            dve('tensor_tensor', [T2], [cv], out=cdst, in0=T2.ap, in1=T2.ap, op=ALU.mult)
            dve('tensor_scalar', [cv], [cv], out=cdst, in0=cdst, scalar1=-2.0, scalar2=1.0, op0=ALU.mult, op1=ALU.add)
            dve('tensor_tensor', [T3], [T3], out=T3.ap, in0=T3.ap, in1=T3.ap, op=ALU.mult)
            dve('tensor_scalar', [T3], [T3], out=T3.ap, in0=T3.ap, scalar1=-2.0, scalar2=1.0, op0=ALU.mult, op1=ALU.add)
            dve('scalar_tensor_tensor', [T2, T3], [sv_], out=sdst[:, :, 64:128], in0=T2.ap[:, :, 64:128], scalar=2.0,
                in1=T3.ap[:, :, 64:128], op0=ALU.mult, op1=ALU.mult)
            dve('scalar_tensor_tensor', [T2, T3, sv_], [sv_], out=sdst[:, :, 0:64], in0=T2.ap[:, :, 0:64], scalar=-2.0,
                in1=T3.ap[:, :, 0:64], op0=ALU.mult, op1=ALU.mult)
            if seq:
                dma('sp', ccD[:, piece * 2048:(piece + 1) * 2048], T4.ap.rearrange("p a b -> p (a b)"), [T4], [("ccD", piece)], "stcc")
                dma('sp', ssD[:, piece * 2048:(piece + 1) * 2048], T5.ap.rearrange("p a b -> p (a b)"), [T5], [("ssD", piece)], "stss")

        if DEBUG and mode == 'full':
            dma('sp', dbgCC, CC[:].rearrange("p a b -> p (a b)"), [vCC], [("dbgCC",)], "dbgCC")
            dma('sp', dbgSS, SS[:].rearrange("p a b -> p (a b)"), [vSS], [("dbgSS",)], "dbgSS")
        for l in range(L) if True else []:
            if mode == 'fused' or seq:
                xsrc = xT if l == 0 else xbuf[(l - 1) % 2]
                xdst = yT if l == L - 1 else xbuf[l % 2]
                xsn = "xT" if l == 0 else "xb%d" % ((l - 1) % 2)
                xdn = "yT" if l == L - 1 else "xb%d" % (l % 2)
            else:
                xsrc = xT
                xdst = None if mode == 'pre' else yT
                xsn, xdn = "xT", "yT"
            dma('sp', SPR[:], spD[l], [], [vSPR], "ldsp")
            dma('pool', PWB[:], pwD[l], [], [vPWB], "ldpw")
            dve('memset', [], [vST[0]], ap=ST[:, 0:512], constant=0.0)
            dve('memset', [], [vST[1]], ap=ST[:, 512:1024], constant=0.0)
            dve('memset', [], [vUPT], ap=UPT[:], constant=0.0)
            dve('memset', [], [vHBT], ap=HBT[:], constant=0.0)
            for tt in range(0 if seq else NTILE):
                t0 = tt * TT
                load_x_tile(xsrc, xsn, tt)
                norm(Y, 0, A)
                pre_kv_tile(l, tt, tt * 4)
                dma('sp', ktmD[t0:t0 + TT, :].rearrange("(n p) c -> p n c", p=128), KTM_all.ap, [KTM_all], [("ktmD", tt)], "stk")
                dma('sp', vtmD[t0:t0 + TT, :].rearrange("(n p) c -> p n c", p=128), VTM_all.ap, [VTM_all], [("vtmD", tt)], "stv")
                for n in range(4):
                    kv_update(n, False)
                if tt == NTILE - 1:
                    u_proj_conv_pool(l, tt, True)
            if mode == 'pre':
                xo = xoutD
            elif mode == 'fused':
                xo = xchD
            if mode in ('pre', 'fused'):
                i1 = dma('sp', xo[:, 0:512], ST[:, 0:512], [vST[0]], [("xo", 0)], "sx0")
                i2 = dma('sp', xo[:, 512:1024], ST[:, 512:1024], [vST[1]], [("xo", 1)], "sx1")
                i3 = dma('sp', xo[:, 1024:1152], HBT[:].rearrange("p a b -> p (a b)"), [vHBT], [("xo", 2)], "sx2")
                i4 = dma('sp', xo[:, 1152:1216], UPT[:].rearrange("p a b -> p (a b)"), [vUPT], [("xo", 3)], "sx3")
                out_dmas += [i1, i2, i3, i4]
            if mode == 'pre':
                continue
            if mode == 'fused':
                P.add('pool', 'collective_compute',
                      dict(kind="AllGather", op=ALU.bypass, replica_groups=[list(range(8))],
                           ins=[xchD[:, :]], outs=[xgD[:, :]]),
                      R=[("xo", 0), ("xo", 1), ("xo", 2), ("xo", 3)], W=[("xg",)], group="cc")
                xin_src = xgD
            elif not seq:
                xin_src = xinD
            if not seq:
                dve('memset', [vST[0]], [vST[0]], ap=ST[:, 0:512], constant=0.0)
                dve('memset', [vST[1]], [vST[1]], ap=ST[:, 512:1024], constant=0.0)
                dve('memset', [vUPT], [vUPT], ap=UPT[:], constant=0.0)
                dve('memset', [vHBT], [vHBT], ap=HBT[:], constant=0.0)
            for r in range(0 if seq else 8):
                xt = XT_[r % 2]
                dma('sp', xt.ap, xin_src[r * 128:(r + 1) * 128, :], [("xg",)], [xt], "ldxg%d" % (r % 2))
                for h in range(8):
                    s = vST[h // 4]
                    dve('scalar_tensor_tensor', [xt, s, vCST], [s], out=ST[:, h * 128:(h + 1) * 128],
                        in0=xt.ap[:, h * 128:(h + 1) * 128], scalar=CST[:, 1168 + r * 8 + h:1169 + r * 8 + h],
                        in1=ST[:, h * 128:(h + 1) * 128], op0=ALU.mult, op1=ALU.add)
                hb2 = HBT[:].rearrange("p a b -> p (a b)")
                dve('scalar_tensor_tensor', [xt, vHBT, vCST], [vHBT], out=hb2, in0=xt.ap[:, 1024:1152],
                    scalar=CST[:, 1232 + r:1233 + r], in1=hb2, op0=ALU.mult, op1=ALU.add)
                up2 = UPT[:].rearrange("p a b -> p (a b)")
                dve('scalar_tensor_tensor', [xt, vUPT, vCST], [vUPT], out=up2, in0=xt.ap[:, 1152:1216],
                    scalar=CST[:, 1232 + r:1233 + r], in1=up2, op0=ALU.mult, op1=ALU.add)
            for tt in range(NTILE):
                t0 = tt * TT
                if seq:
                    dma('sp', CC[:], ccD[:, tt * 512:(tt + 1) * 512].rearrange("p (a b) -> p a b", a=4), [("ccD", tt // 4)], [vCC], "ldcc")
                    dma('sp', SS[:], ssD[:, tt * 512:(tt + 1) * 512].rearrange("p (a b) -> p a b", a=4), [("ssD", tt // 4)], [vSS], "ldss")
                load_x_tile(xsrc, xsn, tt)
                norm(Y, 0, A)
                if DEBUG and tt == 0:
                    dma('sp', dbgA, AR[:, 0:8192], [A_all], [("dbgA",)], "dbgA")
                u_proj_conv_pool(l, tt, False)
                if STOP == 'proj' and tt == 0:
                    raise _Stop()
                for cg in range(2):
                    w0 = WS.get(w_in, l, 0, 8, O_Q + cg * 512, 512)
                    w1 = WS.get(w_in, l, 1024, 8, O_Q + cg * 512, 512)
                    tok_proj.w = (w0, w1)
                    for n in range(4):
                        pb = tok_proj(l, O_Q, cg, n, None)
                        qd = nxt("ft", vFT)
                        dve('tensor_tensor', [pb, vCST], [qd], out=b3(qd.ap, 4), in0=b3(pb.ap, 4),
                            in1=CST[:, 128 + cg * 4:128 + cg * 4 + 4].unsqueeze(2).to_broadcast([128, 4, 128]),
                            op=ALU.mult)
                        rotary(qd, (0 if seq else tt * 4) + n, QTM[n].ap[:, cg * 512:(cg + 1) * 512], QTM[n])
                if seq:
                    pre_kv_tile(l, tt, 0)
                else:
                    dma('sp', KTM_all.ap, ktmD[t0:t0 + TT, :].rearrange("(n p) c -> p n c", p=128), [("ktmD", tt)], [KTM_all], "ldk")
                    dma('sp', VTM_all.ap, vtmD[t0:t0 + TT, :].rearrange("(n p) c -> p n c", p=128), [("vtmD", tt)], [VTM_all], "ldv")
                if DEBUG and tt == 0:
                    dma('sp', dbgQ, AR[:, 24 * KB:28 * KB], [QTM], [("dbgQ",)], "dbgQ")
                    dma('sp', dbgK, AR[:, 28 * KB:32 * KB], [KTM], [("dbgK",)], "dbgK")
                for (src, dst) in ((QTM, QT), (KTM, KT)):
                    for h in range(8):
                        pb = bank()
                        pbb = pb.ap.bitcast(BF)
                        for n in range(4):
                            P.add('pe', 'transpose', dict(out=pbb[:, n * 128:(n + 1) * 128],
                                                          in_=src[n].ap[:, h * 128:(h + 1) * 128], identity=IDN[:]),
                                  R=[src[n], vIDN], W=[pb])
                        act(dst[h].ap, pbb[:, 0:512], AF.Copy, [pb], [dst[h]])
                if STOP == 'q' and tt == 0:
                    raise _Stop()
                def stage_G(i4):
                    w = WS.get(w_in, l, 0, 16, O_GR + i4 * 256, 256)
                    for gi in range(2):
                        h = i4 * 2 + gi
                        pb = bank()
                        for kc in range(16):
                            mm(pb.ap, w.ap[:, kc, gi * 128:(gi + 1) * 128], A[kc].ap, kc == 0, kc == 15, [A[kc], w], [pb])
                        act(GS[h].ap, pb.ap, AF.Silu, [pb], [GS[h]])
                for i4 in range(4):
                    stage_G(i4)
                for hf in range(2):
                    act(SBS[0][hf].ap, vST[hf].ap, AF.Copy, [vST[hf]], [SBS[0][hf]])
                for n in range(4):
                    kv_update(n, True)
                OFb = [arv(32 * KB + i * 2 * KB, [128, 512], F32) for i in range(2)]
                MSb = [arv(36 * KB + i * 2 * KB, [128, 512], F32) for i in range(2)]
                M2b = [arv(40 * KB + i * 2 * KB, [128, 512], F32) for i in range(2)]
                OBb = [arv(44 * KB + i * KB, [128, 512], BF) for i in range(2)]
                OQb = [arv(46 * KB + i * KB, [128, 512], BF) for i in range(2)]
                smts = {}
                pOs = {}

                def stage_S(h):
                    pS = bank()
                    for n in range(4):
                        mm(pS.ap[:, n * 128:(n + 1) * 128], KT[h].ap[:, n * 128:(n + 1) * 128],
                           QT[h].ap[:, n * 128:(n + 1) * 128], True, True, [KT[h], QT[h]], [pS])
                    smt = nxt("smt", vSMT)
                    dve('tensor_tensor', [pS, vCST], [smt], out=b3(smt.ap, 4), in0=b3(pS.ap, 4),
                        in1=CST[:, 1306:1434].unsqueeze(1).to_broadcast([128, 4, 128]), op=ALU.mult)
                    smts[h] = smt

                def stage_O(h):
                    hf, hq = h // 4, h % 4
                    smt = smts[h]
                    pO = bank()
                    for n in range(4):
                        mm(pO.ap[:, n * 128:(n + 1) * 128], VTM[n].ap[:, h * 128:(h + 1) * 128],
                           smt.ap[:, n * 128:(n + 1) * 128], True, False, [VTM[n], smt], [pO])
                        mm(pO.ap[:, n * 128:(n + 1) * 128], SBS[n][hf].ap[:, hq * 128:(hq + 1) * 128],
                           QT[h].ap[:, n * 128:(n + 1) * 128], False, True, [SBS[n][hf], QT[h]], [pO])
                    ob = OBb[h % 2]
                    act(ob.ap, pO.ap, AF.Copy, [pO], [ob])

                def stage_T(h):
                    of, ob, osq = OFb[h % 2], OBb[h % 2], OQb[h % 2]
                    pC = bank()
                    mm(pC.ap, CEN[:], ob.ap, True, True, [ob, vCEN], [pC])
                    act(osq.ap, pC.ap, AF.Square, [pC], [osq])
                    pq = bank()
                    mm(pq.ap, ON_1[:], osq.ap, True, True, [osq, vON1], [pq])
                    r = rstd_of(pq, 1563)
                    dve('tensor_tensor', [pC, r], [of], out=of.ap, in0=pC.ap, in1=r.ap, op=ALU.mult)
                    dve('scalar_tensor_tensor', [of, GS[h], vSPR], [BR[8 + h]], out=BR[8 + h].ap, in0=of.ap,
                        scalar=SPR[:, 80 + h:81 + h], in1=GS[h].ap, op0=ALU.mult, op1=ALU.mult)
                for step in range(10):
                    if step < 8:
                        stage_S(step)
                    if 0 <= step - 1 < 8:
                        stage_O(step - 1)
                    if 0 <= step - 2 < 8:
                        stage_T(step - 2)
                if DEBUG and tt == 0:
                    dma('sp', dbgBR, AR[:, 8192:16384], [BR], [("dbgBR",)], "dbgBR")
                if STOP == 'ret' and tt == 0:
                    raise _Stop()
                accs = [vFT[0], vFT[1]]
                sgb = [vFT[2], vFT[3]]
                for grp in range(8):
                    wbr = WS.getm([(w_pp, l, 0, 4, grp * 256, 256), (w_cp, l, 0, 4, grp * 256, 256),
                                   (w_rp, l, 0, 8, grp * 256, 256)])
                    for b in range(3):
                        wg = WS.get(w_in, l, 0, 16, O_GATE + b * 2048 + grp * 256, 256, back=b + 1)
                        nkb, off = ((4, 0), (4, 4), (8, 8))[b]
                        for dci in range(2):
                            dc = grp * 2 + dci
                            pg = bank()
                            for kc in range(16):
                                mm(pg.ap, wg.ap[:, kc, dci * 128:(dci + 1) * 128], A[kc].ap, kc == 0, kc == 15,
                                   [A[kc], wg], [pg])
                            sg = sgb[dci]
                            act(sg.ap, pg.ap, AF.Sigmoid, [pg], [sg])
                            py = bank()
                            for kc in range(nkb):
                                mm(py.ap, wbr[b].ap[:, kc, dci * 128:(dci + 1) * 128], BR[off + kc].ap, kc == 0,
                                   kc == nkb - 1, [BR[off + kc], wbr[b]], [py])
                            if b == 0:
                                dve('tensor_tensor', [sg, py], [accs[dci]], out=accs[dci].ap, in0=sg.ap, in1=py.ap, op=ALU.mult)
                            else:
                                dve('tensor_tensor', [sg, py], [sg], out=sg.ap, in0=sg.ap, in1=py.ap, op=ALU.mult)
                                if b == 1:
                                    dve('tensor_tensor', [accs[dci], sg], [accs[dci]], out=accs[dci].ap, in0=accs[dci].ap,
                                        in1=sg.ap, op=ALU.add)
                                else:
                                    dve('tensor_tensor', [accs[dci], sg], [M[dc]], out=M[dc].ap, in0=accs[dci].ap,
                                        in1=sg.ap, op=ALU.add)
                if DEBUG and tt == 0:
                    dma('sp', dbgM, AR[:, 16384:24576], [M], [("dbgM",)], "dbgM")
                if STOP == 'merge' and tt == 0:
                    raise _Stop()
                for cgi in range(8):
                    w = WS.get(w_out, l, 0, 16, cgi * 256, 256)
                    for dci in range(2):
                        dc = cgi * 2 + dci
                        pb = bank()
                        for kc in range(16):
                            mm(pb.ap, w.ap[:, kc, dci * 128:(dci + 1) * 128], M[kc].ap, kc == 0, kc == 15, [M[kc], w], [pb])
                        evac_stats(pb, Y, dc)

                def xget(dc):
                    xc = nxt("xc", vXC)
                    dma('sp', xc.ap, xsrc[dc * 128:(dc + 1) * 128, t0:t0 + TT], [(xsn, tt, dc)], [xc], "ldxc%d" % (rot["xc"] % 2))
                    return xc
                post_norm_residual(Y, 16, xget, tt)
                if DEBUG and tt == 0:
                    dma('sp', dbgY, Y_all.ap.rearrange("p a b -> p (a b)"), [Y_all], [("dbgY",)], "dbgY")
                if STOP == 'wout' and tt == 0:
                    raise _Stop()
                norm(Y, 32, A)
                for jg in range(22):
                    wa = WS.get(w_fi, l, 0, 16, jg * 256, 256)
                    wb = WS.get(w_fi, l, 0, 16, FH + jg * 256, 256)
                    for ji in range(2):
                        j = jg * 2 + ji
                        pa = bank()
                        for kc in range(16):
                            mm(pa.ap, wa.ap[:, kc, ji * 128:(ji + 1) * 128], A[kc].ap, kc == 0, kc == 15, [A[kc], wa], [pa])
                        sa = nxt("ft", vFT)
                        act(sa.ap, pa.ap, AF.Silu, [pa], [sa])
                        pbb = bank()
                        for kc in range(16):
                            mm(pbb.ap, wb.ap[:, kc, ji * 128:(ji + 1) * 128], A[kc].ap, kc == 0, kc == 15, [A[kc], wb], [pbb])
                        dve('tensor_tensor', [sa, pbb], [HID[j]], out=HID[j].ap, in0=sa.ap, in1=pbb.ap, op=ALU.mult)
                for cgi in range(8):
                    pbs = [bank(), bank()]
                    for rg in range(3):
                        nk = 16 if rg < 2 else 12
                        w = WS.get(w_fo, l, rg * 2048, nk, cgi * 256, 256)
                        for dci in range(2):
                            for kk in range(nk):
                                mm(pbs[dci].ap, w.ap[:, kk, dci * 128:(dci + 1) * 128], HID[rg * 16 + kk].ap,
                                   rg == 0 and kk == 0, rg == 2 and kk == nk - 1, [HID[rg * 16 + kk], w], [pbs[dci]])
                    for dci in range(2):
                        evac_stats(pbs[dci], Y2, cgi * 2 + dci)
                def store_chunk(dc):
                    i = dma('sp', xdst[dc * 128:(dc + 1) * 128, t0:t0 + TT], Y[dc].ap, [Y[dc]], [(xdn, tt, dc)], "sty%d" % dc)
                    if l == L - 1:
                        out_dmas.append(i)
                post_norm_residual(Y2, 48, lambda dc: Y[dc], tt, after=store_chunk)
            if mode == 'fused' and l < L - 1:
                pass
    def body_fin(out_dmas):
        if not P.dry:
            lastd = {}
            for i, op in enumerate(P.ops):
                if op['dma']:
                    lastd[op['stream']] = i
            out_dmas = list(out_dmas) + list(lastd.values())
            P.ops.append(dict(eng='sp', meth=None, kw=None, deps=set(out_dmas), stream='sp', seq=P.seqc.get('sp', 0) + 1,
                              dma=False, ms=False))
            P.seqc['sp'] = P.seqc.get('sp', 0) + 1

    P.dry = True
    body()
    P.dry = False
    body()
    streams = P.finalize()
    sems = {}
    for s in streams:
        sems[s] = es.enter_context(nc.semaphore("s_" + s))
    with nc.Block() as block:
        P.emit(nc, block, sems)
    es.close()
    return nc, len(P.ops)


_GAMMA = 1.0 - np.exp2(-5.0 - np.arange(8, dtype=np.float64))


def _consts(core):
    s = core % 4
    b = core // 4
    c = np.zeros((128, NCST), np.float32)
    half = 64
    inv = (np.float32(10000.0) ** (-np.arange(half, dtype=np.float32) / np.float32(half))).astype(np.float32)
    c[:, 0:64] = inv[None, :]
    c[:, 64:128] = inv[None, :]
    p = np.arange(128, dtype=np.float64)
    lg = np.log(_GAMMA)
    c[:, 128:136] = np.exp((p[:, None] + 1.0) * lg[None, :])
    c[:, 136:144] = np.exp(-(p[:, None] + 1.0) * lg[None, :]) * (128.0 ** -0.5)
    gC = np.exp(128.0 * lg)
    c[:, 144:1168] = np.repeat(gC, 128)[None, :]
    G = np.exp(2048.0 * lg)
    for r in range(8):
        rb, rs = r // 4, r % 4
        if rb == b and rs < s:
            c[:, 1168 + r * 8:1168 + r * 8 + 8] = (G ** (s - 1 - rs))[None, :]
        if rb == b and rs == s - 1:
            c[:, 1232 + r] = 1.0
    wins = (2, 4, 8, 16)
    for g in range(4):
        t = np.arange(16)
        if s == 0:
            c[:, 1240 + g * 16:1240 + (g + 1) * 16] = (1.0 / np.minimum(t + 1, wins[g]))[None, :]
        else:
            c[:, 1240 + g * 16:1240 + (g + 1) * 16] = 1.0 / wins[g]
    c[:, 1304] = -np.pi
    c[:, 1305] = np.pi
    e = np.arange(128)
    c[:, 1306:1434] = (e[:, None] <= e[None, :]).astype(np.float32)
    c[:, 1434:1562] = np.eye(128, dtype=np.float32)
    c[:, 1562] = 1e-6
    c[:, 1563] = 1e-5
    return c


def _fm(v, n):
    return np.ascontiguousarray(v.reshape(n, 128).T)


def _pack_sp(inp, l):
    sp = np.zeros((128, NSP), np.float32)
    sp[:, 0:16] = _fm(inp["g_mix_pre"][l], 16)
    sp[:, 16:32] = _fm(inp["g_mix_post"][l], 16)
    sp[:, 32:48] = _fm(inp["g_ffn_pre"][l], 16)
    sp[:, 48:64] = _fm(inp["g_ffn_post"][l], 16)
    sp[:, 64:68] = _fm(inp["pool_scale"][l], 4)
    sp[:, 68:72] = _fm(inp["conv_b"][l], 4)
    sp[:, 72:76] = _fm(inp["conv_ln_g"][l], 4)
    sp[:, 76:80] = _fm(inp["conv_ln_b"][l], 4)
    sp[:, 80:88] = _fm(inp["ret_gn_g"][l], 8)
    dw = inp["conv_dw"][l]
    sp[:, 88:212] = dw.T.reshape(4, 128, 31).transpose(1, 0, 2).reshape(128, 124)
    return sp


def _pack_pw(inp, l):
    pw = inp["pool_w"][l]
    return np.ascontiguousarray(pw.transpose(1, 0, 2).reshape(128, 512))


_CACHE = {}


def _get(L, mode, ntok=2048):
    k = (L, mode, ntok)
    if k not in _CACHE:
        _CACHE[k] = build(L, mode, ntok)[0]
    return _CACHE[k]


def _core_x(x, c):
    b, s = c // 4, c % 4
    return np.ascontiguousarray(x[b, s * NTOK:(s + 1) * NTOK, :].T)


def _core_pos(pos, c):
    b, s = c // 4, c % 4
    return np.ascontiguousarray(pos[b, s * NTOK:(s + 1) * NTOK].reshape(16, 128).T.astype(np.int32))


def _layer_maps(inp, layers, xTs):
    sp = np.stack([_pack_sp(inp, l) for l in layers])
    pw = np.stack([_pack_pw(inp, l) for l in layers])
    sl = layers if len(layers) > 1 else slice(layers[0], layers[0] + 1)
    ws = dict(w_in=inp["w_in"][sl], w_pp=inp["w_pool_proj"][sl], w_cp=inp["w_conv_proj"][sl],
              w_rp=inp["w_ret_proj"][sl], w_out=inp["w_out"][sl], w_fi=inp["w_ffn_in"][sl], w_fo=inp["w_ffn_out"][sl])
    maps = []
    for c in range(8):
        m = dict(xT=xTs[c], posT=_core_pos(inp["positions"], c), cst=_CSTS[c], sp=sp, pw=pw)
        m.update(ws)
        maps.append(m)
    return maps


_CSTS = None


def kernel_unfused(**inputs):
    inp = {k: np.asarray(v) for k, v in inputs.items()}
    global _CSTS
    _CSTS = [_consts(c) for c in range(8)]
    x = inp["x"].astype(np.float32, copy=False)
    xTs = [_core_x(x, c) for c in range(8)]
    for l in range(4):
        maps = _layer_maps(inp, [l], xTs)
        nc_pre = _get(1, 'pre')
        pre_keys = ("xT", "posT", "cst", "sp", "pw", "w_in")
        res = run_bass_kernel_spmd(nc_pre, [{k: m[k] for k in pre_keys} for m in maps], core_ids=list(range(8)))
        xin = np.concatenate([res.results[c]["xout"] for c in range(8)], axis=0)
        for m in maps:
            m["xin"] = xin
        nc_full = _get(1, 'full')
        res = run_bass_kernel_spmd(nc_full, maps, core_ids=list(range(8)))
        xTs = [res.results[c]["yT"] for c in range(8)]
    out = np.empty((2, 8192, DM), np.float32)
    for c in range(8):
        b, s = c // 4, c % 4
        out[b, s * NTOK:(s + 1) * NTOK, :] = xTs[c].T
    return out


def kernel(**inputs):
    inp = {k: np.asarray(v) for k, v in inputs.items()}
    S = 8192
    nc = _get(4, 'seq', S)
    sp = np.stack([_pack_sp(inp, l) for l in range(4)])
    pw = np.stack([_pack_pw(inp, l) for l in range(4)])
    cst = _consts(0)
    x = inp["x"].astype(np.float32, copy=False)
    maps = []
    for b in range(2):
        maps.append(dict(
            xT=np.ascontiguousarray(x[b].T),
            posT=np.ascontiguousarray(inp["positions"][b].reshape(S // 128, 128).T.astype(np.int32)),
            cst=cst, sp=sp, pw=pw,
            w_in=inp["w_in"], w_pp=inp["w_pool_proj"], w_cp=inp["w_conv_proj"], w_rp=inp["w_ret_proj"],
            w_out=inp["w_out"], w_fi=inp["w_ffn_in"], w_fo=inp["w_ffn_out"]))
    res = run_bass_kernel_spmd(nc, maps, core_ids=[0, 1])
    out = np.empty((2, S, DM), np.float32)
    for b in range(2):
        out[b] = res.results[b]["yT"].T
    return out
```

```python
import numpy as np
from contextlib import ExitStack
import concourse.bass as bass
import concourse.mybir as mybir
from concourse.bass_utils import run_bass_kernel_spmd

F32 = mybir.dt.float32
BF = mybir.dt.bfloat16
I32 = mybir.dt.int32
ALU = mybir.AluOpType
AF = mybir.ActivationFunctionType

DM = 2048
NTOK = 2048
TT = 512
NTILE = NTOK // TT
NIN = 11776
FH = 5632
O_POOL, O_CA, O_CG, O_Q, O_K, O_V, O_GR, O_GATE = 0, 512, 1024, 1536, 2560, 3584, 4608, 5632
NSLOT = 6
XW = 1216
NSP = 212
NCST = 1564
PI = float(np.pi)
DEBUG = False
STOP = None


class _Stop(Exception):
    pass


class V:
    def __init__(self, ap, k):
        self.ap = ap
        self.k = k


def _flat(lst):
    out = []
    for x in lst:
        if isinstance(x, V):
            out.extend(x.k)
        elif isinstance(x, list):
            out.extend(_flat(x))
        else:
            out.append(x)
    return out


class Prog:
    ENG = ['pe', 'act', 'dve', 'pool', 'sp']

    def __init__(self):
        self.ops = []
        self.st = {}
        self.seqc = {}
        self.dry = False

    def add(self, eng, meth, kw, R=(), W=(), group=None):
        if self.dry:
            return -1
        R = _flat(list(R))
        W = _flat(list(W))
        idx = len(self.ops)
        stream = group if group is not None else eng
        deps = set()
        for k in R:
            e = self.st.get(k)
            if e is not None and e[0] is not None:
                deps.add(e[0])
        for k in W:
            e = self.st.get(k)
            if e is not None:
                if e[0] is not None:
                    deps.add(e[0])
                deps.update(e[1].values())
        seq = self.seqc.get(stream, 0) + 1
        self.seqc[stream] = seq
        self.ops.append(dict(eng=eng, meth=meth, kw=kw, deps=deps, stream=stream, seq=seq,
                             dma=group is not None, ms=False))
        for k in R:
            e = self.st.get(k)
            if e is None:
                e = [None, {}]
                self.st[k] = e
            e[1][stream] = idx
        for k in W:
            self.st[k] = [idx, {}]
        return idx

    def finalize(self):
        ops = self.ops
        hasdep = set()
        for op in ops:
            hasdep.update(op['deps'])
        know = {e: {} for e in self.ENG}
        snap = {}
        for i, op in enumerate(ops):
            E = op['eng']
            kn = know[E]
            waits = []
            for j in sorted(op['deps'], reverse=True):
                d = ops[j]
                if d['stream'] == 'pe' and E == 'pe':
                    continue
                if kn.get(d['stream'], 0) >= d['seq']:
                    continue
                waits.append(j)
                d['ms'] = True
                for s2, sq in snap[j].items():
                    if kn.get(s2, 0) < sq:
                        kn[s2] = sq
            op['waits'] = waits
            if i in hasdep:
                sn = dict(kn)
                if sn.get(op['stream'], 0) < op['seq']:
                    sn[op['stream']] = op['seq']
                snap[i] = sn
        cnt = {}
        for op in ops:
            if op['dma']:
                op['ms'] = True
            if op['ms']:
                c = cnt.get(op['stream'], 0) + (16 if op['dma'] else 1)
                cnt[op['stream']] = c
                op['cnt'] = c
        return sorted(cnt.keys())

    def emit(self, nc, block, sems):
        per = {e: [] for e in self.ENG}
        for op in self.ops:
            per[op['eng']].append(op)
        ops = self.ops

        def mk(E):
            def f(e):
                for op in per[E]:
                    for j in op['waits']:
                        d = ops[j]
                        e.wait_ge(sems[d['stream']], d['cnt'])
                    if op['meth'] is None:
                        continue
                    ins = getattr(e, op['meth'])(**op['kw'])
                    if op['ms']:
                        ins.then_inc(sems[op['stream']], 16 if op['dma'] else 1)
            return f
        block.tensor(mk('pe'))
        block.scalar(mk('act'))
        block.vector(mk('dve'))
        block.gpsimd(mk('pool'))
        block.sync(mk('sp'))


def build(L, mode, ntok=2048):
    nc = bass.Bass("TRN2", target_bir_lowering=False)
    NTOK = ntok
    NTILE = ntok // TT
    NCH = ntok // 128
    seq = mode == 'seq'
    P = Prog()
    es = ExitStack()

    def din(name, shape, dt):
        return nc.dram_tensor(name, shape, dt, kind="ExternalInput").ap()

    xT = din("xT", [DM, NTOK], F32)
    posT = din("posT", [128, NCH], I32)
    cstD = din("cst", [128, NCST], F32)
    spD = din("sp", [L, 128, NSP], F32)
    pwD = din("pw", [L, 128, 512], F32)
    w_in = din("w_in", [L, DM, NIN], F32)
    if mode != 'pre':
        w_pp = din("w_pp", [L, 512, DM], F32)
        w_cp = din("w_cp", [L, 512, DM], F32)
        w_rp = din("w_rp", [L, 1024, DM], F32)
        w_out = din("w_out", [L, DM, DM], F32)
        w_fi = din("w_fi", [L, DM, 2 * FH], F32)
        w_fo = din("w_fo", [L, FH, DM], F32)
    if mode == 'full':
        xinD = din("xin", [8 * 128, XW], F32)
    if mode == 'pre':
        xoutD = nc.dram_tensor("xout", [128, XW], F32, kind="ExternalOutput").ap()
    else:
        yT = nc.dram_tensor("yT", [DM, NTOK], F32, kind="ExternalOutput").ap()
    if DEBUG and mode == 'full':
        dbgA = nc.dram_tensor("dbgA", [128, 16 * 512], BF, kind="ExternalOutput").ap()
        dbgBR = nc.dram_tensor("dbgBR", [128, 16 * 512], BF, kind="ExternalOutput").ap()
        dbgM = nc.dram_tensor("dbgM", [128, 16 * 512], BF, kind="ExternalOutput").ap()
        dbgY = nc.dram_tensor("dbgY", [128, 16 * 512], F32, kind="ExternalOutput").ap()
        dbgQ = nc.dram_tensor("dbgQ", [128, 4 * 1024], BF, kind="ExternalOutput").ap()
        dbgK = nc.dram_tensor("dbgK", [128, 4 * 1024], BF, kind="ExternalOutput").ap()
        dbgCC = nc.dram_tensor("dbgCC", [128, 2048], F32, kind="ExternalOutput").ap()
        dbgSS = nc.dram_tensor("dbgSS", [128, 2048], F32, kind="ExternalOutput").ap()
        dbgHB = nc.dram_tensor("dbgHB", [128, 4 * 544], F32, kind="ExternalOutput").ap()
        dbgACC = nc.dram_tensor("dbgACC", [128, 4 * 512], F32, kind="ExternalOutput").ap()
        dbgSG = nc.dram_tensor("dbgSG", [128, 4 * 512], F32, kind="ExternalOutput").ap()
    ktmD = nc.dram_tensor("ktm_s", [NTOK, 1024], BF).ap()
    vtmD = nc.dram_tensor("vtm_s", [NTOK, 1024], BF).ap()
    if seq:
        xbuf = [nc.dram_tensor("xb%d" % i, [DM, NTOK], F32).ap() for i in range(2)]
        ccD = nc.dram_tensor("cc_s", [128, NCH * 128], F32).ap()
        ssD = nc.dram_tensor("ss_s", [128, NCH * 128], F32).ap()
    if mode == 'fused':
        xbuf = [nc.dram_tensor("xb%d" % i, [DM, NTOK], F32).ap() for i in range(2)]
        xchD = nc.dram_tensor("xch_s", [128, XW], F32).ap()
        xgD = nc.dram_tensor("xg_s", [8 * 128, XW], F32).ap()

    def sb(name, shape, dt):
        return es.enter_context(nc.sbuf_tensor(name, shape, dt))

    WR = [sb("wr%d" % i, [128, 4096], BF) for i in range(NSLOT)]
    ARB = 112 * 1024
    AR = sb("arena", [128, ARB // 2], BF)
    CST = sb("cstt", [128, NCST], F32)
    SPR = sb("spr", [128, NSP], F32)
    PWB = sb("pwb", [128, 512], BF)
    CC = sb("cc", [128, 4 if seq else 16, 128], F32)
    SS = sb("ss", [128, 4 if seq else 16, 128], F32)
    IDN = sb("idn", [128, 128], BF)
    ON_D = sb("ond", [128, 128], BF)
    ON_5 = sb("on5", [128, 128], BF)
    ON_1 = sb("on1", [128, 128], BF)
    ST = sb("stt", [128, 1024], F32)
    UPT = sb("upt", [128, 4, 16], F32)
    HBT = sb("hbt", [128, 4, 32], F32)
    POSI = sb("posi", [128, NCH], I32)
    RS = [sb("rs%d" % i, [128, 512], F32) for i in range(2)]
    FT = [sb("ft%d" % i, [128, 512], F32) for i in range(4)]
    BT = [sb("bt%d" % i, [128, 512], BF) for i in range(3)]
    SMTB = [sb("smt%d" % i, [128, 512], BF) for i in range(2)]
    DGB = [sb("dg%d" % i, [128, 128], BF) for i in range(4)]
    CEN = sb("cen", [128, 128], BF)
    PSB = [es.enter_context(nc.psum_tensor("ps%d" % i, [128, 512], F32)) for i in range(8)]

    def sv(t, name):
        return V(t[:], [(name,)])

    vCST = sv(CST, "cst"); vSPR = sv(SPR, "spr"); vPWB = sv(PWB, "pwb")
    vCC = sv(CC, "cc"); vSS = sv(SS, "ss"); vIDN = sv(IDN, "idn")
    vOND = sv(ON_D, "ond"); vON5 = sv(ON_5, "on5"); vON1 = sv(ON_1, "on1")
    vST = [V(ST[:, i * 512:(i + 1) * 512], [("st", i)]) for i in range(2)]
    vUPT = sv(UPT, "upt"); vHBT = sv(HBT, "hbt"); vPOSI = sv(POSI, "posi")
    vRS = [sv(RS[i], "rs%d" % i) for i in range(2)]
    vFT = [sv(FT[i], "ft%d" % i) for i in range(4)]
    vBT = [sv(BT[i], "bt%d" % i) for i in range(3)]
    vSMT = [sv(SMTB[i], "smt%d" % i) for i in range(2)]
    vDG = [sv(DGB[i], "dg%d" % i) for i in range(4)]
    vCEN = sv(CEN, "cen")
    PS = [V(PSB[i][:], [("ps", i)]) for i in range(8)]
    rot = {}

    def nxt(name, lst):
        i = rot.get(name, 0)
        rot[name] = i + 1
        return lst[i % len(lst)]

    def bank():
        return nxt("bank", PS[0:7])
    PSTAT = PS[7]

    def arv(off, shape, dt):
        n = 1
        for s in shape[1:]:
            n *= s
        nb = n * (2 if dt == BF else 4)
        ap = AR[:, off // 2: off // 2 + nb // 2]
        if dt != BF:
            ap = ap.bitcast(dt)
        if len(shape) == 3:
            ap = ap.rearrange("p (a b) -> p a b", a=shape[1])
        keys = [("AR", g) for g in range(off // 1024, (off + nb + 1023) // 1024)]
        return V(ap, keys)

    KB = 1024
    A = [arv(kc * KB, [128, 512], BF) for kc in range(16)]
    A_all = arv(0, [128, 16, 512], BF)
    BR = [arv(16 * KB + i * KB, [128, 512], BF) for i in range(16)]
    M = [arv(32 * KB + i * KB, [128, 512], BF) for i in range(16)]
    HID = [arv(32 * KB + j * KB, [128, 512], BF) for j in range(44)]
    Y = [arv(80 * KB + i * 2 * KB, [128, 512], F32) for i in range(16)]
    Y_all = arv(80 * KB, [128, 16, 512], F32)
    Y2 = [arv(i * 2 * KB, [128, 512], F32) for i in range(16)]
    QTM = [arv(48 * KB + n * 2 * KB, [128, 1024], BF) for n in range(4)]
    KTM = [arv(56 * KB + n * 2 * KB, [128, 1024], BF) for n in range(4)]
    VTM = [arv(64 * KB + n * 2 * KB, [128, 1024], BF) for n in range(4)]
    KTM_all = arv(56 * KB, [128, 4, 1024], BF)
    VTM_all = arv(64 * KB, [128, 4, 1024], BF)
    QT = [arv(72 * KB + h * KB, [128, 512], BF) for h in range(8)]
    KT = [arv(80 * KB + h * KB, [128, 512], BF) for h in range(8)]
    GS = [arv(88 * KB + h * KB, [128, 512], BF) for h in range(8)]
    SBS = [[arv(96 * KB + n * 2 * KB + hf * KB, [128, 512], BF) for hf in range(2)] for n in range(5)]
    UP = arv(48 * KB, [128, 4, 528], F32)
    SA = arv(48 * KB + 8448, [128, 4, 528], F32)
    SBF = arv(48 * KB + 2 * 8448, [128, 4, 528], F32)
    PP = [arv(48 * KB + 3 * 8448 + g * KB, [128, 512], BF) for g in range(4)]
    HB = arv(48 * KB, [128, 4, 544], BF)
    SG = arv(48 * KB + 8704, [128, 4, 512], F32)
    ACC = arv(48 * KB + 8704 + 8192, [128, 4, 512], F32)
    XT_ = [arv(80 * KB + i * 5 * KB, [128, XW], F32) for i in range(2)]
    vXC = [arv(i * 2 * KB, [128, 512], F32) for i in range(2)]

    class WStream:
        def __init__(self):
            self.reqs = []
            self.pos = 0
            self.issued = 0

        @staticmethod
        def _views(slot, parts):
            vs = []
            off = 0
            for (wt, l, r0, nk, c0, cols) in parts:
                vs.append(WR[slot][:, off:off + nk * cols].rearrange("p (k c) -> p k c", k=nk))
                off += nk * cols
            return vs

        @staticmethod
        def _keys(slot, pi, np_):
            if np_ == 1:
                return [("wr", slot, 0), ("wr", slot, 1), ("wr", slot, 2)]
            return [("wr", slot, pi)]

        def _issue(self, j):
            parts = self.reqs[j]
            slot = j % NSLOT
            vs = self._views(slot, parts)
            np_ = len(parts)
            for pi, (view, (wt, l, r0, nk, c0, cols)) in enumerate(zip(vs, parts)):
                src = wt[l, r0:r0 + nk * 128, c0:c0 + cols].rearrange("(k p) c -> p k c", p=128)
                P.add('pool', 'dma_start', dict(out=view, in_=src), R=[], W=self._keys(slot, pi, np_),
                      group="w%d_%d" % (slot, pi))

        def getm(self, parts, back=1):
            if P.dry:
                self.reqs.append(tuple(parts))
                return [V(v, [("wr", 0, 0)]) for v in self._views(0, parts)]
            i = self.pos
            self.pos += 1
            while self.issued <= min(len(self.reqs) - 1, i - back + NSLOT - 1):
                self._issue(self.issued)
                self.issued += 1
            slot = i % NSLOT
            return [V(v, self._keys(slot, pi, len(parts))) for pi, v in enumerate(self._views(slot, parts))]

        def get(self, wt, l, r0, nk, c0, cols, back=1):
            return self.getm([(wt, l, r0, nk, c0, cols)], back)[0]
    WS = WStream()

    def mm(out, lhsT, rhs, start, stop, R, W):
        P.add('pe', 'matmul', dict(out=out, lhsT=lhsT, rhs=rhs, start=start, stop=stop), R=R, W=W)

    def dve(meth, R, W, **kw):
        P.add('dve', meth, kw, R=R, W=W)

    def act(out, in_, func, R, W, **kw):
        P.add('act', 'activation', dict(out=out, in_=in_, func=func, **kw), R=R, W=W)

    def dma(q, out, in_, R, W, group):
        return P.add(q, 'dma_start', dict(out=out, in_=in_), R=R, W=W, group=group)

    def b3(ap2, n):
        return ap2.rearrange("p (a b) -> p a b", a=n)

    def rstd_of(src, eps_col):
        r = nxt("rs", vRS)
        act(r.ap, src.ap, AF.Sqrt, [src, vCST], [r], bias=CST[:, eps_col:eps_col + 1], scale=1.0)
        dve('reciprocal', [r], [r], out=r.ap, in_=r.ap)
        return r

    def norm(X, gcol0, Aout):
        pb = bank()
        for kc in range(16):
            sq = nxt("bt", vBT)
            act(sq.ap, X[kc].ap, AF.Square, [X[kc]], [sq])
            mm(pb.ap, ON_D[:], sq.ap, kc == 0, kc == 15, [sq, vOND], [pb])
        r = rstd_of(pb, 1562)
        for kc in range(16):
            dve('scalar_tensor_tensor', [X[kc], r, vSPR], [Aout[kc]], out=Aout[kc].ap, in0=X[kc].ap,
                scalar=SPR[:, gcol0 + kc:gcol0 + kc + 1], in1=r.ap, op0=ALU.mult, op1=ALU.mult)

    def rotary(src, gn, dst_ap, dstv):
        s3 = b3(src.ap, 4)
        ta = nxt("ft", vFT)
        tb = nxt("ft", vFT)
        dve('tensor_tensor', [src, vCC], [ta], out=b3(ta.ap, 4), in0=s3,
            in1=CC[:, gn, :].unsqueeze(1).to_broadcast([128, 4, 128]), op=ALU.mult)
        dve('tensor_tensor', [src, vSS], [tb], out=b3(tb.ap, 4)[:, :, 0:64], in0=s3[:, :, 64:128],
            in1=SS[:, gn, 0:64].unsqueeze(1).to_broadcast([128, 4, 64]), op=ALU.mult)
        dve('tensor_tensor', [src, vSS, tb], [tb], out=b3(tb.ap, 4)[:, :, 64:128], in0=s3[:, :, 0:64],
            in1=SS[:, gn, 64:128].unsqueeze(1).to_broadcast([128, 4, 64]), op=ALU.mult)
        dve('tensor_tensor', [ta, tb], [dstv], out=dst_ap, in0=ta.ap, in1=tb.ap, op=ALU.add)

    def tok_proj(l, coff, cg, n, Aall):
        pb = bank()
        w0, w1 = tok_proj.w
        for kc in range(16):
            w = w0 if kc < 8 else w1
            mm(pb.ap, A[kc].ap[:, n * 128:(n + 1) * 128], w.ap[:, kc % 8, :], kc == 0, kc == 15,
               [A[kc], w], [pb])
        return pb

    def load_x_tile(xsrc, xname, tt):
        t0 = tt * TT
        for kc in range(16):
            dma('sp', Y[kc].ap, xsrc[kc * 128:(kc + 1) * 128, t0:t0 + TT], [(xname, tt, kc)], [Y[kc]], "ldx%d" % kc)

    def kv_update(n, write_sb):
        for hf in range(2):
            pb = bank()
            for hq in range(4):
                h = hf * 4 + hq
                mm(pb.ap[:, hq * 128:(hq + 1) * 128], KTM[n].ap[:, h * 128:(h + 1) * 128],
                   VTM[n].ap[:, h * 128:(h + 1) * 128], True, True, [KTM[n], VTM[n]], [pb])
            s = vST[hf]
            dve('tensor_tensor', [s, pb], [s], out=s.ap, in0=s.ap, in1=pb.ap, op=ALU.add)
            dve('tensor_tensor', [s, vCST], [s], out=s.ap, in0=s.ap,
                in1=CST[:, 144 + hf * 512:144 + (hf + 1) * 512], op=ALU.mult)
            if write_sb:
                act(SBS[n + 1][hf].ap, s.ap, AF.Copy, [s], [SBS[n + 1][hf]])

    def pre_kv_tile(l, tt, cbase):
        for (coff, isk) in ((O_K, True), (O_V, False)):
            for cg in range(2):
                w0 = WS.get(w_in, l, 0, 8, coff + cg * 512, 512)
                w1 = WS.get(w_in, l, 1024, 8, coff + cg * 512, 512)
                tok_proj.w = (w0, w1)
                for n in range(4):
                    pb = tok_proj(l, coff, cg, n, None)
                    if isk:
                        kd = nxt("ft", vFT)
                        dve('tensor_tensor', [pb, vCST], [kd], out=b3(kd.ap, 4), in0=b3(pb.ap, 4),
                            in1=CST[:, 136 + cg * 4:136 + cg * 4 + 4].unsqueeze(2).to_broadcast([128, 4, 128]),
                            op=ALU.mult)
                        rotary(kd, cbase + n, KTM[n].ap[:, cg * 512:(cg + 1) * 512], KTM[n])
                    else:
                        act(VTM[n].ap[:, cg * 512:(cg + 1) * 512], pb.ap, AF.Copy, [pb], [VTM[n]])

    def u_proj_conv_pool(l, tt, tails_only):
        dve('tensor_copy', [vUPT], [UP], out=UP.ap[:, :, 1:16], in_=UPT[:, :, 1:16])
        for half in range(2):
            w = WS.get(w_in, l, 0, 16, O_POOL + half * 256, 256)
            for gi in range(2):
                g = half * 2 + gi
                pb = bank()
                for kc in range(16):
                    mm(pb.ap, w.ap[:, kc, gi * 128:(gi + 1) * 128], A[kc].ap, kc == 0, kc == 15, [A[kc], w], [pb])
                act(UP.ap[:, g, 16:528], pb.ap, AF.Copy, [pb], [UP])
        dve('tensor_copy', [UP], [vUPT], out=UPT[:, :, 1:16], in_=UP.ap[:, :, 513:528])
        if not tails_only:
            wins = (2, 4, 8, 16)
            src = UP
            bufs = [SA, SBF]
            for g in range(4):
                sh = wins[g] // 2
                dstb = bufs[g % 2]
                dve('tensor_tensor', [src], [dstb], out=dstb.ap[:, g:4, 2 * sh:528], in0=src.ap[:, g:4, 2 * sh:528],
                    in1=src.ap[:, g:4, sh:528 - sh], op=ALU.add)
                dve('scalar_tensor_tensor', [dstb, UP], [PP[g]], out=PP[g].ap, in0=dstb.ap[:, g, 16:528],
                    scalar=1.0 / wins[g], in1=UP.ap[:, g, 16:528], op0=ALU.mult, op1=ALU.subtract)
                if tt == 0:
                    t1 = nxt("ft", vFT)
                    dve('tensor_tensor', [dstb, vCST], [t1], out=t1.ap[:, 0:16], in0=dstb.ap[:, g, 16:32],
                        in1=CST[:, 1240 + g * 16:1240 + (g + 1) * 16], op=ALU.mult)
                    dve('tensor_tensor', [t1, UP, PP[g]], [PP[g]], out=PP[g].ap[:, 0:16], in0=t1.ap[:, 0:16],
                        in1=UP.ap[:, g, 16:32], op=ALU.subtract)
                src = dstb
            for g in range(4):
                pb = bank()
                mm(pb.ap, PWB[:, g * 128:(g + 1) * 128], PP[g].ap, True, True, [PP[g], vPWB], [pb])
                dve('tensor_scalar', [pb, vSPR], [BR[g]], out=BR[g].ap, in0=pb.ap, scalar1=SPR[:, 64 + g:65 + g],
                    scalar2=None, op0=ALU.mult)
        for half in range(2):
            w = WS.get(w_in, l, 0, 16, O_CG + half * 256, 256)
            for gi in range(2):
                j = half * 2 + gi
                pb = bank()
                for kc in range(16):
                    mm(pb.ap, w.ap[:, kc, gi * 128:(gi + 1) * 128], A[kc].ap, kc == 0, kc == 15, [A[kc], w], [pb])
                act(SG.ap[:, j, :], pb.ap, AF.Sigmoid, [pb], [SG])
        dve('tensor_copy', [vHBT], [HB], out=HB.ap[:, :, 2:32], in_=HBT[:, :, 2:32])
        for half in range(2):
            w = WS.get(w_in, l, 0, 16, O_CA + half * 256, 256)
            for gi in range(2):
                j = half * 2 + gi
                pb = bank()
                for kc in range(16):
                    mm(pb.ap, w.ap[:, kc, gi * 128:(gi + 1) * 128], A[kc].ap, kc == 0, kc == 15, [A[kc], w], [pb])
                dve('tensor_tensor', [pb, SG], [HB], out=HB.ap[:, j, 32:544], in0=pb.ap, in1=SG.ap[:, j, :], op=ALU.mult)
        dve('tensor_copy', [HB], [vHBT], out=HBT[:, :, 2:32], in_=HB.ap[:, :, 514:544])
        if tails_only:
            return
        for j in range(4):
            pc = bank()
            for t in range(31):
                dg = nxt("dg", vDG)
                dve('tensor_scalar', [vIDN, vSPR], [dg], out=dg.ap, in0=IDN[:], scalar1=SPR[:, 88 + j * 31 + t:89 + j * 31 + t],
                    scalar2=None, op0=ALU.mult)
                mm(pc.ap, dg.ap, HB.ap[:, j, 2 + t:514 + t], t == 0, t == 30, [dg, HB], [pc])
            act(ACC.ap[:, j, :], pc.ap, AF.Identity, [pc, vSPR], [ACC], bias=SPR[:, 68 + j:69 + j], scale=1.0)
        if DEBUG and mode == 'full' and tt == 0:
            dma('sp', dbgHB, HB.ap.rearrange("p a b -> p (a b)"), [HB], [("dbgHB",)], "dbgHB")
            dma('sp', dbgACC, ACC.ap.rearrange("p a b -> p (a b)"), [ACC], [("dbgACC",)], "dbgACC")
            dma('sp', dbgSG, SG.ap.rearrange("p a b -> p (a b)"), [SG], [("dbgSG",)], "dbgSG")
        pm = bank()
        pq = bank()
        c16s = [arv(48 * KB + 8704 + j * KB, [128, 512], BF) for j in range(8)]
        for j in range(4):
            act(c16s[j].ap, ACC.ap[:, j, :], AF.Copy, [ACC], [c16s[j]])
            act(c16s[4 + j].ap, ACC.ap[:, j, :], AF.Square, [ACC], [c16s[4 + j]])
        for j in range(4):
            mm(pm.ap, ON_5[:], c16s[j].ap, j == 0, j == 3, [c16s[j], vON5], [pm])
        for j in range(4):
            mm(pq.ap, ON_5[:], c16s[4 + j].ap, j == 0, j == 3, [c16s[4 + j], vON5], [pq])
        ms = nxt("ft", vFT)
        act(ms.ap, pm.ap, AF.Copy, [pm], [ms])
        m2 = nxt("ft", vFT)
        dve('tensor_tensor', [ms], [m2], out=m2.ap, in0=ms.ap, in1=ms.ap, op=ALU.mult)
        dve('tensor_tensor', [pq, m2], [m2], out=m2.ap, in0=pq.ap, in1=m2.ap, op=ALU.subtract)
        dve('tensor_scalar', [m2], [m2], out=m2.ap, in0=m2.ap, scalar1=0.0, scalar2=None, op0=ALU.max)
        r = rstd_of(m2, 1563)
        tpair = [nxt("ft", vFT), nxt("ft", vFT)]
        for j in range(4):
            t = tpair[j % 2]
            dve('tensor_tensor', [ACC, ms], [t], out=t.ap, in0=ACC.ap[:, j, :], in1=ms.ap, op=ALU.subtract)
            dve('tensor_tensor', [t, r], [t], out=t.ap, in0=t.ap, in1=r.ap, op=ALU.mult)
            act(BR[4 + j].ap, t.ap, AF.Silu, [t, vSPR], [BR[4 + j]], scale=SPR[:, 72 + j:73 + j],
                bias=SPR[:, 76 + j:77 + j])

    def post_norm_residual(Yo, gcol0, xget, tt, after=None):
        flush_stats()
        r = rstd_of(PSTAT, 1562)
        for dc in range(16):
            xv = xget(dc)
            t = nxt("ft", vFT)
            dve('scalar_tensor_tensor', [Yo[dc], r, vSPR], [t], out=t.ap, in0=Yo[dc].ap,
                scalar=SPR[:, gcol0 + dc:gcol0 + dc + 1], in1=r.ap, op0=ALU.mult, op1=ALU.mult)
            dve('tensor_tensor', [t, xv], [Y[dc]], out=Y[dc].ap, in0=t.ap, in1=xv.ap, op=ALU.add)
            if after is not None:
                after(dc)

    pend_stats = []

    def flush_stats(keep=0):
        while len(pend_stats) > keep:
            sq, dc = pend_stats.pop(0)
            mm(PSTAT.ap, ON_D[:], sq.ap, dc == 0, dc == 15, [sq, vOND], [PSTAT])

    def evac_stats(pb, Yo, dc):
        flush_stats(1)
        act(Yo[dc].ap, pb.ap, AF.Copy, [pb], [Yo[dc]])
        sq = nxt("bt", vBT)
        act(sq.ap, pb.ap, AF.Square, [pb], [sq])
        pend_stats.append((sq, dc))

    def body():
        rot.clear()
        out_dmas = []
        try:
            body_main(out_dmas)
        except _Stop:
            pass
        body_fin(out_dmas)

    def body_main(out_dmas):
        dma('sp', CST[:], cstD[:, :], [], [vCST], "ldc")
        dma('sp', POSI[:], posT[:, :], [], [vPOSI], "ldp")
        dve('tensor_copy', [vCST], [vIDN], out=IDN[:], in_=CST[:, 1434:1562])
        dve('tensor_scalar', [vIDN], [vCEN], out=CEN[:], in0=IDN[:], scalar1=-1.0 / 128, scalar2=None, op0=ALU.add)
        dve('memset', [], [vOND], ap=ON_D[:], constant=1.0 / 2048)
        dve('memset', [], [vON5], ap=ON_5[:], constant=1.0 / 512)
        dve('memset', [], [vON1], ap=ON_1[:], constant=1.0 / 128)
        posf = vFT[0]
        dve('tensor_copy', [vPOSI], [posf], out=posf.ap[:, 0:NCH], in_=POSI[:])
        T1 = arv(0, [128, 16, 128], F32)
        T2 = arv(8 * KB, [128, 16, 128], F32)
        T3 = arv(16 * KB, [128, 16, 128], F32)
        TIv = arv(24 * KB, [128, 16, 128], F32)
        TI = V(TIv.ap.bitcast(I32), TIv.k)
        T4 = arv(32 * KB, [128, 16, 128], F32)
        T5 = arv(40 * KB, [128, 16, 128], F32)
        for piece in range(NCH // 16):
            if seq:
                cdst, sdst, cv, sv_ = T4.ap, T5.ap, T4, T5
            else:
                cdst, sdst, cv, sv_ = CC[:], SS[:], vCC, vSS
            dve('tensor_tensor', [posf, vCST], [T1], out=T1.ap,
                in0=posf.ap[:, piece * 16:(piece + 1) * 16].unsqueeze(2).to_broadcast([128, 16, 128]),
                in1=CST[:, 0:128].unsqueeze(1).to_broadcast([128, 16, 128]), op=ALU.mult)
            dve('tensor_scalar', [T1], [T2], out=T2.ap, in0=T1.ap, scalar1=1.0 / (2 * PI), scalar2=None, op0=ALU.mult)
            dve('tensor_copy', [T2], [TI], out=TI.ap, in_=T2.ap)
            dve('tensor_copy', [TI], [T2], out=T2.ap, in_=TI.ap)
            dve('scalar_tensor_tensor', [T2, T1], [T1], out=T1.ap, in0=T2.ap, scalar=-2 * PI, in1=T1.ap, op0=ALU.mult, op1=ALU.add)
            act(T2.ap, T1.ap, AF.Sin, [T1], [T2], scale=0.5)
            act(T3.ap, T1.ap, AF.Sin, [T1], [T3], scale=0.25)
            dve('tensor_tensor', [T2], [cv], out=cdst, in0=T2.ap, in1=T2.ap, op=ALU.mult)
            dve('tensor_scalar', [cv], [cv], out=cdst, in0=cdst, scalar1=-2.0, scalar2=1.0, op0=ALU.mult, op1=ALU.add)
            dve('tensor_tensor', [T3], [T3], out=T3.ap, in0=T3.ap, in1=T3.ap, op=ALU.mult)
            dve('tensor_scalar', [T3], [T3], out=T3.ap, in0=T3.ap, scalar1=-2.0, scalar2=1.0, op0=ALU.mult, op1=ALU.add)
            dve('scalar_tensor_tensor', [T2, T3], [sv_], out=sdst[:, :, 64:128], in0=T2.ap[:, :, 64:128], scalar=2.0,
                in1=T3.ap[:, :, 64:128], op0=ALU.mult, op1=ALU.mult)
            dve('scalar_tensor_tensor', [T2, T3, sv_], [sv_], out=sdst[:, :, 0:64], in0=T2.ap[:, :, 0:64], scalar=-2.0,
                in1=T3.ap[:, :, 0:64], op0=ALU.mult, op1=ALU.mult)
            if seq:
                dma('sp', ccD[:, piece * 2048:(piece + 1) * 2048], T4.ap.rearrange("p a b -> p (a b)"), [T4], [("ccD", piece)], "stcc")
                dma('sp', ssD[:, piece * 2048:(piece + 1) * 2048], T5.ap.rearrange("p a b -> p (a b)"), [T5], [("ssD", piece)], "stss")

        if DEBUG and mode == 'full':
            dma('sp', dbgCC, CC[:].rearrange("p a b -> p (a b)"), [vCC], [("dbgCC",)], "dbgCC")
            dma('sp', dbgSS, SS[:].rearrange("p a b -> p (a b)"), [vSS], [("dbgSS",)], "dbgSS")
        for l in range(L) if True else []:
            if mode == 'fused' or seq:
                xsrc = xT if l == 0 else xbuf[(l - 1) % 2]
                xdst = yT if l == L - 1 else xbuf[l % 2]
                xsn = "xT" if l == 0 else "xb%d" % ((l - 1) % 2)
                xdn = "yT" if l == L - 1 else "xb%d" % (l % 2)
            else:
                xsrc = xT
                xdst = None if mode == 'pre' else yT
                xsn, xdn = "xT", "yT"
            dma('sp', SPR[:], spD[l], [], [vSPR], "ldsp")
            dma('pool', PWB[:], pwD[l], [], [vPWB], "ldpw")
            dve('memset', [], [vST[0]], ap=ST[:, 0:512], constant=0.0)
            dve('memset', [], [vST[1]], ap=ST[:, 512:1024], constant=0.0)
            dve('memset', [], [vUPT], ap=UPT[:], constant=0.0)
            dve('memset', [], [vHBT], ap=HBT[:], constant=0.0)
            for tt in range(0 if seq else NTILE):
                t0 = tt * TT
                load_x_tile(xsrc, xsn, tt)
                norm(Y, 0, A)
                pre_kv_tile(l, tt, tt * 4)
                dma('sp', ktmD[t0:t0 + TT, :].rearrange("(n p) c -> p n c", p=128), KTM_all.ap, [KTM_all], [("ktmD", tt)], "stk")
                dma('sp', vtmD[t0:t0 + TT, :].rearrange("(n p) c -> p n c", p=128), VTM_all.ap, [VTM_all], [("vtmD", tt)], "stv")
                for n in range(4):
                    kv_update(n, False)
                if tt == NTILE - 1:
                    u_proj_conv_pool(l, tt, True)
            if mode == 'pre':
                xo = xoutD
            elif mode == 'fused':
                xo = xchD
            if mode in ('pre', 'fused'):
                i1 = dma('sp', xo[:, 0:512], ST[:, 0:512], [vST[0]], [("xo", 0)], "sx0")
                i2 = dma('sp', xo[:, 512:1024], ST[:, 512:1024], [vST[1]], [("xo", 1)], "sx1")
                i3 = dma('sp', xo[:, 1024:1152], HBT[:].rearrange("p a b -> p (a b)"), [vHBT], [("xo", 2)], "sx2")
                i4 = dma('sp', xo[:, 1152:1216], UPT[:].rearrange("p a b -> p (a b)"), [vUPT], [("xo", 3)], "sx3")
                out_dmas += [i1, i2, i3, i4]
            if mode == 'pre':
                continue
            if mode == 'fused':
                P.add('pool', 'collective_compute',
                      dict(kind="AllGather", op=ALU.bypass, replica_groups=[list(range(8))],
                           ins=[xchD[:, :]], outs=[xgD[:, :]]),
                      R=[("xo", 0), ("xo", 1), ("xo", 2), ("xo", 3)], W=[("xg",)], group="cc")
                xin_src = xgD
            elif not seq:
                xin_src = xinD
            if not seq:
                dve('memset', [vST[0]], [vST[0]], ap=ST[:, 0:512], constant=0.0)
                dve('memset', [vST[1]], [vST[1]], ap=ST[:, 512:1024], constant=0.0)
                dve('memset', [vUPT], [vUPT], ap=UPT[:], constant=0.0)
                dve('memset', [vHBT], [vHBT], ap=HBT[:], constant=0.0)
            for r in range(0 if seq else 8):
                xt = XT_[r % 2]
                dma('sp', xt.ap, xin_src[r * 128:(r + 1) * 128, :], [("xg",)], [xt], "ldxg%d" % (r % 2))
                for h in range(8):
                    s = vST[h // 4]
                    dve('scalar_tensor_tensor', [xt, s, vCST], [s], out=ST[:, h * 128:(h + 1) * 128],
                        in0=xt.ap[:, h * 128:(h + 1) * 128], scalar=CST[:, 1168 + r * 8 + h:1169 + r * 8 + h],
                        in1=ST[:, h * 128:(h + 1) * 128], op0=ALU.mult, op1=ALU.add)
                hb2 = HBT[:].rearrange("p a b -> p (a b)")
                dve('scalar_tensor_tensor', [xt, vHBT, vCST], [vHBT], out=hb2, in0=xt.ap[:, 1024:1152],
                    scalar=CST[:, 1232 + r:1233 + r], in1=hb2, op0=ALU.mult, op1=ALU.add)
                up2 = UPT[:].rearrange("p a b -> p (a b)")
                dve('scalar_tensor_tensor', [xt, vUPT, vCST], [vUPT], out=up2, in0=xt.ap[:, 1152:1216],
                    scalar=CST[:, 1232 + r:1233 + r], in1=up2, op0=ALU.mult, op1=ALU.add)
            for tt in range(NTILE):
                t0 = tt * TT
                if seq:
                    dma('sp', CC[:], ccD[:, tt * 512:(tt + 1) * 512].rearrange("p (a b) -> p a b", a=4), [("ccD", tt // 4)], [vCC], "ldcc")
                    dma('sp', SS[:], ssD[:, tt * 512:(tt + 1) * 512].rearrange("p (a b) -> p a b", a=4), [("ssD", tt // 4)], [vSS], "ldss")
                load_x_tile(xsrc, xsn, tt)
                norm(Y, 0, A)
                if DEBUG and tt == 0:
                    dma('sp', dbgA, AR[:, 0:8192], [A_all], [("dbgA",)], "dbgA")
                u_proj_conv_pool(l, tt, False)
                if STOP == 'proj' and tt == 0:
                    raise _Stop()
                for cg in range(2):
                    w0 = WS.get(w_in, l, 0, 8, O_Q + cg * 512, 512)
                    w1 = WS.get(w_in, l, 1024, 8, O_Q + cg * 512, 512)
                    tok_proj.w = (w0, w1)
                    for n in range(4):
                        pb = tok_proj(l, O_Q, cg, n, None)
                        qd = nxt("ft", vFT)
                        dve('tensor_tensor', [pb, vCST], [qd], out=b3(qd.ap, 4), in0=b3(pb.ap, 4),
                            in1=CST[:, 128 + cg * 4:128 + cg * 4 + 4].unsqueeze(2).to_broadcast([128, 4, 128]),
                            op=ALU.mult)
                        rotary(qd, (0 if seq else tt * 4) + n, QTM[n].ap[:, cg * 512:(cg + 1) * 512], QTM[n])
                if seq:
                    pre_kv_tile(l, tt, 0)
                else:
                    dma('sp', KTM_all.ap, ktmD[t0:t0 + TT, :].rearrange("(n p) c -> p n c", p=128), [("ktmD", tt)], [KTM_all], "ldk")
                    dma('sp', VTM_all.ap, vtmD[t0:t0 + TT, :].rearrange("(n p) c -> p n c", p=128), [("vtmD", tt)], [VTM_all], "ldv")
                if DEBUG and tt == 0:
                    dma('sp', dbgQ, AR[:, 24 * KB:28 * KB], [QTM], [("dbgQ",)], "dbgQ")
                    dma('sp', dbgK, AR[:, 28 * KB:32 * KB], [KTM], [("dbgK",)], "dbgK")
                for (src, dst) in ((QTM, QT), (KTM, KT)):
                    for h in range(8):
                        pb = bank()
                        pbb = pb.ap.bitcast(BF)
                        for n in range(4):
                            P.add('pe', 'transpose', dict(out=pbb[:, n * 128:(n + 1) * 128],
                                                          in_=src[n].ap[:, h * 128:(h + 1) * 128], identity=IDN[:]),
                                  R=[src[n], vIDN], W=[pb])
                        act(dst[h].ap, pbb[:, 0:512], AF.Copy, [pb], [dst[h]])
                if STOP == 'q' and tt == 0:
                    raise _Stop()
                def stage_G(i4):
                    w = WS.get(w_in, l, 0, 16, O_GR + i4 * 256, 256)
                    for gi in range(2):
                        h = i4 * 2 + gi
                        pb = bank()
                        for kc in range(16):
                            mm(pb.ap, w.ap[:, kc, gi * 128:(gi + 1) * 128], A[kc].ap, kc == 0, kc == 15, [A[kc], w], [pb])
                        act(GS[h].ap, pb.ap, AF.Silu, [pb], [GS[h]])
                for i4 in range(4):
                    stage_G(i4)
                for hf in range(2):
                    act(SBS[0][hf].ap, vST[hf].ap, AF.Copy, [vST[hf]], [SBS[0][hf]])
                for n in range(4):
                    kv_update(n, True)
                OFb = [arv(32 * KB + i * 2 * KB, [128, 512], F32) for i in range(2)]
                MSb = [arv(36 * KB + i * 2 * KB, [128, 512], F32) for i in range(2)]
                M2b = [arv(40 * KB + i * 2 * KB, [128, 512], F32) for i in range(2)]
                OBb = [arv(44 * KB + i * KB, [128, 512], BF) for i in range(2)]
                OQb = [arv(46 * KB + i * KB, [128, 512], BF) for i in range(2)]
                smts = {}
                pOs = {}

                def stage_S(h):
                    pS = bank()
                    for n in range(4):
                        mm(pS.ap[:, n * 128:(n + 1) * 128], KT[h].ap[:, n * 128:(n + 1) * 128],
                           QT[h].ap[:, n * 128:(n + 1) * 128], True, True, [KT[h], QT[h]], [pS])
                    smt = nxt("smt", vSMT)
                    dve('tensor_tensor', [pS, vCST], [smt], out=b3(smt.ap, 4), in0=b3(pS.ap, 4),
                        in1=CST[:, 1306:1434].unsqueeze(1).to_broadcast([128, 4, 128]), op=ALU.mult)
                    smts[h] = smt

                def stage_O(h):
                    hf, hq = h // 4, h % 4
                    smt = smts[h]
                    pO = bank()
                    for n in range(4):
                        mm(pO.ap[:, n * 128:(n + 1) * 128], VTM[n].ap[:, h * 128:(h + 1) * 128],
                           smt.ap[:, n * 128:(n + 1) * 128], True, False, [VTM[n], smt], [pO])
                        mm(pO.ap[:, n * 128:(n + 1) * 128], SBS[n][hf].ap[:, hq * 128:(hq + 1) * 128],
                           QT[h].ap[:, n * 128:(n + 1) * 128], False, True, [SBS[n][hf], QT[h]], [pO])
                    ob = OBb[h % 2]
                    act(ob.ap, pO.ap, AF.Copy, [pO], [ob])

                def stage_T(h):
                    of, ob, osq = OFb[h % 2], OBb[h % 2], OQb[h % 2]
                    pC = bank()
                    mm(pC.ap, CEN[:], ob.ap, True, True, [ob, vCEN], [pC])
                    act(osq.ap, pC.ap, AF.Square, [pC], [osq])
                    pq = bank()
                    mm(pq.ap, ON_1[:], osq.ap, True, True, [osq, vON1], [pq])
                    r = rstd_of(pq, 1563)
                    dve('tensor_tensor', [pC, r], [of], out=of.ap, in0=pC.ap, in1=r.ap, op=ALU.mult)
                    dve('scalar_tensor_tensor', [of, GS[h], vSPR], [BR[8 + h]], out=BR[8 + h].ap, in0=of.ap,
                        scalar=SPR[:, 80 + h:81 + h], in1=GS[h].ap, op0=ALU.mult, op1=ALU.mult)
                for step in range(10):
                    if step < 8:
                        stage_S(step)
                    if 0 <= step - 1 < 8:
                        stage_O(step - 1)
                    if 0 <= step - 2 < 8:
                        stage_T(step - 2)
                if DEBUG and tt == 0:
                    dma('sp', dbgBR, AR[:, 8192:16384], [BR], [("dbgBR",)], "dbgBR")
                if STOP == 'ret' and tt == 0:
                    raise _Stop()
                accs = [vFT[0], vFT[1]]
                sgb = [vFT[2], vFT[3]]
                for grp in range(8):
                    wbr = WS.getm([(w_pp, l, 0, 4, grp * 256, 256), (w_cp, l, 0, 4, grp * 256, 256),
                                   (w_rp, l, 0, 8, grp * 256, 256)])
                    for b in range(3):
                        wg = WS.get(w_in, l, 0, 16, O_GATE + b * 2048 + grp * 256, 256, back=b + 1)
                        nkb, off = ((4, 0), (4, 4), (8, 8))[b]
                        for dci in range(2):
                            dc = grp * 2 + dci
                            pg = bank()
                            for kc in range(16):
                                mm(pg.ap, wg.ap[:, kc, dci * 128:(dci + 1) * 128], A[kc].ap, kc == 0, kc == 15,
                                   [A[kc], wg], [pg])
                            sg = sgb[dci]
                            act(sg.ap, pg.ap, AF.Sigmoid, [pg], [sg])
                            py = bank()
                            for kc in range(nkb):
                                mm(py.ap, wbr[b].ap[:, kc, dci * 128:(dci + 1) * 128], BR[off + kc].ap, kc == 0,
                                   kc == nkb - 1, [BR[off + kc], wbr[b]], [py])
                            if b == 0:
                                dve('tensor_tensor', [sg, py], [accs[dci]], out=accs[dci].ap, in0=sg.ap, in1=py.ap, op=ALU.mult)
                            else:
                                dve('tensor_tensor', [sg, py], [sg], out=sg.ap, in0=sg.ap, in1=py.ap, op=ALU.mult)
                                if b == 1:
                                    dve('tensor_tensor', [accs[dci], sg], [accs[dci]], out=accs[dci].ap, in0=accs[dci].ap,
                                        in1=sg.ap, op=ALU.add)
                                else:
                                    dve('tensor_tensor', [accs[dci], sg], [M[dc]], out=M[dc].ap, in0=accs[dci].ap,
                                        in1=sg.ap, op=ALU.add)
                if DEBUG and tt == 0:
                    dma('sp', dbgM, AR[:, 16384:24576], [M], [("dbgM",)], "dbgM")
                if STOP == 'merge' and tt == 0:
                    raise _Stop()
                for cgi in range(8):
                    w = WS.get(w_out, l, 0, 16, cgi * 256, 256)
                    for dci in range(2):
                        dc = cgi * 2 + dci
                        pb = bank()
                        for kc in range(16):
                            mm(pb.ap, w.ap[:, kc, dci * 128:(dci + 1) * 128], M[kc].ap, kc == 0, kc == 15, [M[kc], w], [pb])
                        evac_stats(pb, Y, dc)

                def xget(dc):
                    xc = nxt("xc", vXC)
                    dma('sp', xc.ap, xsrc[dc * 128:(dc + 1) * 128, t0:t0 + TT], [(xsn, tt, dc)], [xc], "ldxc%d" % (rot["xc"] % 2))
                    return xc
                post_norm_residual(Y, 16, xget, tt)
                if DEBUG and tt == 0:
                    dma('sp', dbgY, Y_all.ap.rearrange("p a b -> p (a b)"), [Y_all], [("dbgY",)], "dbgY")
                if STOP == 'wout' and tt == 0:
                    raise _Stop()
                norm(Y, 32, A)
                for jg in range(22):
                    wa = WS.get(w_fi, l, 0, 16, jg * 256, 256)
                    wb = WS.get(w_fi, l, 0, 16, FH + jg * 256, 256)
                    for ji in range(2):
                        j = jg * 2 + ji
                        pa = bank()
                        for kc in range(16):
                            mm(pa.ap, wa.ap[:, kc, ji * 128:(ji + 1) * 128], A[kc].ap, kc == 0, kc == 15, [A[kc], wa], [pa])
                        sa = nxt("ft", vFT)
                        act(sa.ap, pa.ap, AF.Silu, [pa], [sa])
                        pbb = bank()
                        for kc in range(16):
                            mm(pbb.ap, wb.ap[:, kc, ji * 128:(ji + 1) * 128], A[kc].ap, kc == 0, kc == 15, [A[kc], wb], [pbb])
                        dve('tensor_tensor', [sa, pbb], [HID[j]], out=HID[j].ap, in0=sa.ap, in1=pbb.ap, op=ALU.mult)
                for cgi in range(8):
                    pbs = [bank(), bank()]
                    for rg in range(3):
                        nk = 16 if rg < 2 else 12
                        w = WS.get(w_fo, l, rg * 2048, nk, cgi * 256, 256)
                        for dci in range(2):
                            for kk in range(nk):
                                mm(pbs[dci].ap, w.ap[:, kk, dci * 128:(dci + 1) * 128], HID[rg * 16 + kk].ap,
                                   rg == 0 and kk == 0, rg == 2 and kk == nk - 1, [HID[rg * 16 + kk], w], [pbs[dci]])
                    for dci in range(2):
                        evac_stats(pbs[dci], Y2, cgi * 2 + dci)
                def store_chunk(dc):
                    i = dma('sp', xdst[dc * 128:(dc + 1) * 128, t0:t0 + TT], Y[dc].ap, [Y[dc]], [(xdn, tt, dc)], "sty%d" % dc)
                    if l == L - 1:
                        out_dmas.append(i)
                post_norm_residual(Y2, 48, lambda dc: Y[dc], tt, after=store_chunk)
            if mode == 'fused' and l < L - 1:
                pass
    def body_fin(out_dmas):
        if not P.dry:
            lastd = {}
            for i, op in enumerate(P.ops):
                if op['dma']:
                    lastd[op['stream']] = i
            out_dmas = list(out_dmas) + list(lastd.values())
            P.ops.append(dict(eng='sp', meth=None, kw=None, deps=set(out_dmas), stream='sp', seq=P.seqc.get('sp', 0) + 1,
                              dma=False, ms=False))
            P.seqc['sp'] = P.seqc.get('sp', 0) + 1

    P.dry = True
    body()
    P.dry = False
    body()
    streams = P.finalize()
    sems = {}
    for s in streams:
        sems[s] = es.enter_context(nc.semaphore("s_" + s))
    with nc.Block() as block:
        P.emit(nc, block, sems)
    es.close()
    return nc, len(P.ops)


_GAMMA = 1.0 - np.exp2(-5.0 - np.arange(8, dtype=np.float64))


def _consts(core):
    s = core % 4
    b = core // 4
    c = np.zeros((128, NCST), np.float32)
    half = 64
    inv = (np.float32(10000.0) ** (-np.arange(half, dtype=np.float32) / np.float32(half))).astype(np.float32)
    c[:, 0:64] = inv[None, :]
    c[:, 64:128] = inv[None, :]
    p = np.arange(128, dtype=np.float64)
    lg = np.log(_GAMMA)
    c[:, 128:136] = np.exp((p[:, None] + 1.0) * lg[None, :])
    c[:, 136:144] = np.exp(-(p[:, None] + 1.0) * lg[None, :]) * (128.0 ** -0.5)
    gC = np.exp(128.0 * lg)
    c[:, 144:1168] = np.repeat(gC, 128)[None, :]
    G = np.exp(2048.0 * lg)
    for r in range(8):
        rb, rs = r // 4, r % 4
        if rb == b and rs < s:
            c[:, 1168 + r * 8:1168 + r * 8 + 8] = (G ** (s - 1 - rs))[None, :]
        if rb == b and rs == s - 1:
            c[:, 1232 + r] = 1.0
    wins = (2, 4, 8, 16)
    for g in range(4):
        t = np.arange(16)
        if s == 0:
            c[:, 1240 + g * 16:1240 + (g + 1) * 16] = (1.0 / np.minimum(t + 1, wins[g]))[None, :]
        else:
            c[:, 1240 + g * 16:1240 + (g + 1) * 16] = 1.0 / wins[g]
    c[:, 1304] = -np.pi
    c[:, 1305] = np.pi
    e = np.arange(128)
    c[:, 1306:1434] = (e[:, None] <= e[None, :]).astype(np.float32)
    c[:, 1434:1562] = np.eye(128, dtype=np.float32)
    c[:, 1562] = 1e-6
    c[:, 1563] = 1e-5
    return c


def _fm(v, n):
    return np.ascontiguousarray(v.reshape(n, 128).T)


def _pack_sp(inp, l):
    sp = np.zeros((128, NSP), np.float32)
    sp[:, 0:16] = _fm(inp["g_mix_pre"][l], 16)
    sp[:, 16:32] = _fm(inp["g_mix_post"][l], 16)
    sp[:, 32:48] = _fm(inp["g_ffn_pre"][l], 16)
    sp[:, 48:64] = _fm(inp["g_ffn_post"][l], 16)
    sp[:, 64:68] = _fm(inp["pool_scale"][l], 4)
    sp[:, 68:72] = _fm(inp["conv_b"][l], 4)
    sp[:, 72:76] = _fm(inp["conv_ln_g"][l], 4)
    sp[:, 76:80] = _fm(inp["conv_ln_b"][l], 4)
    sp[:, 80:88] = _fm(inp["ret_gn_g"][l], 8)
    dw = inp["conv_dw"][l]
    sp[:, 88:212] = dw.T.reshape(4, 128, 31).transpose(1, 0, 2).reshape(128, 124)
    return sp


def _pack_pw(inp, l):
    pw = inp["pool_w"][l]
    return np.ascontiguousarray(pw.transpose(1, 0, 2).reshape(128, 512))


_CACHE = {}


def _get(L, mode, ntok=2048):
    k = (L, mode, ntok)
    if k not in _CACHE:
        _CACHE[k] = build(L, mode, ntok)[0]
    return _CACHE[k]


def _core_x(x, c):
    b, s = c // 4, c % 4
    return np.ascontiguousarray(x[b, s * NTOK:(s + 1) * NTOK, :].T)


def _core_pos(pos, c):
    b, s = c // 4, c % 4
    return np.ascontiguousarray(pos[b, s * NTOK:(s + 1) * NTOK].reshape(16, 128).T.astype(np.int32))


def _layer_maps(inp, layers, xTs):
    sp = np.stack([_pack_sp(inp, l) for l in layers])
    pw = np.stack([_pack_pw(inp, l) for l in layers])
    sl = layers if len(layers) > 1 else slice(layers[0], layers[0] + 1)
    ws = dict(w_in=inp["w_in"][sl], w_pp=inp["w_pool_proj"][sl], w_cp=inp["w_conv_proj"][sl],
              w_rp=inp["w_ret_proj"][sl], w_out=inp["w_out"][sl], w_fi=inp["w_ffn_in"][sl], w_fo=inp["w_ffn_out"][sl])
    maps = []
    for c in range(8):
        m = dict(xT=xTs[c], posT=_core_pos(inp["positions"], c), cst=_CSTS[c], sp=sp, pw=pw)
        m.update(ws)
        maps.append(m)
    return maps


_CSTS = None


def kernel_unfused(**inputs):
    inp = {k: np.asarray(v) for k, v in inputs.items()}
    global _CSTS
    _CSTS = [_consts(c) for c in range(8)]
    x = inp["x"].astype(np.float32, copy=False)
    xTs = [_core_x(x, c) for c in range(8)]
    for l in range(4):
        maps = _layer_maps(inp, [l], xTs)
        nc_pre = _get(1, 'pre')
        pre_keys = ("xT", "posT", "cst", "sp", "pw", "w_in")
        res = run_bass_kernel_spmd(nc_pre, [{k: m[k] for k in pre_keys} for m in maps], core_ids=list(range(8)))
        xin = np.concatenate([res.results[c]["xout"] for c in range(8)], axis=0)
        for m in maps:
            m["xin"] = xin
        nc_full = _get(1, 'full')
        res = run_bass_kernel_spmd(nc_full, maps, core_ids=list(range(8)))
        xTs = [res.results[c]["yT"] for c in range(8)]
    out = np.empty((2, 8192, DM), np.float32)
    for c in range(8):
        b, s = c // 4, c % 4
        out[b, s * NTOK:(s + 1) * NTOK, :] = xTs[c].T
    return out


def kernel(**inputs):
    inp = {k: np.asarray(v) for k, v in inputs.items()}
    S = 8192
    nc = _get(4, 'seq', S)
    sp = np.stack([_pack_sp(inp, l) for l in range(4)])
    pw = np.stack([_pack_pw(inp, l) for l in range(4)])
    cst = _consts(0)
    x = inp["x"].astype(np.float32, copy=False)
    maps = []
    for b in range(2):
        maps.append(dict(
            xT=np.ascontiguousarray(x[b].T),
            posT=np.ascontiguousarray(inp["positions"][b].reshape(S // 128, 128).T.astype(np.int32)),
            cst=cst, sp=sp, pw=pw,
            w_in=inp["w_in"], w_pp=inp["w_pool_proj"], w_cp=inp["w_conv_proj"], w_rp=inp["w_ret_proj"],
            w_out=inp["w_out"], w_fi=inp["w_ffn_in"], w_fo=inp["w_ffn_out"]))
    res = run_bass_kernel_spmd(nc, maps, core_ids=[0, 1])
    out = np.empty((2, S, DM), np.float32)
    for b in range(2):
        out[b] = res.results[b]["yT"].T
    return out
```

```python
import numpy as np
from contextlib import ExitStack
import concourse.bass as bass
import concourse.mybir as mybir
from concourse.bass_utils import run_bass_kernel_spmd

F32 = mybir.dt.float32
BF = mybir.dt.bfloat16
I32 = mybir.dt.int32
ALU = mybir.AluOpType
AF = mybir.ActivationFunctionType

DM = 2048
NTOK = 2048
TT = 512
NTILE = NTOK // TT
NIN = 11776
FH = 5632
O_POOL, O_CA, O_CG, O_Q, O_K, O_V, O_GR, O_GATE = 0, 512, 1024, 1536, 2560, 3584, 4608, 5632
NSLOT = 6
XW = 1216
NSP = 212
NCST = 1564
PI = float(np.pi)
DEBUG = False
STOP = None


class _Stop(Exception):
    pass


class V:
    def __init__(self, ap, k):
        self.ap = ap
        self.k = k


def _flat(lst):
    out = []
    for x in lst:
        if isinstance(x, V):
            out.extend(x.k)
        elif isinstance(x, list):
            out.extend(_flat(x))
        else:
            out.append(x)
    return out


class Prog:
    ENG = ['pe', 'act', 'dve', 'pool', 'sp']

    def __init__(self):
        self.ops = []
        self.st = {}
        self.seqc = {}
        self.dry = False

    def add(self, eng, meth, kw, R=(), W=(), group=None):
        if self.dry:
            return -1
        R = _flat(list(R))
        W = _flat(list(W))
        idx = len(self.ops)
        stream = group if group is not None else eng
        deps = set()
        for k in R:
            e = self.st.get(k)
            if e is not None and e[0] is not None:
                deps.add(e[0])
        for k in W:
            e = self.st.get(k)
            if e is not None:
                if e[0] is not None:
                    deps.add(e[0])
                deps.update(e[1].values())
        seq = self.seqc.get(stream, 0) + 1
        self.seqc[stream] = seq
        self.ops.append(dict(eng=eng, meth=meth, kw=kw, deps=deps, stream=stream, seq=seq,
                             dma=group is not None, ms=False))
        for k in R:
            e = self.st.get(k)
            if e is None:
                e = [None, {}]
                self.st[k] = e
            e[1][stream] = idx
        for k in W:
            self.st[k] = [idx, {}]
        return idx

    def finalize(self):
        ops = self.ops
        hasdep = set()
        for op in ops:
            hasdep.update(op['deps'])
        know = {e: {} for e in self.ENG}
        snap = {}
        for i, op in enumerate(ops):
            E = op['eng']
            kn = know[E]
            waits = []
            for j in sorted(op['deps'], reverse=True):
                d = ops[j]
                if d['stream'] == 'pe' and E == 'pe':
                    continue
                if kn.get(d['stream'], 0) >= d['seq']:
                    continue
                waits.append(j)
                d['ms'] = True
                for s2, sq in snap[j].items():
                    if kn.get(s2, 0) < sq:
                        kn[s2] = sq
            op['waits'] = waits
            if i in hasdep:
                sn = dict(kn)
                if sn.get(op['stream'], 0) < op['seq']:
                    sn[op['stream']] = op['seq']
                snap[i] = sn
        cnt = {}
        for op in ops:
            if op['dma']:
                op['ms'] = True
            if op['ms']:
                c = cnt.get(op['stream'], 0) + (16 if op['dma'] else 1)
                cnt[op['stream']] = c
                op['cnt'] = c
        return sorted(cnt.keys())

    def emit(self, nc, block, sems):
        per = {e: [] for e in self.ENG}
        for op in self.ops:
            per[op['eng']].append(op)
        ops = self.ops

        def mk(E):
            def f(e):
                for op in per[E]:
                    for j in op['waits']:
                        d = ops[j]
                        e.wait_ge(sems[d['stream']], d['cnt'])
                    if op['meth'] is None:
                        continue
                    ins = getattr(e, op['meth'])(**op['kw'])
                    if op['ms']:
                        ins.then_inc(sems[op['stream']], 16 if op['dma'] else 1)
            return f
        block.tensor(mk('pe'))
        block.scalar(mk('act'))
        block.vector(mk('dve'))
        block.gpsimd(mk('pool'))
        block.sync(mk('sp'))


def build(L, mode, ntok=2048):
    nc = bass.Bass("TRN2", target_bir_lowering=False)
    NTOK = ntok
    NTILE = ntok // TT
    NCH = ntok // 128
    seq = mode == 'seq'
    P = Prog()
    es = ExitStack()

    def din(name, shape, dt):
        return nc.dram_tensor(name, shape, dt, kind="ExternalInput").ap()

    xT = din("xT", [DM, NTOK], F32)
    posT = din("posT", [128, NCH], I32)
    cstD = din("cst", [128, NCST], F32)
    spD = din("sp", [L, 128, NSP], F32)
    pwD = din("pw", [L, 128, 512], F32)
    w_in = din("w_in", [L, DM, NIN], F32)
    if mode != 'pre':
        w_pp = din("w_pp", [L, 512, DM], F32)
        w_cp = din("w_cp", [L, 512, DM], F32)
        w_rp = din("w_rp", [L, 1024, DM], F32)
        w_out = din("w_out", [L, DM, DM], F32)
        w_fi = din("w_fi", [L, DM, 2 * FH], F32)
        w_fo = din("w_fo", [L, FH, DM], F32)
    if mode == 'full':
        xinD = din("xin", [8 * 128, XW], F32)
    if mode == 'pre':
        xoutD = nc.dram_tensor("xout", [128, XW], F32, kind="ExternalOutput").ap()
    else:
        yT = nc.dram_tensor("yT", [DM, NTOK], F32, kind="ExternalOutput").ap()
    if DEBUG and mode == 'full':
        dbgA = nc.dram_tensor("dbgA", [128, 16 * 512], BF, kind="ExternalOutput").ap()
        dbgBR = nc.dram_tensor("dbgBR", [128, 16 * 512], BF, kind="ExternalOutput").ap()
        dbgM = nc.dram_tensor("dbgM", [128, 16 * 512], BF, kind="ExternalOutput").ap()
        dbgY = nc.dram_tensor("dbgY", [128, 16 * 512], F32, kind="ExternalOutput").ap()
        dbgQ = nc.dram_tensor("dbgQ", [128, 4 * 1024], BF, kind="ExternalOutput").ap()
        dbgK = nc.dram_tensor("dbgK", [128, 4 * 1024], BF, kind="ExternalOutput").ap()
        dbgCC = nc.dram_tensor("dbgCC", [128, 2048], F32, kind="ExternalOutput").ap()
        dbgSS = nc.dram_tensor("dbgSS", [128, 2048], F32, kind="ExternalOutput").ap()
        dbgHB = nc.dram_tensor("dbgHB", [128, 4 * 544], F32, kind="ExternalOutput").ap()
        dbgACC = nc.dram_tensor("dbgACC", [128, 4 * 512], F32, kind="ExternalOutput").ap()
        dbgSG = nc.dram_tensor("dbgSG", [128, 4 * 512], F32, kind="ExternalOutput").ap()
    ktmD = nc.dram_tensor("ktm_s", [NTOK, 1024], BF).ap()
    vtmD = nc.dram_tensor("vtm_s", [NTOK, 1024], BF).ap()
    if seq:
        xbuf = [nc.dram_tensor("xb%d" % i, [DM, NTOK], F32).ap() for i in range(2)]
        ccD = nc.dram_tensor("cc_s", [128, NCH * 128], F32).ap()
        ssD = nc.dram_tensor("ss_s", [128, NCH * 128], F32).ap()
    if mode == 'fused':
        xbuf = [nc.dram_tensor("xb%d" % i, [DM, NTOK], F32).ap() for i in range(2)]
        xchD = nc.dram_tensor("xch_s", [128, XW], F32).ap()
        xgD = nc.dram_tensor("xg_s", [8 * 128, XW], F32).ap()

    def sb(name, shape, dt):
        return es.enter_context(nc.sbuf_tensor(name, shape, dt))

    WR = [sb("wr%d" % i, [128, 4096], BF) for i in range(NSLOT)]
    ARB = 112 * 1024
    AR = sb("arena", [128, ARB // 2], BF)
    CST = sb("cstt", [128, NCST], F32)
    SPR = sb("spr", [128, NSP], F32)
    PWB = sb("pwb", [128, 512], BF)
    CC = sb("cc", [128, 4 if seq else 16, 128], F32)
    SS = sb("ss", [128, 4 if seq else 16, 128], F32)
    IDN = sb("idn", [128, 128], BF)
    ON_D = sb("ond", [128, 128], BF)
    ON_5 = sb("on5", [128, 128], BF)
    ON_1 = sb("on1", [128, 128], BF)
    ST = sb("stt", [128, 1024], F32)
    UPT = sb("upt", [128, 4, 16], F32)
    HBT = sb("hbt", [128, 4, 32], F32)
    POSI = sb("posi", [128, NCH], I32)
    RS = [sb("rs%d" % i, [128, 512], F32) for i in range(2)]
    FT = [sb("ft%d" % i, [128, 512], F32) for i in range(4)]
    BT = [sb("bt%d" % i, [128, 512], BF) for i in range(3)]
    SMTB = [sb("smt%d" % i, [128, 512], BF) for i in range(2)]
    DGB = [sb("dg%d" % i, [128, 128], BF) for i in range(4)]
    CEN = sb("cen", [128, 128], BF)
    PSB = [es.enter_context(nc.psum_tensor("ps%d" % i, [128, 512], F32)) for i in range(8)]

    def sv(t, name):
        return V(t[:], [(name,)])

    vCST = sv(CST, "cst"); vSPR = sv(SPR, "spr"); vPWB = sv(PWB, "pwb")
    vCC = sv(CC, "cc"); vSS = sv(SS, "ss"); vIDN = sv(IDN, "idn")
    vOND = sv(ON_D, "ond"); vON5 = sv(ON_5, "on5"); vON1 = sv(ON_1, "on1")
    vST = [V(ST[:, i * 512:(i + 1) * 512], [("st", i)]) for i in range(2)]
    vUPT = sv(UPT, "upt"); vHBT = sv(HBT, "hbt"); vPOSI = sv(POSI, "posi")
    vRS = [sv(RS[i], "rs%d" % i) for i in range(2)]
    vFT = [sv(FT[i], "ft%d" % i) for i in range(4)]
    vBT = [sv(BT[i], "bt%d" % i) for i in range(3)]
    vSMT = [sv(SMTB[i], "smt%d" % i) for i in range(2)]
    vDG = [sv(DGB[i], "dg%d" % i) for i in range(4)]
    vCEN = sv(CEN, "cen")
    PS = [V(PSB[i][:], [("ps", i)]) for i in range(8)]
    rot = {}

    def nxt(name, lst):
        i = rot.get(name, 0)
        rot[name] = i + 1
        return lst[i % len(lst)]

    def bank():
        return nxt("bank", PS[0:7])
    PSTAT = PS[7]

    def arv(off, shape, dt):
        n = 1
        for s in shape[1:]:
            n *= s
        nb = n * (2 if dt == BF else 4)
        ap = AR[:, off // 2: off // 2 + nb // 2]
        if dt != BF:
            ap = ap.bitcast(dt)
        if len(shape) == 3:
            ap = ap.rearrange("p (a b) -> p a b", a=shape[1])
        keys = [("AR", g) for g in range(off // 1024, (off + nb + 1023) // 1024)]
        return V(ap, keys)

    KB = 1024
    A = [arv(kc * KB, [128, 512], BF) for kc in range(16)]
    A_all = arv(0, [128, 16, 512], BF)
    BR = [arv(16 * KB + i * KB, [128, 512], BF) for i in range(16)]
    M = [arv(32 * KB + i * KB, [128, 512], BF) for i in range(16)]
    HID = [arv(32 * KB + j * KB, [128, 512], BF) for j in range(44)]
    Y = [arv(80 * KB + i * 2 * KB, [128, 512], F32) for i in range(16)]
    Y_all = arv(80 * KB, [128, 16, 512], F32)
    Y2 = [arv(i * 2 * KB, [128, 512], F32) for i in range(16)]
    QTM = [arv(48 * KB + n * 2 * KB, [128, 1024], BF) for n in range(4)]
    KTM = [arv(56 * KB + n * 2 * KB, [128, 1024], BF) for n in range(4)]
    VTM = [arv(64 * KB + n * 2 * KB, [128, 1024], BF) for n in range(4)]
    KTM_all = arv(56 * KB, [128, 4, 1024], BF)
    VTM_all = arv(64 * KB, [128, 4, 1024], BF)
    QT = [arv(72 * KB + h * KB, [128, 512], BF) for h in range(8)]
    KT = [arv(80 * KB + h * KB, [128, 512], BF) for h in range(8)]
    GS = [arv(88 * KB + h * KB, [128, 512], BF) for h in range(8)]
    SBS = [[arv(96 * KB + n * 2 * KB + hf * KB, [128, 512], BF) for hf in range(2)] for n in range(5)]
    UP = arv(48 * KB, [128, 4, 528], F32)
    SA = arv(48 * KB + 8448, [128, 4, 528], F32)
    SBF = arv(48 * KB + 2 * 8448, [128, 4, 528], F32)
    PP = [arv(48 * KB + 3 * 8448 + g * KB, [128, 512], BF) for g in range(4)]
    HB = arv(48 * KB, [128, 4, 544], BF)
    SG = arv(48 * KB + 8704, [128, 4, 512], F32)
    ACC = arv(48 * KB + 8704 + 8192, [128, 4, 512], F32)
    XT_ = [arv(80 * KB + i * 5 * KB, [128, XW], F32) for i in range(2)]
    vXC = [arv(i * 2 * KB, [128, 512], F32) for i in range(2)]

    class WStream:
        def __init__(self):
            self.reqs = []
            self.pos = 0
            self.issued = 0

        @staticmethod
        def _views(slot, parts):
            vs = []
            off = 0
            for (wt, l, r0, nk, c0, cols) in parts:
                vs.append(WR[slot][:, off:off + nk * cols].rearrange("p (k c) -> p k c", k=nk))
                off += nk * cols
            return vs

        @staticmethod
        def _keys(slot, pi, np_):
            if np_ == 1:
                return [("wr", slot, 0), ("wr", slot, 1), ("wr", slot, 2)]
            return [("wr", slot, pi)]

        def _issue(self, j):
            parts = self.reqs[j]
            slot = j % NSLOT
            vs = self._views(slot, parts)
            np_ = len(parts)
            for pi, (view, (wt, l, r0, nk, c0, cols)) in enumerate(zip(vs, parts)):
                src = wt[l, r0:r0 + nk * 128, c0:c0 + cols].rearrange("(k p) c -> p k c", p=128)
                P.add('pool', 'dma_start', dict(out=view, in_=src), R=[], W=self._keys(slot, pi, np_),
                      group="w%d_%d" % (slot, pi))

        def getm(self, parts, back=1):
            if P.dry:
                self.reqs.append(tuple(parts))
                return [V(v, [("wr", 0, 0)]) for v in self._views(0, parts)]
            i = self.pos
            self.pos += 1
            while self.issued <= min(len(self.reqs) - 1, i - back + NSLOT - 1):
                self._issue(self.issued)
                self.issued += 1
            slot = i % NSLOT
            return [V(v, self._keys(slot, pi, len(parts))) for pi, v in enumerate(self._views(slot, parts))]

        def get(self, wt, l, r0, nk, c0, cols, back=1):
            return self.getm([(wt, l, r0, nk, c0, cols)], back)[0]
    WS = WStream()

    def mm(out, lhsT, rhs, start, stop, R, W):
        P.add('pe', 'matmul', dict(out=out, lhsT=lhsT, rhs=rhs, start=start, stop=stop), R=R, W=W)

    def dve(meth, R, W, **kw):
        P.add('dve', meth, kw, R=R, W=W)

    def act(out, in_, func, R, W, **kw):
        P.add('act', 'activation', dict(out=out, in_=in_, func=func, **kw), R=R, W=W)

    def dma(q, out, in_, R, W, group):
        return P.add(q, 'dma_start', dict(out=out, in_=in_), R=R, W=W, group=group)

    def b3(ap2, n):
        return ap2.rearrange("p (a b) -> p a b", a=n)

    def rstd_of(src, eps_col):
        r = nxt("rs", vRS)
        act(r.ap, src.ap, AF.Sqrt, [src, vCST], [r], bias=CST[:, eps_col:eps_col + 1], scale=1.0)
        dve('reciprocal', [r], [r], out=r.ap, in_=r.ap)
        return r

    def norm(X, gcol0, Aout):
        pb = bank()
        for kc in range(16):
            sq = nxt("bt", vBT)
            act(sq.ap, X[kc].ap, AF.Square, [X[kc]], [sq])
            mm(pb.ap, ON_D[:], sq.ap, kc == 0, kc == 15, [sq, vOND], [pb])
        r = rstd_of(pb, 1562)
        for kc in range(16):
            dve('scalar_tensor_tensor', [X[kc], r, vSPR], [Aout[kc]], out=Aout[kc].ap, in0=X[kc].ap,
                scalar=SPR[:, gcol0 + kc:gcol0 + kc + 1], in1=r.ap, op0=ALU.mult, op1=ALU.mult)

    def rotary(src, gn, dst_ap, dstv):
        s3 = b3(src.ap, 4)
        ta = nxt("ft", vFT)
        tb = nxt("ft", vFT)
        dve('tensor_tensor', [src, vCC], [ta], out=b3(ta.ap, 4), in0=s3,
            in1=CC[:, gn, :].unsqueeze(1).to_broadcast([128, 4, 128]), op=ALU.mult)
        dve('tensor_tensor', [src, vSS], [tb], out=b3(tb.ap, 4)[:, :, 0:64], in0=s3[:, :, 64:128],
            in1=SS[:, gn, 0:64].unsqueeze(1).to_broadcast([128, 4, 64]), op=ALU.mult)
        dve('tensor_tensor', [src, vSS, tb], [tb], out=b3(tb.ap, 4)[:, :, 64:128], in0=s3[:, :, 0:64],
            in1=SS[:, gn, 64:128].unsqueeze(1).to_broadcast([128, 4, 64]), op=ALU.mult)
        dve('tensor_tensor', [ta, tb], [dstv], out=dst_ap, in0=ta.ap, in1=tb.ap, op=ALU.add)

    def tok_proj(l, coff, cg, n, Aall):
        pb = bank()
        w0, w1 = tok_proj.w
        for kc in range(16):
            w = w0 if kc < 8 else w1
            mm(pb.ap, A[kc].ap[:, n * 128:(n + 1) * 128], w.ap[:, kc % 8, :], kc == 0, kc == 15,
               [A[kc], w], [pb])
        return pb

    def load_x_tile(xsrc, xname, tt):
        t0 = tt * TT
        for kc in range(16):
            dma('sp', Y[kc].ap, xsrc[kc * 128:(kc + 1) * 128, t0:t0 + TT], [(xname, tt, kc)], [Y[kc]], "ldx%d" % kc)

    def kv_update(n, write_sb):
        for hf in range(2):
            pb = bank()
            for hq in range(4):
                h = hf * 4 + hq
                mm(pb.ap[:, hq * 128:(hq + 1) * 128], KTM[n].ap[:, h * 128:(h + 1) * 128],
                   VTM[n].ap[:, h * 128:(h + 1) * 128], True, True, [KTM[n], VTM[n]], [pb])
            s = vST[hf]
            dve('tensor_tensor', [s, pb], [s], out=s.ap, in0=s.ap, in1=pb.ap, op=ALU.add)
            dve('tensor_tensor', [s, vCST], [s], out=s.ap, in0=s.ap,
                in1=CST[:, 144 + hf * 512:144 + (hf + 1) * 512], op=ALU.mult)
            if write_sb:
                act(SBS[n + 1][hf].ap, s.ap, AF.Copy, [s], [SBS[n + 1][hf]])

    def pre_kv_tile(l, tt, cbase):
        for (coff, isk) in ((O_K, True), (O_V, False)):
            for cg in range(2):
                w0 = WS.get(w_in, l, 0, 8, coff + cg * 512, 512)
                w1 = WS.get(w_in, l, 1024, 8, coff + cg * 512, 512)
                tok_proj.w = (w0, w1)
                for n in range(4):
                    pb = tok_proj(l, coff, cg, n, None)
                    if isk:
                        kd = nxt("ft", vFT)
                        dve('tensor_tensor', [pb, vCST], [kd], out=b3(kd.ap, 4), in0=b3(pb.ap, 4),
                            in1=CST[:, 136 + cg * 4:136 + cg * 4 + 4].unsqueeze(2).to_broadcast([128, 4, 128]),
                            op=ALU.mult)
                        rotary(kd, cbase + n, KTM[n].ap[:, cg * 512:(cg + 1) * 512], KTM[n])
                    else:
                        act(VTM[n].ap[:, cg * 512:(cg + 1) * 512], pb.ap, AF.Copy, [pb], [VTM[n]])

    def u_proj_conv_pool(l, tt, tails_only):
        dve('tensor_copy', [vUPT], [UP], out=UP.ap[:, :, 1:16], in_=UPT[:, :, 1:16])
        for half in range(2):
            w = WS.get(w_in, l, 0, 16, O_POOL + half * 256, 256)
            for gi in range(2):
                g = half * 2 + gi
                pb = bank()
                for kc in range(16):
                    mm(pb.ap, w.ap[:, kc, gi * 128:(gi + 1) * 128], A[kc].ap, kc == 0, kc == 15, [A[kc], w], [pb])
                act(UP.ap[:, g, 16:528], pb.ap, AF.Copy, [pb], [UP])
        dve('tensor_copy', [UP], [vUPT], out=UPT[:, :, 1:16], in_=UP.ap[:, :, 513:528])
        if not tails_only:
            wins = (2, 4, 8, 16)
            src = UP
            bufs = [SA, SBF]
            for g in range(4):
                sh = wins[g] // 2
                dstb = bufs[g % 2]
                dve('tensor_tensor', [src], [dstb], out=dstb.ap[:, g:4, 2 * sh:528], in0=src.ap[:, g:4, 2 * sh:528],
                    in1=src.ap[:, g:4, sh:528 - sh], op=ALU.add)
                dve('scalar_tensor_tensor', [dstb, UP], [PP[g]], out=PP[g].ap, in0=dstb.ap[:, g, 16:528],
                    scalar=1.0 / wins[g], in1=UP.ap[:, g, 16:528], op0=ALU.mult, op1=ALU.subtract)
                if tt == 0:
                    t1 = nxt("ft", vFT)
                    dve('tensor_tensor', [dstb, vCST], [t1], out=t1.ap[:, 0:16], in0=dstb.ap[:, g, 16:32],
                        in1=CST[:, 1240 + g * 16:1240 + (g + 1) * 16], op=ALU.mult)
                    dve('tensor_tensor', [t1, UP, PP[g]], [PP[g]], out=PP[g].ap[:, 0:16], in0=t1.ap[:, 0:16],
                        in1=UP.ap[:, g, 16:32], op=ALU.subtract)
                src = dstb
        for half in range(2):
            w = WS.get(w_in, l, 0, 16, O_CG + half * 256, 256)
            for gi in range(2):
                j = half * 2 + gi
                pb = bank()
                for kc in range(16):
                    mm(pb.ap, w.ap[:, kc, gi * 128:(gi + 1) * 128], A[kc].ap, kc == 0, kc == 15, [A[kc], w], [pb])
                act(SG.ap[:, j, :], pb.ap, AF.Sigmoid, [pb], [SG])
        dve('tensor_copy', [vHBT], [HB], out=HB.ap[:, :, 2:32], in_=HBT[:, :, 2:32])
        for half in range(2):
            w = WS.get(w_in, l, 0, 16, O_CA + half * 256, 256)
            for gi in range(2):
                j = half * 2 + gi
                pb = bank()
                for kc in range(16):
                    mm(pb.ap, w.ap[:, kc, gi * 128:(gi + 1) * 128], A[kc].ap, kc == 0, kc == 15, [A[kc], w], [pb])
                dve('tensor_tensor', [pb, SG], [HB], out=HB.ap[:, j, 32:544], in0=pb.ap, in1=SG.ap[:, j, :], op=ALU.mult)
        dve('tensor_copy', [HB], [vHBT], out=HBT[:, :, 2:32], in_=HB.ap[:, :, 514:544])
        if tails_only:
            return
        for g in range(4):
            pb = bank()
            mm(pb.ap, PWB[:, g * 128:(g + 1) * 128], PP[g].ap, True, True, [PP[g], vPWB], [pb])
            dve('tensor_scalar', [pb, vSPR], [BR[g]], out=BR[g].ap, in0=pb.ap, scalar1=SPR[:, 64 + g:65 + g],
                scalar2=None, op0=ALU.mult)
        for j in range(4):
            pc = bank()
            for t in range(31):
                dg = nxt("dg", vDG)
                dve('tensor_scalar', [vIDN, vSPR], [dg], out=dg.ap, in0=IDN[:], scalar1=SPR[:, 88 + j * 31 + t:89 + j * 31 + t],
                    scalar2=None, op0=ALU.mult)
                mm(pc.ap, dg.ap, HB.ap[:, j, 2 + t:514 + t], t == 0, t == 30, [dg, HB], [pc])
            act(ACC.ap[:, j, :], pc.ap, AF.Identity, [pc, vSPR], [ACC], bias=SPR[:, 68 + j:69 + j], scale=1.0)
        if DEBUG and mode == 'full' and tt == 0:
            dma('sp', dbgHB, HB.ap.rearrange("p a b -> p (a b)"), [HB], [("dbgHB",)], "dbgHB")
            dma('sp', dbgACC, ACC.ap.rearrange("p a b -> p (a b)"), [ACC], [("dbgACC",)], "dbgACC")
            dma('sp', dbgSG, SG.ap.rearrange("p a b -> p (a b)"), [SG], [("dbgSG",)], "dbgSG")
        pm = bank()
        pq = bank()
        c16s = [arv(48 * KB + 8704 + j * KB, [128, 512], BF) for j in range(8)]
        for j in range(4):
            act(c16s[j].ap, ACC.ap[:, j, :], AF.Copy, [ACC], [c16s[j]])
            act(c16s[4 + j].ap, ACC.ap[:, j, :], AF.Square, [ACC], [c16s[4 + j]])
        for j in range(4):
            mm(pm.ap, ON_5[:], c16s[j].ap, j == 0, j == 3, [c16s[j], vON5], [pm])
        for j in range(4):
            mm(pq.ap, ON_5[:], c16s[4 + j].ap, j == 0, j == 3, [c16s[4 + j], vON5], [pq])
        ms = nxt("ft", vFT)
        act(ms.ap, pm.ap, AF.Copy, [pm], [ms])
        m2 = nxt("ft", vFT)
        dve('tensor_tensor', [ms], [m2], out=m2.ap, in0=ms.ap, in1=ms.ap, op=ALU.mult)
        dve('tensor_tensor', [pq, m2], [m2], out=m2.ap, in0=pq.ap, in1=m2.ap, op=ALU.subtract)
        dve('tensor_scalar', [m2], [m2], out=m2.ap, in0=m2.ap, scalar1=0.0, scalar2=None, op0=ALU.max)
        r = rstd_of(m2, 1563)
        tpair = [nxt("ft", vFT), nxt("ft", vFT)]
        for j in range(4):
            t = tpair[j % 2]
            dve('tensor_tensor', [ACC, ms], [t], out=t.ap, in0=ACC.ap[:, j, :], in1=ms.ap, op=ALU.subtract)
            dve('tensor_tensor', [t, r], [t], out=t.ap, in0=t.ap, in1=r.ap, op=ALU.mult)
            act(BR[4 + j].ap, t.ap, AF.Silu, [t, vSPR], [BR[4 + j]], scale=SPR[:, 72 + j:73 + j],
                bias=SPR[:, 76 + j:77 + j])

    def post_norm_residual(Yo, gcol0, xget, tt, after=None):
        flush_stats()
        r = rstd_of(PSTAT, 1562)
        for d4 in range(0, 16, 4):
            ts_ = []
            for dc in range(d4, d4 + 4):
                t = vFT[dc % 4]
                dve('scalar_tensor_tensor', [Yo[dc], r, vSPR], [t], out=t.ap, in0=Yo[dc].ap,
                    scalar=SPR[:, gcol0 + dc:gcol0 + dc + 1], in1=r.ap, op0=ALU.mult, op1=ALU.mult)
                ts_.append(t)
            for dc in range(d4, d4 + 4):
                xv = xget(dc)
                t = ts_[dc - d4]
                dve('tensor_tensor', [t, xv], [Y[dc]], out=Y[dc].ap, in0=t.ap, in1=xv.ap, op=ALU.add)
                if after is not None:
                    after(dc)

    pend_stats = []

    def flush_stats(keep=0):
        while len(pend_stats) > keep:
            sq, dc = pend_stats.pop(0)
            mm(PSTAT.ap, ON_D[:], sq.ap, dc == 0, dc == 15, [sq, vOND], [PSTAT])

    def evac_stats(pb, Yo, dc):
        flush_stats(1)
        act(Yo[dc].ap, pb.ap, AF.Copy, [pb], [Yo[dc]])
        sq = nxt("bt", vBT)
        act(sq.ap, pb.ap, AF.Square, [pb], [sq])
        pend_stats.append((sq, dc))

    def body():
        rot.clear()
        out_dmas = []
        try:
            body_main(out_dmas)
        except _Stop:
            pass
        body_fin(out_dmas)

    def body_main(out_dmas):
        dma('sp', CST[:], cstD[:, :], [], [vCST], "ldc")
        dma('sp', POSI[:], posT[:, :], [], [vPOSI], "ldp")
        dve('tensor_copy', [vCST], [vIDN], out=IDN[:], in_=CST[:, 1434:1562])
        dve('tensor_scalar', [vIDN], [vCEN], out=CEN[:], in0=IDN[:], scalar1=-1.0 / 128, scalar2=None, op0=ALU.add)
        dve('memset', [], [vOND], ap=ON_D[:], constant=1.0 / 2048)
        dve('memset', [], [vON5], ap=ON_5[:], constant=1.0 / 512)
        dve('memset', [], [vON1], ap=ON_1[:], constant=1.0 / 128)
        posf = vFT[0]
        dve('tensor_copy', [vPOSI], [posf], out=posf.ap[:, 0:NCH], in_=POSI[:])
        T1 = arv(0, [128, 16, 128], F32)
        T2 = arv(8 * KB, [128, 16, 128], F32)
        T3 = arv(16 * KB, [128, 16, 128], F32)
        TIv = arv(24 * KB, [128, 16, 128], F32)
        TI = V(TIv.ap.bitcast(I32), TIv.k)
        T4 = arv(32 * KB, [128, 16, 128], F32)
        T5 = arv(40 * KB, [128, 16, 128], F32)
        for piece in range(NCH // 16):
            if seq:
                cdst, sdst, cv, sv_ = T4.ap, T5.ap, T4, T5
            else:
                cdst, sdst, cv, sv_ = CC[:], SS[:], vCC, vSS
            dve('tensor_tensor', [posf, vCST], [T1], out=T1.ap,
                in0=posf.ap[:, piece * 16:(piece + 1) * 16].unsqueeze(2).to_broadcast([128, 16, 128]),
                in1=CST[:, 0:128].unsqueeze(1).to_broadcast([128, 16, 128]), op=ALU.mult)
            dve('tensor_scalar', [T1], [T2], out=T2.ap, in0=T1.ap, scalar1=1.0 / (2 * PI), scalar2=None, op0=ALU.mult)
            dve('tensor_copy', [T2], [TI], out=TI.ap, in_=T2.ap)
            dve('tensor_copy', [TI], [T2], out=T2.ap, in_=TI.ap)
            dve('scalar_tensor_tensor', [T2, T1], [T1], out=T1.ap, in0=T2.ap, scalar=-2 * PI, in1=T1.ap, op0=ALU.mult, op1=ALU.add)
            act(T2.ap, T1.ap, AF.Sin, [T1], [T2], scale=0.5)
            act(T3.ap, T1.ap, AF.Sin, [T1], [T3], scale=0.25)
            dve('tensor_tensor', [T2], [cv], out=cdst, in0=T2.ap, in1=T2.ap, op=ALU.mult)
            dve('tensor_scalar', [cv], [cv], out=cdst, in0=cdst, scalar1=-2.0, scalar2=1.0, op0=ALU.mult, op1=ALU.add)
            dve('tensor_tensor', [T3], [T3], out=T3.ap, in0=T3.ap, in1=T3.ap, op=ALU.mult)
            dve('tensor_scalar', [T3], [T3], out=T3.ap, in0=T3.ap, scalar1=-2.0, scalar2=1.0, op0=ALU.mult, op1=ALU.add)
            dve('scalar_tensor_tensor', [T2, T3], [sv_], out=sdst[:, :, 64:128], in0=T2.ap[:, :, 64:128], scalar=2.0,
                in1=T3.ap[:, :, 64:128], op0=ALU.mult, op1=ALU.mult)
            dve('scalar_tensor_tensor', [T2, T3, sv_], [sv_], out=sdst[:, :, 0:64], in0=T2.ap[:, :, 0:64], scalar=-2.0,
                in1=T3.ap[:, :, 0:64], op0=ALU.mult, op1=ALU.mult)
            if seq:
                dma('sp', ccD[:, piece * 2048:(piece + 1) * 2048], T4.ap.rearrange("p a b -> p (a b)"), [T4], [("ccD", piece)], "stcc")
                dma('sp', ssD[:, piece * 2048:(piece + 1) * 2048], T5.ap.rearrange("p a b -> p (a b)"), [T5], [("ssD", piece)], "stss")

        if DEBUG and mode == 'full':
            dma('sp', dbgCC, CC[:].rearrange("p a b -> p (a b)"), [vCC], [("dbgCC",)], "dbgCC")
            dma('sp', dbgSS, SS[:].rearrange("p a b -> p (a b)"), [vSS], [("dbgSS",)], "dbgSS")
        for l in range(L) if True else []:
            if mode == 'fused' or seq:
                xsrc = xT if l == 0 else xbuf[(l - 1) % 2]
                xdst = yT if l == L - 1 else xbuf[l % 2]
                xsn = "xT" if l == 0 else "xb%d" % ((l - 1) % 2)
                xdn = "yT" if l == L - 1 else "xb%d" % (l % 2)
            else:
                xsrc = xT
                xdst = None if mode == 'pre' else yT
                xsn, xdn = "xT", "yT"
            dma('sp', SPR[:], spD[l], [], [vSPR], "ldsp")
            dma('pool', PWB[:], pwD[l], [], [vPWB], "ldpw")
            dve('memset', [], [vST[0]], ap=ST[:, 0:512], constant=0.0)
            dve('memset', [], [vST[1]], ap=ST[:, 512:1024], constant=0.0)
            dve('memset', [], [vUPT], ap=UPT[:], constant=0.0)
            dve('memset', [], [vHBT], ap=HBT[:], constant=0.0)
            for tt in range(0 if seq else NTILE):
                t0 = tt * TT
                load_x_tile(xsrc, xsn, tt)
                norm(Y, 0, A)
                pre_kv_tile(l, tt, tt * 4)
                dma('sp', ktmD[t0:t0 + TT, :].rearrange("(n p) c -> p n c", p=128), KTM_all.ap, [KTM_all], [("ktmD", tt)], "stk")
                dma('sp', vtmD[t0:t0 + TT, :].rearrange("(n p) c -> p n c", p=128), VTM_all.ap, [VTM_all], [("vtmD", tt)], "stv")
                for n in range(4):
                    kv_update(n, False)
                if tt == NTILE - 1:
                    u_proj_conv_pool(l, tt, True)
            if mode == 'pre':
                xo = xoutD
            elif mode == 'fused':
                xo = xchD
            if mode in ('pre', 'fused'):
                i1 = dma('sp', xo[:, 0:512], ST[:, 0:512], [vST[0]], [("xo", 0)], "sx0")
                i2 = dma('sp', xo[:, 512:1024], ST[:, 512:1024], [vST[1]], [("xo", 1)], "sx1")
                i3 = dma('sp', xo[:, 1024:1152], HBT[:].rearrange("p a b -> p (a b)"), [vHBT], [("xo", 2)], "sx2")
                i4 = dma('sp', xo[:, 1152:1216], UPT[:].rearrange("p a b -> p (a b)"), [vUPT], [("xo", 3)], "sx3")
                out_dmas += [i1, i2, i3, i4]
            if mode == 'pre':
                continue
            if mode == 'fused':
                P.add('pool', 'collective_compute',
                      dict(kind="AllGather", op=ALU.bypass, replica_groups=[list(range(8))],
                           ins=[xchD[:, :]], outs=[xgD[:, :]]),
                      R=[("xo", 0), ("xo", 1), ("xo", 2), ("xo", 3)], W=[("xg",)], group="cc")
                xin_src = xgD
            elif not seq:
                xin_src = xinD
            if not seq:
                dve('memset', [vST[0]], [vST[0]], ap=ST[:, 0:512], constant=0.0)
                dve('memset', [vST[1]], [vST[1]], ap=ST[:, 512:1024], constant=0.0)
                dve('memset', [vUPT], [vUPT], ap=UPT[:], constant=0.0)
                dve('memset', [vHBT], [vHBT], ap=HBT[:], constant=0.0)
            for r in range(0 if seq else 8):
                xt = XT_[r % 2]
                dma('sp', xt.ap, xin_src[r * 128:(r + 1) * 128, :], [("xg",)], [xt], "ldxg%d" % (r % 2))
                for h in range(8):
                    s = vST[h // 4]
                    dve('scalar_tensor_tensor', [xt, s, vCST], [s], out=ST[:, h * 128:(h + 1) * 128],
                        in0=xt.ap[:, h * 128:(h + 1) * 128], scalar=CST[:, 1168 + r * 8 + h:1169 + r * 8 + h],
                        in1=ST[:, h * 128:(h + 1) * 128], op0=ALU.mult, op1=ALU.add)
                hb2 = HBT[:].rearrange("p a b -> p (a b)")
                dve('scalar_tensor_tensor', [xt, vHBT, vCST], [vHBT], out=hb2, in0=xt.ap[:, 1024:1152],
                    scalar=CST[:, 1232 + r:1233 + r], in1=hb2, op0=ALU.mult, op1=ALU.add)
                up2 = UPT[:].rearrange("p a b -> p (a b)")
                dve('scalar_tensor_tensor', [xt, vUPT, vCST], [vUPT], out=up2, in0=xt.ap[:, 1152:1216],
                    scalar=CST[:, 1232 + r:1233 + r], in1=up2, op0=ALU.mult, op1=ALU.add)
            for tt in range(NTILE):
                t0 = tt * TT
                if seq:
                    dma('sp', CC[:], ccD[:, tt * 512:(tt + 1) * 512].rearrange("p (a b) -> p a b", a=4), [("ccD", tt // 4)], [vCC], "ldcc")
                    dma('sp', SS[:], ssD[:, tt * 512:(tt + 1) * 512].rearrange("p (a b) -> p a b", a=4), [("ssD", tt // 4)], [vSS], "ldss")
                load_x_tile(xsrc, xsn, tt)
                norm(Y, 0, A)
                if DEBUG and tt == 0:
                    dma('sp', dbgA, AR[:, 0:8192], [A_all], [("dbgA",)], "dbgA")
                u_proj_conv_pool(l, tt, False)
                if STOP == 'proj' and tt == 0:
                    raise _Stop()
                for cg in range(2):
                    w0 = WS.get(w_in, l, 0, 8, O_Q + cg * 512, 512)
                    w1 = WS.get(w_in, l, 1024, 8, O_Q + cg * 512, 512)
                    tok_proj.w = (w0, w1)
                    for n in range(4):
                        pb = tok_proj(l, O_Q, cg, n, None)
                        qd = nxt("ft", vFT)
                        dve('tensor_tensor', [pb, vCST], [qd], out=b3(qd.ap, 4), in0=b3(pb.ap, 4),
                            in1=CST[:, 128 + cg * 4:128 + cg * 4 + 4].unsqueeze(2).to_broadcast([128, 4, 128]),
                            op=ALU.mult)
                        rotary(qd, (0 if seq else tt * 4) + n, QTM[n].ap[:, cg * 512:(cg + 1) * 512], QTM[n])
                if seq:
                    pre_kv_tile(l, tt, 0)
                else:
                    dma('sp', KTM_all.ap, ktmD[t0:t0 + TT, :].rearrange("(n p) c -> p n c", p=128), [("ktmD", tt)], [KTM_all], "ldk")
                    dma('sp', VTM_all.ap, vtmD[t0:t0 + TT, :].rearrange("(n p) c -> p n c", p=128), [("vtmD", tt)], [VTM_all], "ldv")
                if DEBUG and tt == 0:
                    dma('sp', dbgQ, AR[:, 24 * KB:28 * KB], [QTM], [("dbgQ",)], "dbgQ")
                    dma('sp', dbgK, AR[:, 28 * KB:32 * KB], [KTM], [("dbgK",)], "dbgK")
                for (src, dst) in ((QTM, QT), (KTM, KT)):
                    for h in range(8):
                        pb = bank()
                        pbb = pb.ap.bitcast(BF)
                        for n in range(4):
                            P.add('pe', 'transpose', dict(out=pbb[:, n * 128:(n + 1) * 128],
                                                          in_=src[n].ap[:, h * 128:(h + 1) * 128], identity=IDN[:]),
                                  R=[src[n], vIDN], W=[pb])
                        act(dst[h].ap, pbb[:, 0:512], AF.Copy, [pb], [dst[h]])
                if STOP == 'q' and tt == 0:
                    raise _Stop()
                def stage_G(i4):
                    w = WS.get(w_in, l, 0, 16, O_GR + i4 * 256, 256)
                    for gi in range(2):
                        h = i4 * 2 + gi
                        pb = bank()
                        for kc in range(16):
                            mm(pb.ap, w.ap[:, kc, gi * 128:(gi + 1) * 128], A[kc].ap, kc == 0, kc == 15, [A[kc], w], [pb])
                        act(GS[h].ap, pb.ap, AF.Silu, [pb], [GS[h]])
                for i4 in range(4):
                    stage_G(i4)
                for hf in range(2):
                    act(SBS[0][hf].ap, vST[hf].ap, AF.Copy, [vST[hf]], [SBS[0][hf]])
                for n in range(4):
                    kv_update(n, True)
                OFb = [arv(32 * KB + i * 2 * KB, [128, 512], F32) for i in range(2)]
                MSb = [arv(36 * KB + i * 2 * KB, [128, 512], F32) for i in range(2)]
                M2b = [arv(40 * KB + i * 2 * KB, [128, 512], F32) for i in range(2)]
                OBb = [arv(44 * KB + i * KB, [128, 512], BF) for i in range(2)]
                OQb = [arv(46 * KB + i * KB, [128, 512], BF) for i in range(2)]
                smts = {}
                pOs = {}

                def stage_S(h):
                    pS = bank()
                    for n in range(4):
                        mm(pS.ap[:, n * 128:(n + 1) * 128], KT[h].ap[:, n * 128:(n + 1) * 128],
                           QT[h].ap[:, n * 128:(n + 1) * 128], True, True, [KT[h], QT[h]], [pS])
                    smt = nxt("smt", vSMT)
                    dve('tensor_tensor', [pS, vCST], [smt], out=b3(smt.ap, 4), in0=b3(pS.ap, 4),
                        in1=CST[:, 1306:1434].unsqueeze(1).to_broadcast([128, 4, 128]), op=ALU.mult)
                    smts[h] = smt

                def stage_O(h):
                    hf, hq = h // 4, h % 4
                    smt = smts[h]
                    pO = bank()
                    for n in range(4):
                        mm(pO.ap[:, n * 128:(n + 1) * 128], VTM[n].ap[:, h * 128:(h + 1) * 128],
                           smt.ap[:, n * 128:(n + 1) * 128], True, False, [VTM[n], smt], [pO])
                        mm(pO.ap[:, n * 128:(n + 1) * 128], SBS[n][hf].ap[:, hq * 128:(hq + 1) * 128],
                           QT[h].ap[:, n * 128:(n + 1) * 128], False, True, [SBS[n][hf], QT[h]], [pO])
                    ob = OBb[h % 2]
                    act(ob.ap, pO.ap, AF.Copy, [pO], [ob])

                pCs = {}

                def stage_T1(h):
                    ob, osq = OBb[h % 2], OQb[h % 2]
                    pC = bank()
                    mm(pC.ap, CEN[:], ob.ap, True, True, [ob, vCEN], [pC])
                    act(osq.ap, pC.ap, AF.Square, [pC], [osq])
                    pq = bank()
                    mm(pq.ap, ON_1[:], osq.ap, True, True, [osq, vON1], [pq])
                    pCs[h] = (pC, pq)

                def stage_T(h):
                    of = OFb[h % 2]
                    pC, pq = pCs[h]
                    r = rstd_of(pq, 1563)
                    dve('tensor_tensor', [pC, r], [of], out=of.ap, in0=pC.ap, in1=r.ap, op=ALU.mult)
                    dve('scalar_tensor_tensor', [of, GS[h], vSPR], [BR[8 + h]], out=BR[8 + h].ap, in0=of.ap,
                        scalar=SPR[:, 80 + h:81 + h], in1=GS[h].ap, op0=ALU.mult, op1=ALU.mult)
                for step in range(10):
                    if step < 8:
                        stage_S(step)
                    if 0 <= step - 1 < 8:
                        stage_O(step - 1)
                    if 0 <= step - 2 < 8:
                        stage_T1(step - 2)
                        stage_T(step - 2)
                if DEBUG and tt == 0:
                    dma('sp', dbgBR, AR[:, 8192:16384], [BR], [("dbgBR",)], "dbgBR")
                if STOP == 'ret' and tt == 0:
                    raise _Stop()
                accs = [vFT[0], vFT[1]]
                sgb = [vFT[2], vFT[3]]
                for grp in range(8):
                    wbr = WS.getm([(w_pp, l, 0, 4, grp * 256, 256), (w_cp, l, 0, 4, grp * 256, 256),
                                   (w_rp, l, 0, 8, grp * 256, 256)])
                    for b in range(3):
                        wg = WS.get(w_in, l, 0, 16, O_GATE + b * 2048 + grp * 256, 256, back=b + 1)
                        nkb, off = ((4, 0), (4, 4), (8, 8))[b]
                        for dci in range(2):
                            dc = grp * 2 + dci
                            pg = bank()
                            for kc in range(16):
                                mm(pg.ap, wg.ap[:, kc, dci * 128:(dci + 1) * 128], A[kc].ap, kc == 0, kc == 15,
                                   [A[kc], wg], [pg])
                            sg = sgb[dci]
                            act(sg.ap, pg.ap, AF.Sigmoid, [pg], [sg])
                            py = bank()
                            for kc in range(nkb):
                                mm(py.ap, wbr[b].ap[:, kc, dci * 128:(dci + 1) * 128], BR[off + kc].ap, kc == 0,
                                   kc == nkb - 1, [BR[off + kc], wbr[b]], [py])
                            if b == 0:
                                dve('tensor_tensor', [sg, py], [accs[dci]], out=accs[dci].ap, in0=sg.ap, in1=py.ap, op=ALU.mult)
                            else:
                                dve('tensor_tensor', [sg, py], [sg], out=sg.ap, in0=sg.ap, in1=py.ap, op=ALU.mult)
                                if b == 1:
                                    dve('tensor_tensor', [accs[dci], sg], [accs[dci]], out=accs[dci].ap, in0=accs[dci].ap,
                                        in1=sg.ap, op=ALU.add)
                                else:
                                    dve('tensor_tensor', [accs[dci], sg], [M[dc]], out=M[dc].ap, in0=accs[dci].ap,
                                        in1=sg.ap, op=ALU.add)
                if DEBUG and tt == 0:
                    dma('sp', dbgM, AR[:, 16384:24576], [M], [("dbgM",)], "dbgM")
                if STOP == 'merge' and tt == 0:
                    raise _Stop()
                for cgi in range(8):
                    w = WS.get(w_out, l, 0, 16, cgi * 256, 256)
                    for dci in range(2):
                        dc = cgi * 2 + dci
                        pb = bank()
                        for kc in range(16):
                            mm(pb.ap, w.ap[:, kc, dci * 128:(dci + 1) * 128], M[kc].ap, kc == 0, kc == 15, [M[kc], w], [pb])
                        evac_stats(pb, Y, dc)

                def xget(dc):
                    xc = nxt("xc", vXC)
                    dma('sp', xc.ap, xsrc[dc * 128:(dc + 1) * 128, t0:t0 + TT], [(xsn, tt, dc)], [xc], "ldxc%d" % (rot["xc"] % 2))
                    return xc
                post_norm_residual(Y, 16, xget, tt)
                if DEBUG and tt == 0:
                    dma('sp', dbgY, Y_all.ap.rearrange("p a b -> p (a b)"), [Y_all], [("dbgY",)], "dbgY")
                if STOP == 'wout' and tt == 0:
                    raise _Stop()
                norm(Y, 32, A)
                for jg in range(22):
                    wa = WS.get(w_fi, l, 0, 16, jg * 256, 256)
                    wb = WS.get(w_fi, l, 0, 16, FH + jg * 256, 256)
                    for ji in range(2):
                        j = jg * 2 + ji
                        pa = bank()
                        for kc in range(16):
                            mm(pa.ap, wa.ap[:, kc, ji * 128:(ji + 1) * 128], A[kc].ap, kc == 0, kc == 15, [A[kc], wa], [pa])
                        sa = nxt("ft", vFT)
                        act(sa.ap, pa.ap, AF.Silu, [pa], [sa])
                        pbb = bank()
                        for kc in range(16):
                            mm(pbb.ap, wb.ap[:, kc, ji * 128:(ji + 1) * 128], A[kc].ap, kc == 0, kc == 15, [A[kc], wb], [pbb])
                        dve('tensor_tensor', [sa, pbb], [HID[j]], out=HID[j].ap, in0=sa.ap, in1=pbb.ap, op=ALU.mult)
                for cgi in range(8):
                    pbs = [bank(), bank()]
                    for rg in range(3):
                        nk = 16 if rg < 2 else 12
                        w = WS.get(w_fo, l, rg * 2048, nk, cgi * 256, 256)
                        for dci in range(2):
                            for kk in range(nk):
                                mm(pbs[dci].ap, w.ap[:, kk, dci * 128:(dci + 1) * 128], HID[rg * 16 + kk].ap,
                                   rg == 0 and kk == 0, rg == 2 and kk == nk - 1, [HID[rg * 16 + kk], w], [pbs[dci]])
                    for dci in range(2):
                        evac_stats(pbs[dci], Y2, cgi * 2 + dci)
                def store_chunk(dc):
                    i = dma('sp', xdst[dc * 128:(dc + 1) * 128, t0:t0 + TT], Y[dc].ap, [Y[dc]], [(xdn, tt, dc)], "sty%d" % dc)
                    if l == L - 1:
                        out_dmas.append(i)
                post_norm_residual(Y2, 48, lambda dc: Y[dc], tt, after=store_chunk)
            if mode == 'fused' and l < L - 1:
                pass
    def body_fin(out_dmas):
        if not P.dry:
            lastd = {}
            for i, op in enumerate(P.ops):
                if op['dma']:
                    lastd[op['stream']] = i
            out_dmas = list(out_dmas) + list(lastd.values())
            P.ops.append(dict(eng='sp', meth=None, kw=None, deps=set(out_dmas), stream='sp', seq=P.seqc.get('sp', 0) + 1,
                              dma=False, ms=False))
            P.seqc['sp'] = P.seqc.get('sp', 0) + 1

    P.dry = True
    body()
    P.dry = False
    body()
    streams = P.finalize()
    sems = {}
    for s in streams:
        sems[s] = es.enter_context(nc.semaphore("s_" + s))
    with nc.Block() as block:
        P.emit(nc, block, sems)
    es.close()
    return nc, len(P.ops)


_GAMMA = 1.0 - np.exp2(-5.0 - np.arange(8, dtype=np.float64))


def _consts(core):
    s = core % 4
    b = core // 4
    c = np.zeros((128, NCST), np.float32)
    half = 64
    inv = (np.float32(10000.0) ** (-np.arange(half, dtype=np.float32) / np.float32(half))).astype(np.float32)
    c[:, 0:64] = inv[None, :]
    c[:, 64:128] = inv[None, :]
    p = np.arange(128, dtype=np.float64)
    lg = np.log(_GAMMA)
    c[:, 128:136] = np.exp((p[:, None] + 1.0) * lg[None, :])
    c[:, 136:144] = np.exp(-(p[:, None] + 1.0) * lg[None, :]) * (128.0 ** -0.5)
    gC = np.exp(128.0 * lg)
    c[:, 144:1168] = np.repeat(gC, 128)[None, :]
    G = np.exp(2048.0 * lg)
    for r in range(8):
        rb, rs = r // 4, r % 4
        if rb == b and rs < s:
            c[:, 1168 + r * 8:1168 + r * 8 + 8] = (G ** (s - 1 - rs))[None, :]
        if rb == b and rs == s - 1:
            c[:, 1232 + r] = 1.0
    wins = (2, 4, 8, 16)
    for g in range(4):
        t = np.arange(16)
        if s == 0:
            c[:, 1240 + g * 16:1240 + (g + 1) * 16] = (1.0 / np.minimum(t + 1, wins[g]))[None, :]
        else:
            c[:, 1240 + g * 16:1240 + (g + 1) * 16] = 1.0 / wins[g]
    c[:, 1304] = -np.pi
    c[:, 1305] = np.pi
    e = np.arange(128)
    c[:, 1306:1434] = (e[:, None] <= e[None, :]).astype(np.float32)
    c[:, 1434:1562] = np.eye(128, dtype=np.float32)
    c[:, 1562] = 1e-6
    c[:, 1563] = 1e-5
    return c


def _fm(v, n):
    return np.ascontiguousarray(v.reshape(n, 128).T)


def _pack_sp(inp, l):
    sp = np.zeros((128, NSP), np.float32)
    sp[:, 0:16] = _fm(inp["g_mix_pre"][l], 16)
    sp[:, 16:32] = _fm(inp["g_mix_post"][l], 16)
    sp[:, 32:48] = _fm(inp["g_ffn_pre"][l], 16)
    sp[:, 48:64] = _fm(inp["g_ffn_post"][l], 16)
    sp[:, 64:68] = _fm(inp["pool_scale"][l], 4)
    sp[:, 68:72] = _fm(inp["conv_b"][l], 4)
    sp[:, 72:76] = _fm(inp["conv_ln_g"][l], 4)
    sp[:, 76:80] = _fm(inp["conv_ln_b"][l], 4)
    sp[:, 80:88] = _fm(inp["ret_gn_g"][l], 8)
    dw = inp["conv_dw"][l]
    sp[:, 88:212] = dw.T.reshape(4, 128, 31).transpose(1, 0, 2).reshape(128, 124)
    return sp


def _pack_pw(inp, l):
    pw = inp["pool_w"][l]
    return np.ascontiguousarray(pw.transpose(1, 0, 2).reshape(128, 512))


_CACHE = {}


def _get(L, mode, ntok=2048):
    k = (L, mode, ntok)
    if k not in _CACHE:
        _CACHE[k] = build(L, mode, ntok)[0]
    return _CACHE[k]


def _core_x(x, c):
    b, s = c // 4, c % 4
    return np.ascontiguousarray(x[b, s * NTOK:(s + 1) * NTOK, :].T)


def _core_pos(pos, c):
    b, s = c // 4, c % 4
    return np.ascontiguousarray(pos[b, s * NTOK:(s + 1) * NTOK].reshape(16, 128).T.astype(np.int32))


def _layer_maps(inp, layers, xTs):
    sp = np.stack([_pack_sp(inp, l) for l in layers])
    pw = np.stack([_pack_pw(inp, l) for l in layers])
    sl = layers if len(layers) > 1 else slice(layers[0], layers[0] + 1)
    ws = dict(w_in=inp["w_in"][sl], w_pp=inp["w_pool_proj"][sl], w_cp=inp["w_conv_proj"][sl],
              w_rp=inp["w_ret_proj"][sl], w_out=inp["w_out"][sl], w_fi=inp["w_ffn_in"][sl], w_fo=inp["w_ffn_out"][sl])
    maps = []
    for c in range(8):
        m = dict(xT=xTs[c], posT=_core_pos(inp["positions"], c), cst=_CSTS[c], sp=sp, pw=pw)
        m.update(ws)
        maps.append(m)
    return maps


_CSTS = None


def kernel_unfused(**inputs):
    inp = {k: np.asarray(v) for k, v in inputs.items()}
    global _CSTS
    _CSTS = [_consts(c) for c in range(8)]
    x = inp["x"].astype(np.float32, copy=False)
    xTs = [_core_x(x, c) for c in range(8)]
    for l in range(4):
        maps = _layer_maps(inp, [l], xTs)
        nc_pre = _get(1, 'pre')
        pre_keys = ("xT", "posT", "cst", "sp", "pw", "w_in")
        res = run_bass_kernel_spmd(nc_pre, [{k: m[k] for k in pre_keys} for m in maps], core_ids=list(range(8)))
        xin = np.concatenate([res.results[c]["xout"] for c in range(8)], axis=0)
        for m in maps:
            m["xin"] = xin
        nc_full = _get(1, 'full')
        res = run_bass_kernel_spmd(nc_full, maps, core_ids=list(range(8)))
        xTs = [res.results[c]["yT"] for c in range(8)]
    out = np.empty((2, 8192, DM), np.float32)
    for c in range(8):
        b, s = c // 4, c % 4
        out[b, s * NTOK:(s + 1) * NTOK, :] = xTs[c].T
    return out


def kernel(**inputs):
    inp = {k: np.asarray(v) for k, v in inputs.items()}
    S = 8192
    nc = _get(4, 'seq', S)
    sp = np.stack([_pack_sp(inp, l) for l in range(4)])
    pw = np.stack([_pack_pw(inp, l) for l in range(4)])
    cst = _consts(0)
    x = inp["x"].astype(np.float32, copy=False)
    maps = []
    for b in range(2):
        maps.append(dict(
            xT=np.ascontiguousarray(x[b].T),
            posT=np.ascontiguousarray(inp["positions"][b].reshape(S // 128, 128).T.astype(np.int32)),
            cst=cst, sp=sp, pw=pw,
            w_in=inp["w_in"], w_pp=inp["w_pool_proj"], w_cp=inp["w_conv_proj"], w_rp=inp["w_ret_proj"],
            w_out=inp["w_out"], w_fi=inp["w_ffn_in"], w_fo=inp["w_ffn_out"]))
    res = run_bass_kernel_spmd(nc, maps, core_ids=[0, 1])
    out = np.empty((2, S, DM), np.float32)
    for b in range(2):
        out[b] = res.results[b]["yT"].T
    return out
```

```python
import numpy as np
from contextlib import ExitStack
import concourse.bass as bass
import concourse.mybir as mybir
from concourse.bass_utils import run_bass_kernel_spmd

F32 = mybir.dt.float32
BF = mybir.dt.bfloat16
I32 = mybir.dt.int32
ALU = mybir.AluOpType
AF = mybir.ActivationFunctionType

DM = 2048
NTOK = 2048
TT = 512
NTILE = NTOK // TT
NIN = 11776
FH = 5632
O_POOL, O_CA, O_CG, O_Q, O_K, O_V, O_GR, O_GATE = 0, 512, 1024, 1536, 2560, 3584, 4608, 5632
NSLOT = 6
XW = 1216
NSP = 212
NCST = 1564
PI = float(np.pi)
DEBUG = False
STOP = None


class _Stop(Exception):
    pass


class V:
    def __init__(self, ap, k):
        self.ap = ap
        self.k = k


def _flat(lst):
    out = []
    for x in lst:
        if isinstance(x, V):
            out.extend(x.k)
        elif isinstance(x, list):
            out.extend(_flat(x))
        else:
            out.append(x)
    return out


class Prog:
    ENG = ['pe', 'act', 'dve', 'pool', 'sp']

    def __init__(self):
        self.ops = []
        self.st = {}
        self.seqc = {}
        self.dry = False

    def add(self, eng, meth, kw, R=(), W=(), group=None):
        if self.dry:
            return -1
        R = _flat(list(R))
        W = _flat(list(W))
        idx = len(self.ops)
        stream = group if group is not None else eng
        deps = set()
        for k in R:
            e = self.st.get(k)
            if e is not None and e[0] is not None:
                deps.add(e[0])
        for k in W:
            e = self.st.get(k)
            if e is not None:
                if e[0] is not None:
                    deps.add(e[0])
                deps.update(e[1].values())
        seq = self.seqc.get(stream, 0) + 1
        self.seqc[stream] = seq
        self.ops.append(dict(eng=eng, meth=meth, kw=kw, deps=deps, stream=stream, seq=seq,
                             dma=group is not None, ms=False))
        for k in R:
            e = self.st.get(k)
            if e is None:
                e = [None, {}]
                self.st[k] = e
            e[1][stream] = idx
        for k in W:
            self.st[k] = [idx, {}]
        return idx

    def finalize(self):
        ops = self.ops
        hasdep = set()
        for op in ops:
            hasdep.update(op['deps'])
        know = {e: {} for e in self.ENG}
        snap = {}
        for i, op in enumerate(ops):
            E = op['eng']
            kn = know[E]
            waits = []
            for j in sorted(op['deps'], reverse=True):
                d = ops[j]
                if d['stream'] == 'pe' and E == 'pe':
                    continue
                if kn.get(d['stream'], 0) >= d['seq']:
                    continue
                waits.append(j)
                d['ms'] = True
                for s2, sq in snap[j].items():
                    if kn.get(s2, 0) < sq:
                        kn[s2] = sq
            op['waits'] = waits
            if i in hasdep:
                sn = dict(kn)
                if sn.get(op['stream'], 0) < op['seq']:
                    sn[op['stream']] = op['seq']
                snap[i] = sn
        cnt = {}
        for op in ops:
            if op['dma']:
                op['ms'] = True
            if op['ms']:
                c = cnt.get(op['stream'], 0) + (16 if op['dma'] else 1)
                cnt[op['stream']] = c
                op['cnt'] = c
        return sorted(cnt.keys())

    def emit(self, nc, block, sems):
        per = {e: [] for e in self.ENG}
        for op in self.ops:
            per[op['eng']].append(op)
        ops = self.ops

        def mk(E):
            def f(e):
                for op in per[E]:
                    for j in op['waits']:
                        d = ops[j]
                        e.wait_ge(sems[d['stream']], d['cnt'])
                    if op['meth'] is None:
                        continue
                    ins = getattr(e, op['meth'])(**op['kw'])
                    if op['ms']:
                        ins.then_inc(sems[op['stream']], 16 if op['dma'] else 1)
            return f
        block.tensor(mk('pe'))
        block.scalar(mk('act'))
        block.vector(mk('dve'))
        block.gpsimd(mk('pool'))
        block.sync(mk('sp'))


def build(L, mode, ntok=2048):
    nc = bass.Bass("TRN2", target_bir_lowering=False)
    NTOK = ntok
    NTILE = ntok // TT
    NCH = ntok // 128
    seq = mode == 'seq'
    P = Prog()
    es = ExitStack()

    def din(name, shape, dt):
        return nc.dram_tensor(name, shape, dt, kind="ExternalInput").ap()

    xT = din("xT", [DM, NTOK], F32)
    posT = din("posT", [128, NCH], I32)
    cstD = din("cst", [128, NCST], F32)
    spD = din("sp", [L, 128, NSP], F32)
    pwD = din("pw", [L, 128, 512], F32)
    w_in = din("w_in", [L, DM, NIN], F32)
    if mode != 'pre':
        w_pp = din("w_pp", [L, 512, DM], F32)
        w_cp = din("w_cp", [L, 512, DM], F32)
        w_rp = din("w_rp", [L, 1024, DM], F32)
        w_out = din("w_out", [L, DM, DM], F32)
        w_fi = din("w_fi", [L, DM, 2 * FH], F32)
        w_fo = din("w_fo", [L, FH, DM], F32)
    if mode == 'full':
        xinD = din("xin", [8 * 128, XW], F32)
    if mode == 'pre':
        xoutD = nc.dram_tensor("xout", [128, XW], F32, kind="ExternalOutput").ap()
    else:
        yT = nc.dram_tensor("yT", [DM, NTOK], F32, kind="ExternalOutput").ap()
    if DEBUG and mode == 'full':
        dbgA = nc.dram_tensor("dbgA", [128, 16 * 512], BF, kind="ExternalOutput").ap()
        dbgBR = nc.dram_tensor("dbgBR", [128, 16 * 512], BF, kind="ExternalOutput").ap()
        dbgM = nc.dram_tensor("dbgM", [128, 16 * 512], BF, kind="ExternalOutput").ap()
        dbgY = nc.dram_tensor("dbgY", [128, 16 * 512], F32, kind="ExternalOutput").ap()
        dbgQ = nc.dram_tensor("dbgQ", [128, 4 * 1024], BF, kind="ExternalOutput").ap()
        dbgK = nc.dram_tensor("dbgK", [128, 4 * 1024], BF, kind="ExternalOutput").ap()
        dbgCC = nc.dram_tensor("dbgCC", [128, 2048], F32, kind="ExternalOutput").ap()
        dbgSS = nc.dram_tensor("dbgSS", [128, 2048], F32, kind="ExternalOutput").ap()
        dbgHB = nc.dram_tensor("dbgHB", [128, 4 * 544], F32, kind="ExternalOutput").ap()
        dbgACC = nc.dram_tensor("dbgACC", [128, 4 * 512], F32, kind="ExternalOutput").ap()
        dbgSG = nc.dram_tensor("dbgSG", [128, 4 * 512], F32, kind="ExternalOutput").ap()
    ktmD = nc.dram_tensor("ktm_s", [NTOK, 1024], BF).ap()
    vtmD = nc.dram_tensor("vtm_s", [NTOK, 1024], BF).ap()
    if seq:
        xbuf = [nc.dram_tensor("xb%d" % i, [DM, NTOK], F32).ap() for i in range(2)]
        ccD = nc.dram_tensor("cc_s", [128, NCH * 128], F32).ap()
        ssD = nc.dram_tensor("ss_s", [128, NCH * 128], F32).ap()
    if mode == 'fused':
        xbuf = [nc.dram_tensor("xb%d" % i, [DM, NTOK], F32).ap() for i in range(2)]
        xchD = nc.dram_tensor("xch_s", [128, XW], F32).ap()
        xgD = nc.dram_tensor("xg_s", [8 * 128, XW], F32).ap()

    def sb(name, shape, dt):
        return es.enter_context(nc.sbuf_tensor(name, shape, dt))

    WR = [sb("wr%d" % i, [128, 4096], BF) for i in range(NSLOT)]
    ARB = 112 * 1024
    AR = sb("arena", [128, ARB // 2], BF)
    CST = sb("cstt", [128, NCST], F32)
    SPR = sb("spr", [128, NSP], F32)
    PWB = sb("pwb", [128, 512], BF)
    CC = sb("cc", [128, 4 if seq else 16, 128], F32)
    SS = sb("ss", [128, 4 if seq else 16, 128], F32)
    IDN = sb("idn", [128, 128], BF)
    ON_D = sb("ond", [128, 128], BF)
    ON_5 = sb("on5", [128, 128], BF)
    ON_1 = sb("on1", [128, 128], BF)
    ST = sb("stt", [128, 1024], F32)
    UPT = sb("upt", [128, 4, 16], F32)
    HBT = sb("hbt", [128, 4, 32], F32)
    POSI = sb("posi", [128, NCH], I32)
    RS = [sb("rs%d" % i, [128, 512], F32) for i in range(2)]
    FT = [sb("ft%d" % i, [128, 512], F32) for i in range(4)]
    BT = [sb("bt%d" % i, [128, 512], BF) for i in range(3)]
    SMTB = [sb("smt%d" % i, [128, 512], BF) for i in range(2)]
    DGB = [sb("dg%d" % i, [128, 128], BF) for i in range(4)]
    CEN = sb("cen", [128, 128], BF)
    PSB = [es.enter_context(nc.psum_tensor("ps%d" % i, [128, 512], F32)) for i in range(8)]

    def sv(t, name):
        return V(t[:], [(name,)])

    vCST = sv(CST, "cst"); vSPR = sv(SPR, "spr"); vPWB = sv(PWB, "pwb")
    vCC = sv(CC, "cc"); vSS = sv(SS, "ss"); vIDN = sv(IDN, "idn")
    vOND = sv(ON_D, "ond"); vON5 = sv(ON_5, "on5"); vON1 = sv(ON_1, "on1")
    vST = [V(ST[:, i * 512:(i + 1) * 512], [("st", i)]) for i in range(2)]
    vUPT = sv(UPT, "upt"); vHBT = sv(HBT, "hbt"); vPOSI = sv(POSI, "posi")
    vRS = [sv(RS[i], "rs%d" % i) for i in range(2)]
    vFT = [sv(FT[i], "ft%d" % i) for i in range(4)]
    vBT = [sv(BT[i], "bt%d" % i) for i in range(3)]
    vSMT = [sv(SMTB[i], "smt%d" % i) for i in range(2)]
    vDG = [sv(DGB[i], "dg%d" % i) for i in range(4)]
    vCEN = sv(CEN, "cen")
    PS = [V(PSB[i][:], [("ps", i)]) for i in range(8)]
    rot = {}

    def nxt(name, lst):
        i = rot.get(name, 0)
        rot[name] = i + 1
        return lst[i % len(lst)]

    def bank():
        return nxt("bank", PS[0:7])
    PSTAT = PS[7]

    def arv(off, shape, dt):
        n = 1
        for s in shape[1:]:
            n *= s
        nb = n * (2 if dt == BF else 4)
        ap = AR[:, off // 2: off // 2 + nb // 2]
        if dt != BF:
            ap = ap.bitcast(dt)
        if len(shape) == 3:
            ap = ap.rearrange("p (a b) -> p a b", a=shape[1])
        keys = [("AR", g) for g in range(off // 1024, (off + nb + 1023) // 1024)]
        return V(ap, keys)

    KB = 1024
    A = [arv(kc * KB, [128, 512], BF) for kc in range(16)]
    A_all = arv(0, [128, 16, 512], BF)
    BR = [arv(16 * KB + i * KB, [128, 512], BF) for i in range(16)]
    M = [arv(32 * KB + i * KB, [128, 512], BF) for i in range(16)]
    HID = [arv(32 * KB + j * KB, [128, 512], BF) for j in range(44)]
    Y = [arv(80 * KB + i * 2 * KB, [128, 512], F32) for i in range(16)]
    Y_all = arv(80 * KB, [128, 16, 512], F32)
    Y2 = [arv(i * 2 * KB, [128, 512], F32) for i in range(16)]
    QTM = [arv(48 * KB + n * 2 * KB, [128, 1024], BF) for n in range(4)]
    KTM = [arv(56 * KB + n * 2 * KB, [128, 1024], BF) for n in range(4)]
    VTM = [arv(64 * KB + n * 2 * KB, [128, 1024], BF) for n in range(4)]
    KTM_all = arv(56 * KB, [128, 4, 1024], BF)
    VTM_all = arv(64 * KB, [128, 4, 1024], BF)
    QT = [arv(72 * KB + h * KB, [128, 512], BF) for h in range(8)]
    KT = [arv(80 * KB + h * KB, [128, 512], BF) for h in range(8)]
    GS = [arv(88 * KB + h * KB, [128, 512], BF) for h in range(8)]
    SBS = [[arv(96 * KB + n * 2 * KB + hf * KB, [128, 512], BF) for hf in range(2)] for n in range(5)]
    UP = arv(48 * KB, [128, 4, 528], F32)
    SA = arv(48 * KB + 8448, [128, 4, 528], F32)
    SBF = arv(48 * KB + 2 * 8448, [128, 4, 528], F32)
    PP = [arv(48 * KB + 3 * 8448 + g * KB, [128, 512], BF) for g in range(4)]
    HB = arv(48 * KB, [128, 4, 544], BF)
    SG = arv(48 * KB + 8704, [128, 4, 512], F32)
    ACC = arv(48 * KB + 8704 + 8192, [128, 4, 512], F32)
    XT_ = [arv(80 * KB + i * 5 * KB, [128, XW], F32) for i in range(2)]
    vXC = [arv(16 * KB + i * 2 * KB, [128, 512], F32) for i in range(2)]

    class WStream:
        def __init__(self):
            self.reqs = []
            self.pos = 0
            self.issued = 0

        @staticmethod
        def _views(slot, parts):
            vs = []
            off = 0
            for (wt, l, r0, nk, c0, cols) in parts:
                vs.append(WR[slot][:, off:off + nk * cols].rearrange("p (k c) -> p k c", k=nk))
                off += nk * cols
            return vs

        @staticmethod
        def _keys(slot, pi, np_):
            if np_ == 1:
                return [("wr", slot, 0), ("wr", slot, 1), ("wr", slot, 2)]
            return [("wr", slot, pi)]

        def _issue(self, j):
            parts = self.reqs[j]
            slot = j % NSLOT
            vs = self._views(slot, parts)
            np_ = len(parts)
            for pi, (view, (wt, l, r0, nk, c0, cols)) in enumerate(zip(vs, parts)):
                src = wt[l, r0:r0 + nk * 128, c0:c0 + cols].rearrange("(k p) c -> p k c", p=128)
                P.add('pool', 'dma_start', dict(out=view, in_=src), R=[], W=self._keys(slot, pi, np_),
                      group="w%d_%d" % (slot, pi))

        def getm(self, parts, back=1):
            if P.dry:
                self.reqs.append(tuple(parts))
                return [V(v, [("wr", 0, 0)]) for v in self._views(0, parts)]
            i = self.pos
            self.pos += 1
            while self.issued <= min(len(self.reqs) - 1, i - back + NSLOT - 1):
                self._issue(self.issued)
                self.issued += 1
            slot = i % NSLOT
            return [V(v, self._keys(slot, pi, len(parts))) for pi, v in enumerate(self._views(slot, parts))]

        def get(self, wt, l, r0, nk, c0, cols, back=1):
            return self.getm([(wt, l, r0, nk, c0, cols)], back)[0]
    WS = WStream()

    def mm(out, lhsT, rhs, start, stop, R, W):
        P.add('pe', 'matmul', dict(out=out, lhsT=lhsT, rhs=rhs, start=start, stop=stop), R=R, W=W)

    def dve(meth, R, W, **kw):
        P.add('dve', meth, kw, R=R, W=W)

    def act(out, in_, func, R, W, **kw):
        P.add('act', 'activation', dict(out=out, in_=in_, func=func, **kw), R=R, W=W)

    def dma(q, out, in_, R, W, group):
        return P.add(q, 'dma_start', dict(out=out, in_=in_), R=R, W=W, group=group)

    def b3(ap2, n):
        return ap2.rearrange("p (a b) -> p a b", a=n)

    def rstd_of(src, eps_col):
        r = nxt("rs", vRS)
        act(r.ap, src.ap, AF.Sqrt, [src, vCST], [r], bias=CST[:, eps_col:eps_col + 1], scale=1.0)
        dve('reciprocal', [r], [r], out=r.ap, in_=r.ap)
        return r

    def norm(X, gcol0, Aout):
        pb = bank()
        for kc in range(16):
            sq = nxt("bt", vBT)
            act(sq.ap, X[kc].ap, AF.Square, [X[kc]], [sq])
            mm(pb.ap, ON_D[:], sq.ap, kc == 0, kc == 15, [sq, vOND], [pb])
        r = rstd_of(pb, 1562)
        for kc in range(16):
            dve('scalar_tensor_tensor', [X[kc], r, vSPR], [Aout[kc]], out=Aout[kc].ap, in0=X[kc].ap,
                scalar=SPR[:, gcol0 + kc:gcol0 + kc + 1], in1=r.ap, op0=ALU.mult, op1=ALU.mult)

    def rotary(src, gn, dst_ap, dstv):
        s3 = b3(src.ap, 4)
        ta = nxt("ft", vFT)
        tb = nxt("ft", vFT)
        dve('tensor_tensor', [src, vCC], [ta], out=b3(ta.ap, 4), in0=s3,
            in1=CC[:, gn, :].unsqueeze(1).to_broadcast([128, 4, 128]), op=ALU.mult)
        dve('tensor_tensor', [src, vSS], [tb], out=b3(tb.ap, 4)[:, :, 0:64], in0=s3[:, :, 64:128],
            in1=SS[:, gn, 0:64].unsqueeze(1).to_broadcast([128, 4, 64]), op=ALU.mult)
        dve('tensor_tensor', [src, vSS, tb], [tb], out=b3(tb.ap, 4)[:, :, 64:128], in0=s3[:, :, 0:64],
            in1=SS[:, gn, 64:128].unsqueeze(1).to_broadcast([128, 4, 64]), op=ALU.mult)
        dve('tensor_tensor', [ta, tb], [dstv], out=dst_ap, in0=ta.ap, in1=tb.ap, op=ALU.add)

    def tok_proj(l, coff, cg, n, Aall):
        pb = bank()
        w0, w1 = tok_proj.w
        for kc in range(16):
            w = w0 if kc < 8 else w1
            mm(pb.ap, A[kc].ap[:, n * 128:(n + 1) * 128], w.ap[:, kc % 8, :], kc == 0, kc == 15,
               [A[kc], w], [pb])
        return pb

    def load_x_tile(xsrc, xname, tt):
        t0 = tt * TT
        for kc in range(16):
            dma('sp', Y[kc].ap, xsrc[kc * 128:(kc + 1) * 128, t0:t0 + TT], [(xname, tt, kc)], [Y[kc]], "ldx%d" % kc)

    def kv_update(n, write_sb):
        for hf in range(2):
            pb = bank()
            for hq in range(4):
                h = hf * 4 + hq
                mm(pb.ap[:, hq * 128:(hq + 1) * 128], KTM[n].ap[:, h * 128:(h + 1) * 128],
                   VTM[n].ap[:, h * 128:(h + 1) * 128], True, True, [KTM[n], VTM[n]], [pb])
            s = vST[hf]
            dve('tensor_tensor', [s, pb], [s], out=s.ap, in0=s.ap, in1=pb.ap, op=ALU.add)
            dve('tensor_tensor', [s, vCST], [s], out=s.ap, in0=s.ap,
                in1=CST[:, 144 + hf * 512:144 + (hf + 1) * 512], op=ALU.mult)
            if write_sb:
                act(SBS[n + 1][hf].ap, s.ap, AF.Copy, [s], [SBS[n + 1][hf]])

    def pre_kv_tile(l, tt, cbase):
        for (coff, isk) in ((O_K, True), (O_V, False)):
            for cg in range(2):
                w0 = WS.get(w_in, l, 0, 8, coff + cg * 512, 512)
                w1 = WS.get(w_in, l, 1024, 8, coff + cg * 512, 512)
                tok_proj.w = (w0, w1)
                for n in range(4):
                    pb = tok_proj(l, coff, cg, n, None)
                    if isk:
                        kd = nxt("ft", vFT)
                        dve('tensor_tensor', [pb, vCST], [kd], out=b3(kd.ap, 4), in0=b3(pb.ap, 4),
                            in1=CST[:, 136 + cg * 4:136 + cg * 4 + 4].unsqueeze(2).to_broadcast([128, 4, 128]),
                            op=ALU.mult)
                        rotary(kd, cbase + n, KTM[n].ap[:, cg * 512:(cg + 1) * 512], KTM[n])
                    else:
                        act(VTM[n].ap[:, cg * 512:(cg + 1) * 512], pb.ap, AF.Copy, [pb], [VTM[n]])

    def u_proj_conv_pool(l, tt, tails_only):
        dve('tensor_copy', [vUPT], [UP], out=UP.ap[:, :, 1:16], in_=UPT[:, :, 1:16])
        for half in range(2):
            w = WS.get(w_in, l, 0, 16, O_POOL + half * 256, 256)
            for gi in range(2):
                g = half * 2 + gi
                pb = bank()
                for kc in range(16):
                    mm(pb.ap, w.ap[:, kc, gi * 128:(gi + 1) * 128], A[kc].ap, kc == 0, kc == 15, [A[kc], w], [pb])
                act(UP.ap[:, g, 16:528], pb.ap, AF.Copy, [pb], [UP])
        dve('tensor_copy', [UP], [vUPT], out=UPT[:, :, 1:16], in_=UP.ap[:, :, 513:528])
        if not tails_only:
            wins = (2, 4, 8, 16)
            src = UP
            bufs = [SA, SBF]
            for g in range(4):
                sh = wins[g] // 2
                dstb = bufs[g % 2]
                dve('tensor_tensor', [src], [dstb], out=dstb.ap[:, g:4, 2 * sh:528], in0=src.ap[:, g:4, 2 * sh:528],
                    in1=src.ap[:, g:4, sh:528 - sh], op=ALU.add)
                dve('scalar_tensor_tensor', [dstb, UP], [PP[g]], out=PP[g].ap, in0=dstb.ap[:, g, 16:528],
                    scalar=1.0 / wins[g], in1=UP.ap[:, g, 16:528], op0=ALU.mult, op1=ALU.subtract)
                if tt == 0:
                    t1 = nxt("ft", vFT)
                    dve('tensor_tensor', [dstb, vCST], [t1], out=t1.ap[:, 0:16], in0=dstb.ap[:, g, 16:32],
                        in1=CST[:, 1240 + g * 16:1240 + (g + 1) * 16], op=ALU.mult)
                    dve('tensor_tensor', [t1, UP, PP[g]], [PP[g]], out=PP[g].ap[:, 0:16], in0=t1.ap[:, 0:16],
                        in1=UP.ap[:, g, 16:32], op=ALU.subtract)
                src = dstb
        for half in range(2):
            w = WS.get(w_in, l, 0, 16, O_CG + half * 256, 256)
            for gi in range(2):
                j = half * 2 + gi
                pb = bank()
                for kc in range(16):
                    mm(pb.ap, w.ap[:, kc, gi * 128:(gi + 1) * 128], A[kc].ap, kc == 0, kc == 15, [A[kc], w], [pb])
                act(SG.ap[:, j, :], pb.ap, AF.Sigmoid, [pb], [SG])
        dve('tensor_copy', [vHBT], [HB], out=HB.ap[:, :, 2:32], in_=HBT[:, :, 2:32])
        for half in range(2):
            w = WS.get(w_in, l, 0, 16, O_CA + half * 256, 256)
            for gi in range(2):
                j = half * 2 + gi
                pb = bank()
                for kc in range(16):
                    mm(pb.ap, w.ap[:, kc, gi * 128:(gi + 1) * 128], A[kc].ap, kc == 0, kc == 15, [A[kc], w], [pb])
                dve('tensor_tensor', [pb, SG], [HB], out=HB.ap[:, j, 32:544], in0=pb.ap, in1=SG.ap[:, j, :], op=ALU.mult)
        dve('tensor_copy', [HB], [vHBT], out=HBT[:, :, 2:32], in_=HB.ap[:, :, 514:544])
        if tails_only:
            return
        for g in range(4):
            pb = bank()
            mm(pb.ap, PWB[:, g * 128:(g + 1) * 128], PP[g].ap, True, True, [PP[g], vPWB], [pb])
            dve('tensor_scalar', [pb, vSPR], [BR[g]], out=BR[g].ap, in0=pb.ap, scalar1=SPR[:, 64 + g:65 + g],
                scalar2=None, op0=ALU.mult)
        for j in range(4):
            pc = bank()
            for t in range(31):
                dg = nxt("dg", vDG)
                dve('tensor_scalar', [vIDN, vSPR], [dg], out=dg.ap, in0=IDN[:], scalar1=SPR[:, 88 + j * 31 + t:89 + j * 31 + t],
                    scalar2=None, op0=ALU.mult)
                mm(pc.ap, dg.ap, HB.ap[:, j, 2 + t:514 + t], t == 0, t == 30, [dg, HB], [pc])
            act(ACC.ap[:, j, :], pc.ap, AF.Identity, [pc, vSPR], [ACC], bias=SPR[:, 68 + j:69 + j], scale=1.0)
        if DEBUG and mode == 'full' and tt == 0:
            dma('sp', dbgHB, HB.ap.rearrange("p a b -> p (a b)"), [HB], [("dbgHB",)], "dbgHB")
            dma('sp', dbgACC, ACC.ap.rearrange("p a b -> p (a b)"), [ACC], [("dbgACC",)], "dbgACC")
            dma('sp', dbgSG, SG.ap.rearrange("p a b -> p (a b)"), [SG], [("dbgSG",)], "dbgSG")
        pm = bank()
        pq = bank()
        c16s = [arv(48 * KB + 8704 + j * KB, [128, 512], BF) for j in range(8)]
        for j in range(4):
            act(c16s[j].ap, ACC.ap[:, j, :], AF.Copy, [ACC], [c16s[j]])
            act(c16s[4 + j].ap, ACC.ap[:, j, :], AF.Square, [ACC], [c16s[4 + j]])
        for j in range(4):
            mm(pm.ap, ON_5[:], c16s[j].ap, j == 0, j == 3, [c16s[j], vON5], [pm])
        for j in range(4):
            mm(pq.ap, ON_5[:], c16s[4 + j].ap, j == 0, j == 3, [c16s[4 + j], vON5], [pq])
        ms = nxt("ft", vFT)
        act(ms.ap, pm.ap, AF.Copy, [pm], [ms])
        m2 = nxt("ft", vFT)
        dve('tensor_tensor', [ms], [m2], out=m2.ap, in0=ms.ap, in1=ms.ap, op=ALU.mult)
        dve('tensor_tensor', [pq, m2], [m2], out=m2.ap, in0=pq.ap, in1=m2.ap, op=ALU.subtract)
        dve('tensor_scalar', [m2], [m2], out=m2.ap, in0=m2.ap, scalar1=0.0, scalar2=None, op0=ALU.max)
        r = rstd_of(m2, 1563)
        tpair = [nxt("ft", vFT), nxt("ft", vFT)]
        for j in range(4):
            t = tpair[j % 2]
            dve('tensor_tensor', [ACC, ms], [t], out=t.ap, in0=ACC.ap[:, j, :], in1=ms.ap, op=ALU.subtract)
            dve('tensor_tensor', [t, r], [t], out=t.ap, in0=t.ap, in1=r.ap, op=ALU.mult)
            act(BR[4 + j].ap, t.ap, AF.Silu, [t, vSPR], [BR[4 + j]], scale=SPR[:, 72 + j:73 + j],
                bias=SPR[:, 76 + j:77 + j])

    def post_norm_residual(Yo, gcol0, xget, tt, after=None):
        flush_stats()
        r = rstd_of(PSTAT, 1562)
        for d4 in range(0, 16, 4):
            ts_ = []
            for dc in range(d4, d4 + 4):
                t = vFT[dc % 4]
                dve('scalar_tensor_tensor', [Yo[dc], r, vSPR], [t], out=t.ap, in0=Yo[dc].ap,
                    scalar=SPR[:, gcol0 + dc:gcol0 + dc + 1], in1=r.ap, op0=ALU.mult, op1=ALU.mult)
                ts_.append(t)
            for dc in range(d4, d4 + 4):
                xv = xget(dc)
                t = ts_[dc - d4]
                dve('tensor_tensor', [t, xv], [Y[dc]], out=Y[dc].ap, in0=t.ap, in1=xv.ap, op=ALU.add)
                if after is not None:
                    after(dc)

    pend_stats = []

    def flush_stats(keep=0):
        while len(pend_stats) > keep:
            sq, dc = pend_stats.pop(0)
            mm(PSTAT.ap, ON_D[:], sq.ap, dc == 0, dc == 15, [sq, vOND], [PSTAT])

    def evac_stats(pb, Yo, dc):
        flush_stats(1)
        act(Yo[dc].ap, pb.ap, AF.Copy, [pb], [Yo[dc]])
        sq = nxt("bt", vBT)
        act(sq.ap, pb.ap, AF.Square, [pb], [sq])
        pend_stats.append((sq, dc))

    def body():
        rot.clear()
        out_dmas = []
        try:
            body_main(out_dmas)
        except _Stop:
            pass
        body_fin(out_dmas)

    def body_main(out_dmas):
        dma('sp', CST[:], cstD[:, :], [], [vCST], "ldc")
        dma('sp', POSI[:], posT[:, :], [], [vPOSI], "ldp")
        dve('tensor_copy', [vCST], [vIDN], out=IDN[:], in_=CST[:, 1434:1562])
        dve('tensor_scalar', [vIDN], [vCEN], out=CEN[:], in0=IDN[:], scalar1=-1.0 / 128, scalar2=None, op0=ALU.add)
        dve('memset', [], [vOND], ap=ON_D[:], constant=1.0 / 2048)
        dve('memset', [], [vON5], ap=ON_5[:], constant=1.0 / 512)
        dve('memset', [], [vON1], ap=ON_1[:], constant=1.0 / 128)
        posf = vFT[0]
        dve('tensor_copy', [vPOSI], [posf], out=posf.ap[:, 0:NCH], in_=POSI[:])
        T1 = arv(0, [128, 16, 128], F32)
        T2 = arv(8 * KB, [128, 16, 128], F32)
        T3 = arv(16 * KB, [128, 16, 128], F32)
        TIv = arv(24 * KB, [128, 16, 128], F32)
        TI = V(TIv.ap.bitcast(I32), TIv.k)
        T4 = arv(32 * KB, [128, 16, 128], F32)
        T5 = arv(40 * KB, [128, 16, 128], F32)
        for piece in range(NCH // 16):
            if seq:
                cdst, sdst, cv, sv_ = T4.ap, T5.ap, T4, T5
            else:
                cdst, sdst, cv, sv_ = CC[:], SS[:], vCC, vSS
            dve('tensor_tensor', [posf, vCST], [T1], out=T1.ap,
                in0=posf.ap[:, piece * 16:(piece + 1) * 16].unsqueeze(2).to_broadcast([128, 16, 128]),
                in1=CST[:, 0:128].unsqueeze(1).to_broadcast([128, 16, 128]), op=ALU.mult)
            dve('tensor_scalar', [T1], [T2], out=T2.ap, in0=T1.ap, scalar1=1.0 / (2 * PI), scalar2=None, op0=ALU.mult)
            dve('tensor_copy', [T2], [TI], out=TI.ap, in_=T2.ap)
            dve('tensor_copy', [TI], [T2], out=T2.ap, in_=TI.ap)
            dve('scalar_tensor_tensor', [T2, T1], [T1], out=T1.ap, in0=T2.ap, scalar=-2 * PI, in1=T1.ap, op0=ALU.mult, op1=ALU.add)
            act(T2.ap, T1.ap, AF.Sin, [T1], [T2], scale=0.5)
            act(T3.ap, T1.ap, AF.Sin, [T1], [T3], scale=0.25)
            dve('tensor_tensor', [T2], [cv], out=cdst, in0=T2.ap, in1=T2.ap, op=ALU.mult)
            dve('tensor_scalar', [cv], [cv], out=cdst, in0=cdst, scalar1=-2.0, scalar2=1.0, op0=ALU.mult, op1=ALU.add)
            dve('tensor_tensor', [T3], [T3], out=T3.ap, in0=T3.ap, in1=T3.ap, op=ALU.mult)
            dve('tensor_scalar', [T3], [T3], out=T3.ap, in0=T3.ap, scalar1=-2.0, scalar2=1.0, op0=ALU.mult, op1=ALU.add)
            dve('scalar_tensor_tensor', [T2, T3], [sv_], out=sdst[:, :, 64:128], in0=T2.ap[:, :, 64:128], scalar=2.0,
                in1=T3.ap[:, :, 64:128], op0=ALU.mult, op1=ALU.mult)
            dve('scalar_tensor_tensor', [T2, T3, sv_], [sv_], out=sdst[:, :, 0:64], in0=T2.ap[:, :, 0:64], scalar=-2.0,
                in1=T3.ap[:, :, 0:64], op0=ALU.mult, op1=ALU.mult)
            if seq:
                dma('sp', ccD[:, piece * 2048:(piece + 1) * 2048], T4.ap.rearrange("p a b -> p (a b)"), [T4], [("ccD", piece)], "stcc")
                dma('sp', ssD[:, piece * 2048:(piece + 1) * 2048], T5.ap.rearrange("p a b -> p (a b)"), [T5], [("ssD", piece)], "stss")

        if DEBUG and mode == 'full':
            dma('sp', dbgCC, CC[:].rearrange("p a b -> p (a b)"), [vCC], [("dbgCC",)], "dbgCC")
            dma('sp', dbgSS, SS[:].rearrange("p a b -> p (a b)"), [vSS], [("dbgSS",)], "dbgSS")
        for l in range(L) if True else []:
            if mode == 'fused' or seq:
                xsrc = xT if l == 0 else xbuf[(l - 1) % 2]
                xdst = yT if l == L - 1 else xbuf[l % 2]
                xsn = "xT" if l == 0 else "xb%d" % ((l - 1) % 2)
                xdn = "yT" if l == L - 1 else "xb%d" % (l % 2)
            else:
                xsrc = xT
                xdst = None if mode == 'pre' else yT
                xsn, xdn = "xT", "yT"
            dma('sp', SPR[:], spD[l], [], [vSPR], "ldsp")
            dma('pool', PWB[:], pwD[l], [], [vPWB], "ldpw")
            dve('memset', [], [vST[0]], ap=ST[:, 0:512], constant=0.0)
            dve('memset', [], [vST[1]], ap=ST[:, 512:1024], constant=0.0)
            dve('memset', [], [vUPT], ap=UPT[:], constant=0.0)
            dve('memset', [], [vHBT], ap=HBT[:], constant=0.0)
            for tt in range(0 if seq else NTILE):
                t0 = tt * TT
                load_x_tile(xsrc, xsn, tt)
                norm(Y, 0, A)
                pre_kv_tile(l, tt, tt * 4)
                dma('sp', ktmD[t0:t0 + TT, :].rearrange("(n p) c -> p n c", p=128), KTM_all.ap, [KTM_all], [("ktmD", tt)], "stk")
                dma('sp', vtmD[t0:t0 + TT, :].rearrange("(n p) c -> p n c", p=128), VTM_all.ap, [VTM_all], [("vtmD", tt)], "stv")
                for n in range(4):
                    kv_update(n, False)
                if tt == NTILE - 1:
                    u_proj_conv_pool(l, tt, True)
            if mode == 'pre':
                xo = xoutD
            elif mode == 'fused':
                xo = xchD
            if mode in ('pre', 'fused'):
                i1 = dma('sp', xo[:, 0:512], ST[:, 0:512], [vST[0]], [("xo", 0)], "sx0")
                i2 = dma('sp', xo[:, 512:1024], ST[:, 512:1024], [vST[1]], [("xo", 1)], "sx1")
                i3 = dma('sp', xo[:, 1024:1152], HBT[:].rearrange("p a b -> p (a b)"), [vHBT], [("xo", 2)], "sx2")
                i4 = dma('sp', xo[:, 1152:1216], UPT[:].rearrange("p a b -> p (a b)"), [vUPT], [("xo", 3)], "sx3")
                out_dmas += [i1, i2, i3, i4]
            if mode == 'pre':
                continue
            if mode == 'fused':
                P.add('pool', 'collective_compute',
                      dict(kind="AllGather", op=ALU.bypass, replica_groups=[list(range(8))],
                           ins=[xchD[:, :]], outs=[xgD[:, :]]),
                      R=[("xo", 0), ("xo", 1), ("xo", 2), ("xo", 3)], W=[("xg",)], group="cc")
                xin_src = xgD
            elif not seq:
                xin_src = xinD
            if not seq:
                dve('memset', [vST[0]], [vST[0]], ap=ST[:, 0:512], constant=0.0)
                dve('memset', [vST[1]], [vST[1]], ap=ST[:, 512:1024], constant=0.0)
                dve('memset', [vUPT], [vUPT], ap=UPT[:], constant=0.0)
                dve('memset', [vHBT], [vHBT], ap=HBT[:], constant=0.0)
            for r in range(0 if seq else 8):
                xt = XT_[r % 2]
                dma('sp', xt.ap, xin_src[r * 128:(r + 1) * 128, :], [("xg",)], [xt], "ldxg%d" % (r % 2))
                for h in range(8):
                    s = vST[h // 4]
                    dve('scalar_tensor_tensor', [xt, s, vCST], [s], out=ST[:, h * 128:(h + 1) * 128],
                        in0=xt.ap[:, h * 128:(h + 1) * 128], scalar=CST[:, 1168 + r * 8 + h:1169 + r * 8 + h],
                        in1=ST[:, h * 128:(h + 1) * 128], op0=ALU.mult, op1=ALU.add)
                hb2 = HBT[:].rearrange("p a b -> p (a b)")
                dve('scalar_tensor_tensor', [xt, vHBT, vCST], [vHBT], out=hb2, in0=xt.ap[:, 1024:1152],
                    scalar=CST[:, 1232 + r:1233 + r], in1=hb2, op0=ALU.mult, op1=ALU.add)
                up2 = UPT[:].rearrange("p a b -> p (a b)")
                dve('scalar_tensor_tensor', [xt, vUPT, vCST], [vUPT], out=up2, in0=xt.ap[:, 1152:1216],
                    scalar=CST[:, 1232 + r:1233 + r], in1=up2, op0=ALU.mult, op1=ALU.add)
            for tt in range(NTILE):
                t0 = tt * TT
                if seq:
                    dma('sp', CC[:], ccD[:, tt * 512:(tt + 1) * 512].rearrange("p (a b) -> p a b", a=4), [("ccD", tt // 4)], [vCC], "ldcc")
                    dma('sp', SS[:], ssD[:, tt * 512:(tt + 1) * 512].rearrange("p (a b) -> p a b", a=4), [("ssD", tt // 4)], [vSS], "ldss")
                load_x_tile(xsrc, xsn, tt)
                norm(Y, 0, A)
                if DEBUG and tt == 0:
                    dma('sp', dbgA, AR[:, 0:8192], [A_all], [("dbgA",)], "dbgA")
                u_proj_conv_pool(l, tt, False)
                if STOP == 'proj' and tt == 0:
                    raise _Stop()
                for cg in range(2):
                    w0 = WS.get(w_in, l, 0, 8, O_Q + cg * 512, 512)
                    w1 = WS.get(w_in, l, 1024, 8, O_Q + cg * 512, 512)
                    tok_proj.w = (w0, w1)
                    for n in range(4):
                        pb = tok_proj(l, O_Q, cg, n, None)
                        qd = nxt("ft", vFT)
                        dve('tensor_tensor', [pb, vCST], [qd], out=b3(qd.ap, 4), in0=b3(pb.ap, 4),
                            in1=CST[:, 128 + cg * 4:128 + cg * 4 + 4].unsqueeze(2).to_broadcast([128, 4, 128]),
                            op=ALU.mult)
                        rotary(qd, (0 if seq else tt * 4) + n, QTM[n].ap[:, cg * 512:(cg + 1) * 512], QTM[n])
                if seq:
                    pre_kv_tile(l, tt, 0)
                else:
                    dma('sp', KTM_all.ap, ktmD[t0:t0 + TT, :].rearrange("(n p) c -> p n c", p=128), [("ktmD", tt)], [KTM_all], "ldk")
                    dma('sp', VTM_all.ap, vtmD[t0:t0 + TT, :].rearrange("(n p) c -> p n c", p=128), [("vtmD", tt)], [VTM_all], "ldv")
                if DEBUG and tt == 0:
                    dma('sp', dbgQ, AR[:, 24 * KB:28 * KB], [QTM], [("dbgQ",)], "dbgQ")
                    dma('sp', dbgK, AR[:, 28 * KB:32 * KB], [KTM], [("dbgK",)], "dbgK")
                for (src, dst) in ((QTM, QT), (KTM, KT)):
                    for h in range(8):
                        pb = bank()
                        pbb = pb.ap.bitcast(BF)
                        for n in range(4):
                            P.add('pe', 'transpose', dict(out=pbb[:, n * 128:(n + 1) * 128],
                                                          in_=src[n].ap[:, h * 128:(h + 1) * 128], identity=IDN[:]),
                                  R=[src[n], vIDN], W=[pb])
                        act(dst[h].ap, pbb[:, 0:512], AF.Copy, [pb], [dst[h]])
                if STOP == 'q' and tt == 0:
                    raise _Stop()
                def stage_G(i4):
                    w = WS.get(w_in, l, 0, 16, O_GR + i4 * 256, 256)
                    for gi in range(2):
                        h = i4 * 2 + gi
                        pb = bank()
                        for kc in range(16):
                            mm(pb.ap, w.ap[:, kc, gi * 128:(gi + 1) * 128], A[kc].ap, kc == 0, kc == 15, [A[kc], w], [pb])
                        act(GS[h].ap, pb.ap, AF.Silu, [pb], [GS[h]])
                for i4 in range(4):
                    stage_G(i4)
                for hf in range(2):
                    act(SBS[0][hf].ap, vST[hf].ap, AF.Copy, [vST[hf]], [SBS[0][hf]])
                for n in range(4):
                    kv_update(n, True)
                OFb = [arv(32 * KB + i * 2 * KB, [128, 512], F32) for i in range(2)]
                MSb = [arv(36 * KB + i * 2 * KB, [128, 512], F32) for i in range(2)]
                M2b = [arv(40 * KB + i * 2 * KB, [128, 512], F32) for i in range(2)]
                OBb = [arv(44 * KB + i * KB, [128, 512], BF) for i in range(2)]
                OQb = [arv(46 * KB + i * KB, [128, 512], BF) for i in range(2)]
                smts = {}
                pOs = {}

                def stage_S(h):
                    pS = bank()
                    for n in range(4):
                        mm(pS.ap[:, n * 128:(n + 1) * 128], KT[h].ap[:, n * 128:(n + 1) * 128],
                           QT[h].ap[:, n * 128:(n + 1) * 128], True, True, [KT[h], QT[h]], [pS])
                    smt = nxt("smt", vSMT)
                    dve('tensor_tensor', [pS, vCST], [smt], out=b3(smt.ap, 4), in0=b3(pS.ap, 4),
                        in1=CST[:, 1306:1434].unsqueeze(1).to_broadcast([128, 4, 128]), op=ALU.mult)
                    smts[h] = smt

                def stage_O(h):
                    hf, hq = h // 4, h % 4
                    smt = smts[h]
                    pO = bank()
                    for n in range(4):
                        mm(pO.ap[:, n * 128:(n + 1) * 128], VTM[n].ap[:, h * 128:(h + 1) * 128],
                           smt.ap[:, n * 128:(n + 1) * 128], True, False, [VTM[n], smt], [pO])
                        mm(pO.ap[:, n * 128:(n + 1) * 128], SBS[n][hf].ap[:, hq * 128:(hq + 1) * 128],
                           QT[h].ap[:, n * 128:(n + 1) * 128], False, True, [SBS[n][hf], QT[h]], [pO])
                    ob = OBb[h % 2]
                    act(ob.ap, pO.ap, AF.Copy, [pO], [ob])

                pCs = {}

                def stage_T1(h):
                    ob, osq = OBb[h % 2], OQb[h % 2]
                    pC = bank()
                    mm(pC.ap, CEN[:], ob.ap, True, True, [ob, vCEN], [pC])
                    act(osq.ap, pC.ap, AF.Square, [pC], [osq])
                    pq = bank()
                    mm(pq.ap, ON_1[:], osq.ap, True, True, [osq, vON1], [pq])
                    pCs[h] = (pC, pq)

                def stage_T(h):
                    of = OFb[h % 2]
                    pC, pq = pCs[h]
                    r = rstd_of(pq, 1563)
                    dve('tensor_tensor', [pC, r], [of], out=of.ap, in0=pC.ap, in1=r.ap, op=ALU.mult)
                    dve('scalar_tensor_tensor', [of, GS[h], vSPR], [BR[8 + h]], out=BR[8 + h].ap, in0=of.ap,
                        scalar=SPR[:, 80 + h:81 + h], in1=GS[h].ap, op0=ALU.mult, op1=ALU.mult)
                for step in range(10):
                    if step < 8:
                        stage_S(step)
                    if 0 <= step - 1 < 8:
                        stage_O(step - 1)
                    if 0 <= step - 2 < 8:
                        stage_T1(step - 2)
                        stage_T(step - 2)
                if DEBUG and tt == 0:
                    dma('sp', dbgBR, AR[:, 8192:16384], [BR], [("dbgBR",)], "dbgBR")
                if STOP == 'ret' and tt == 0:
                    raise _Stop()
                accs = [vFT[0], vFT[1]]
                sgb = [vFT[2], vFT[3]]
                for grp in range(8):
                    wbr = WS.getm([(w_pp, l, 0, 4, grp * 256, 256), (w_cp, l, 0, 4, grp * 256, 256),
                                   (w_rp, l, 0, 8, grp * 256, 256)])
                    for b in range(3):
                        wg = WS.get(w_in, l, 0, 16, O_GATE + b * 2048 + grp * 256, 256, back=b + 1)
                        nkb, off = ((4, 0), (4, 4), (8, 8))[b]
                        for dci in range(2):
                            dc = grp * 2 + dci
                            pg = bank()
                            for kc in range(16):
                                mm(pg.ap, wg.ap[:, kc, dci * 128:(dci + 1) * 128], A[kc].ap, kc == 0, kc == 15,
                                   [A[kc], wg], [pg])
                            sg = sgb[dci]
                            act(sg.ap, pg.ap, AF.Sigmoid, [pg], [sg])
                            py = bank()
                            for kc in range(nkb):
                                mm(py.ap, wbr[b].ap[:, kc, dci * 128:(dci + 1) * 128], BR[off + kc].ap, kc == 0,
                                   kc == nkb - 1, [BR[off + kc], wbr[b]], [py])
                            if b == 0:
                                dve('tensor_tensor', [sg, py], [accs[dci]], out=accs[dci].ap, in0=sg.ap, in1=py.ap, op=ALU.mult)
                            else:
                                dve('tensor_tensor', [sg, py], [sg], out=sg.ap, in0=sg.ap, in1=py.ap, op=ALU.mult)
                                if b == 1:
                                    dve('tensor_tensor', [accs[dci], sg], [accs[dci]], out=accs[dci].ap, in0=accs[dci].ap,
                                        in1=sg.ap, op=ALU.add)
                                else:
                                    dve('tensor_tensor', [accs[dci], sg], [M[dc]], out=M[dc].ap, in0=accs[dci].ap,
                                        in1=sg.ap, op=ALU.add)
                if DEBUG and tt == 0:
                    dma('sp', dbgM, AR[:, 16384:24576], [M], [("dbgM",)], "dbgM")
                if STOP == 'merge' and tt == 0:
                    raise _Stop()
                for cgi in range(8):
                    w = WS.get(w_out, l, 0, 16, cgi * 256, 256)
                    for dci in range(2):
                        dc = cgi * 2 + dci
                        pb = bank()
                        for kc in range(16):
                            mm(pb.ap, w.ap[:, kc, dci * 128:(dci + 1) * 128], M[kc].ap, kc == 0, kc == 15, [M[kc], w], [pb])
                        evac_stats(pb, Y, dc)

                def xget(dc):
                    xc = nxt("xc", vXC)
                    dma('sp', xc.ap, xsrc[dc * 128:(dc + 1) * 128, t0:t0 + TT], [(xsn, tt, dc)], [xc], "ldxc%d" % (rot["xc"] % 2))
                    return xc
                def after_mix(dc):
                    act(A[dc].ap, Y[dc].ap, AF.Identity, [Y[dc], vSPR], [A[dc]], scale=SPR[:, 32 + dc:33 + dc])
                    sq = nxt("bt", vBT)
                    act(sq.ap, Y[dc].ap, AF.Square, [Y[dc]], [sq])
                    pend_stats.append((sq, dc))
                    flush_stats(2)
                post_norm_residual(Y, 16, xget, tt, after=after_mix)
                flush_stats()
                r2 = rstd_of(PSTAT, 1562)
                if DEBUG and tt == 0:
                    dma('sp', dbgY, Y_all.ap.rearrange("p a b -> p (a b)"), [Y_all], [("dbgY",)], "dbgY")
                if STOP == 'wout' and tt == 0:
                    raise _Stop()
                for jg in range(22):
                    wa = WS.get(w_fi, l, 0, 16, jg * 256, 256)
                    wb = WS.get(w_fi, l, 0, 16, FH + jg * 256, 256)
                    for ji in range(2):
                        j = jg * 2 + ji
                        pa = bank()
                        for kc in range(16):
                            mm(pa.ap, wa.ap[:, kc, ji * 128:(ji + 1) * 128], A[kc].ap, kc == 0, kc == 15, [A[kc], wa], [pa])
                        sa = nxt("ft", vFT)
                        dve('tensor_tensor', [pa, r2], [sa], out=sa.ap, in0=pa.ap, in1=r2.ap, op=ALU.mult)
                        act(sa.ap, sa.ap, AF.Silu, [sa], [sa])
                        pbb = bank()
                        for kc in range(16):
                            mm(pbb.ap, wb.ap[:, kc, ji * 128:(ji + 1) * 128], A[kc].ap, kc == 0, kc == 15, [A[kc], wb], [pbb])
                        tb = nxt("ft", vFT)
                        dve('tensor_tensor', [pbb, r2], [tb], out=tb.ap, in0=pbb.ap, in1=r2.ap, op=ALU.mult)
                        dve('tensor_tensor', [sa, tb], [HID[j]], out=HID[j].ap, in0=sa.ap, in1=tb.ap, op=ALU.mult)
                for cgi in range(8):
                    pbs = [bank(), bank()]
                    for rg in range(3):
                        nk = 16 if rg < 2 else 12
                        w = WS.get(w_fo, l, rg * 2048, nk, cgi * 256, 256)
                        for dci in range(2):
                            for kk in range(nk):
                                mm(pbs[dci].ap, w.ap[:, kk, dci * 128:(dci + 1) * 128], HID[rg * 16 + kk].ap,
                                   rg == 0 and kk == 0, rg == 2 and kk == nk - 1, [HID[rg * 16 + kk], w], [pbs[dci]])
                    for dci in range(2):
                        evac_stats(pbs[dci], Y2, cgi * 2 + dci)
                def store_chunk(dc):
                    i = dma('sp', xdst[dc * 128:(dc + 1) * 128, t0:t0 + TT], Y[dc].ap, [Y[dc]], [(xdn, tt, dc)], "sty%d" % dc)
                    if l == L - 1:
                        out_dmas.append(i)
                post_norm_residual(Y2, 48, lambda dc: Y[dc], tt, after=store_chunk)
            if mode == 'fused' and l < L - 1:
                pass
    def body_fin(out_dmas):
        if not P.dry:
            lastd = {}
            for i, op in enumerate(P.ops):
                if op['dma']:
                    lastd[op['stream']] = i
            out_dmas = list(out_dmas) + list(lastd.values())
            P.ops.append(dict(eng='sp', meth=None, kw=None, deps=set(out_dmas), stream='sp', seq=P.seqc.get('sp', 0) + 1,
                              dma=False, ms=False))
            P.seqc['sp'] = P.seqc.get('sp', 0) + 1

    P.dry = True
    body()
    P.dry = False
    body()
    streams = P.finalize()
    sems = {}
    for s in streams:
        sems[s] = es.enter_context(nc.semaphore("s_" + s))
    with nc.Block() as block:
        P.emit(nc, block, sems)
    es.close()
    return nc, len(P.ops)


_GAMMA = 1.0 - np.exp2(-5.0 - np.arange(8, dtype=np.float64))


def _consts(core):
    s = core % 4
    b = core // 4
    c = np.zeros((128, NCST), np.float32)
    half = 64
    inv = (np.float32(10000.0) ** (-np.arange(half, dtype=np.float32) / np.float32(half))).astype(np.float32)
    c[:, 0:64] = inv[None, :]
    c[:, 64:128] = inv[None, :]
    p = np.arange(128, dtype=np.float64)
    lg = np.log(_GAMMA)
    c[:, 128:136] = np.exp((p[:, None] + 1.0) * lg[None, :])
    c[:, 136:144] = np.exp(-(p[:, None] + 1.0) * lg[None, :]) * (128.0 ** -0.5)
    gC = np.exp(128.0 * lg)
    c[:, 144:1168] = np.repeat(gC, 128)[None, :]
    G = np.exp(2048.0 * lg)
    for r in range(8):
        rb, rs = r // 4, r % 4
        if rb == b and rs < s:
            c[:, 1168 + r * 8:1168 + r * 8 + 8] = (G ** (s - 1 - rs))[None, :]
        if rb == b and rs == s - 1:
            c[:, 1232 + r] = 1.0
    wins = (2, 4, 8, 16)
    for g in range(4):
        t = np.arange(16)
        if s == 0:
            c[:, 1240 + g * 16:1240 + (g + 1) * 16] = (1.0 / np.minimum(t + 1, wins[g]))[None, :]
        else:
            c[:, 1240 + g * 16:1240 + (g + 1) * 16] = 1.0 / wins[g]
    c[:, 1304] = -np.pi
    c[:, 1305] = np.pi
    e = np.arange(128)
    c[:, 1306:1434] = (e[:, None] <= e[None, :]).astype(np.float32)
    c[:, 1434:1562] = np.eye(128, dtype=np.float32)
    c[:, 1562] = 1e-6
    c[:, 1563] = 1e-5
    return c


def _fm(v, n):
    return np.ascontiguousarray(v.reshape(n, 128).T)


def _pack_sp(inp, l):
    sp = np.zeros((128, NSP), np.float32)
    sp[:, 0:16] = _fm(inp["g_mix_pre"][l], 16)
    sp[:, 16:32] = _fm(inp["g_mix_post"][l], 16)
    sp[:, 32:48] = _fm(inp["g_ffn_pre"][l], 16)
    sp[:, 48:64] = _fm(inp["g_ffn_post"][l], 16)
    sp[:, 64:68] = _fm(inp["pool_scale"][l], 4)
    sp[:, 68:72] = _fm(inp["conv_b"][l], 4)
    sp[:, 72:76] = _fm(inp["conv_ln_g"][l], 4)
    sp[:, 76:80] = _fm(inp["conv_ln_b"][l], 4)
    sp[:, 80:88] = _fm(inp["ret_gn_g"][l], 8)
    dw = inp["conv_dw"][l]
    sp[:, 88:212] = dw.T.reshape(4, 128, 31).transpose(1, 0, 2).reshape(128, 124)
    return sp


def _pack_pw(inp, l):
    pw = inp["pool_w"][l]
    return np.ascontiguousarray(pw.transpose(1, 0, 2).reshape(128, 512))


_CACHE = {}


def _get(L, mode, ntok=2048):
    k = (L, mode, ntok)
    if k not in _CACHE:
        _CACHE[k] = build(L, mode, ntok)[0]
    return _CACHE[k]


def _core_x(x, c):
    b, s = c // 4, c % 4
    return np.ascontiguousarray(x[b, s * NTOK:(s + 1) * NTOK, :].T)


def _core_pos(pos, c):
    b, s = c // 4, c % 4
    return np.ascontiguousarray(pos[b, s * NTOK:(s + 1) * NTOK].reshape(16, 128).T.astype(np.int32))


def _layer_maps(inp, layers, xTs):
    sp = np.stack([_pack_sp(inp, l) for l in layers])
    pw = np.stack([_pack_pw(inp, l) for l in layers])
    sl = layers if len(layers) > 1 else slice(layers[0], layers[0] + 1)
    ws = dict(w_in=inp["w_in"][sl], w_pp=inp["w_pool_proj"][sl], w_cp=inp["w_conv_proj"][sl],
              w_rp=inp["w_ret_proj"][sl], w_out=inp["w_out"][sl], w_fi=inp["w_ffn_in"][sl], w_fo=inp["w_ffn_out"][sl])
    maps = []
    for c in range(8):
        m = dict(xT=xTs[c], posT=_core_pos(inp["positions"], c), cst=_CSTS[c], sp=sp, pw=pw)
        m.update(ws)
        maps.append(m)
    return maps


_CSTS = None


def kernel_unfused(**inputs):
    inp = {k: np.asarray(v) for k, v in inputs.items()}
    global _CSTS
    _CSTS = [_consts(c) for c in range(8)]
    x = inp["x"].astype(np.float32, copy=False)
    xTs = [_core_x(x, c) for c in range(8)]
    for l in range(4):
        maps = _layer_maps(inp, [l], xTs)
        nc_pre = _get(1, 'pre')
        pre_keys = ("xT", "posT", "cst", "sp", "pw", "w_in")
        res = run_bass_kernel_spmd(nc_pre, [{k: m[k] for k in pre_keys} for m in maps], core_ids=list(range(8)))
        xin = np.concatenate([res.results[c]["xout"] for c in range(8)], axis=0)
        for m in maps:
            m["xin"] = xin
        nc_full = _get(1, 'full')
        res = run_bass_kernel_spmd(nc_full, maps, core_ids=list(range(8)))
        xTs = [res.results[c]["yT"] for c in range(8)]
    out = np.empty((2, 8192, DM), np.float32)
    for c in range(8):
        b, s = c // 4, c % 4
        out[b, s * NTOK:(s + 1) * NTOK, :] = xTs[c].T
    return out


def kernel(**inputs):
    inp = {k: np.asarray(v) for k, v in inputs.items()}
    S = 8192
    nc = _get(4, 'seq', S)
    sp = np.stack([_pack_sp(inp, l) for l in range(4)])
    pw = np.stack([_pack_pw(inp, l) for l in range(4)])
    cst = _consts(0)
    x = inp["x"].astype(np.float32, copy=False)
    maps = []
    for b in range(2):
        maps.append(dict(
            xT=np.ascontiguousarray(x[b].T),
            posT=np.ascontiguousarray(inp["positions"][b].reshape(S // 128, 128).T.astype(np.int32)),
            cst=cst, sp=sp, pw=pw,
            w_in=inp["w_in"], w_pp=inp["w_pool_proj"], w_cp=inp["w_conv_proj"], w_rp=inp["w_ret_proj"],
            w_out=inp["w_out"], w_fi=inp["w_ffn_in"], w_fo=inp["w_ffn_out"]))
    res = run_bass_kernel_spmd(nc, maps, core_ids=[0, 1])
    out = np.empty((2, S, DM), np.float32)
    for b in range(2):
        out[b] = res.results[b]["yT"].T
    return out
```
